# Optimizing a Trainium2 kernel written in Bass

```python
import math
import jax, jax.numpy as jnp
from jax import lax
import numpy as np


D_MODEL = 1024
BATCH = 8
SEQ = 8192
DEPTH = 2

HEAD_DIM = 64
N_MIX_HEADS = D_MODEL // HEAD_DIM
NSA_HEADS = (3 * N_MIX_HEADS) // 8
NSA_KV_HEADS = NSA_HEADS // 3
NSA_GROUP = NSA_HEADS // NSA_KV_HEADS
FOX_HEADS = (3 * N_MIX_HEADS) // 8
CONV_GROUPS = N_MIX_HEADS - NSA_HEADS - FOX_HEADS
NSA_WIDTH = NSA_HEADS * HEAD_DIM
CONV_WIDTH = CONV_GROUPS * HEAD_DIM
FOX_WIDTH = FOX_HEADS * HEAD_DIM
MIX_WIDTH = NSA_WIDTH + CONV_WIDTH + FOX_WIDTH
CMP_BLOCK = 32
CMP_HIDDEN = 256
SEL_BLOCK = 64
SEL_TOPK = 16
WINDOW = 512
CONV_KERNEL = 31
D_FF = 256 * math.ceil(8 * D_MODEL / 3 / 256)
FFN_CONV = 3
Q_BLOCK = 128
ALPHA = (2.0 * DEPTH) ** 0.25
BETA = (8.0 * DEPTH) ** -0.25
LN_EPS = 1e-5
NEG_INF = -1e30
PROJ_SIZES = (NSA_WIDTH, 6 * NSA_KV_HEADS * HEAD_DIM, 3 * NSA_HEADS, 2 * CONV_WIDTH, 3 * FOX_WIDTH, FOX_HEADS)
PROJ_WIDTH = sum(PROJ_SIZES)
PROJ_SPLITS = tuple(sum(PROJ_SIZES[:i + 1]) for i in range(len(PROJ_SIZES) - 1))

kernel_name = 'hybrid_nsa_conformer_fox_deepnorm'


def layer_norm(x, g, b):
    xf = x.astype(jnp.float32)
    mu = jnp.mean(xf, axis=-1, keepdims=True)
    var = jnp.mean(jnp.square(xf - mu), axis=-1, keepdims=True)
    return ((xf - mu) * lax.rsqrt(var + LN_EPS) * g + b).astype(x.dtype)


def causal_dwconv(x, w, b):
    k = w.shape[0]
    y = lax.conv_general_dilated(x, w[:, None, :].astype(x.dtype), window_strides=(1,), padding=[(k - 1, 0)],
                                 dimension_numbers=('NWC', 'WIO', 'NWC'), feature_group_count=x.shape[-1])
    return y + b.astype(x.dtype)


def masked_softmax(s, mask):
    p = jax.nn.softmax(jnp.where(mask, s, NEG_INF), axis=-1)
    return jnp.where(mask, p, 0.0)


def alibi_slopes(n):
    return jnp.asarray(2.0 ** (-8.0 * np.arange(1, n + 1) / n), dtype=jnp.float32)


def nsa_mixer(q, kv, gate_logits, w_cmp1, w_cmp2, pe_cmp):
    bsz, s_len, _ = q.shape
    hkv, grp, hd = NSA_KV_HEADS, NSA_GROUP, HEAD_DIM
    q = q.reshape(bsz, s_len, hkv, grp, hd).transpose(0, 2, 3, 1, 4)
    kv = kv.reshape(bsz, s_len, 6, hkv, hd).transpose(2, 0, 3, 1, 4)
    k_cmp_raw, v_cmp_raw, k_slc, v_slc, k_win, v_win = kv
    gates = jax.nn.sigmoid(gate_logits.astype(jnp.float32))
    gates = gates.reshape(bsz, s_len, hkv, grp, 3).transpose(0, 2, 3, 1, 4)

    n_cmp = s_len // CMP_BLOCK
    n_sel = s_len // SEL_BLOCK
    top_k = min(SEL_TOPK, n_sel)
    ratio = SEL_BLOCK // CMP_BLOCK

    def compress(t, i):
        blk = t.reshape(bsz, hkv, n_cmp, CMP_BLOCK, hd) + pe_cmp[i].astype(t.dtype)
        hid = jax.nn.gelu(blk.reshape(bsz, hkv, n_cmp, CMP_BLOCK * hd) @ w_cmp1[i])
        return hid @ w_cmp2[i]

    k_cmp = compress(k_cmp_raw, 0)
    v_cmp = compress(v_cmp_raw, 1)
    cmp_end = jnp.arange(n_cmp) * CMP_BLOCK + CMP_BLOCK - 1
    k_slc_blk = k_slc.reshape(bsz, hkv, n_sel, SEL_BLOCK, hd)
    v_slc_blk = v_slc.reshape(bsz, hkv, n_sel, SEL_BLOCK, hd)
    k_win_pad = jnp.pad(k_win, ((0, 0), (0, 0), (WINDOW, 0), (0, 0)))
    v_win_pad = jnp.pad(v_win, ((0, 0), (0, 0), (WINDOW, 0), (0, 0)))
    slopes = alibi_slopes(NSA_HEADS).reshape(hkv, grp)[:, :, None, None]
    scale = hd ** -0.5
    sel_ids = jnp.arange(n_sel)

    def block(i):
        q0 = i * Q_BLOCK
        qb = lax.dynamic_slice_in_dim(q, q0, Q_BLOCK, axis=3)
        gb = lax.dynamic_slice_in_dim(gates, q0, Q_BLOCK, axis=3)
        t = q0 + jnp.arange(Q_BLOCK)
        dist_c = t[:, None] - cmp_end[None, :]
        s_c = jnp.einsum('bhgqd,bhnd->bhgqn', qb, k_cmp).astype(jnp.float32) * scale
        p_c = masked_softmax(s_c - slopes * dist_c.astype(jnp.float32), dist_c >= 0)
        o_c = jnp.einsum('bhgqn,bhnd->bhgqd', p_c.astype(v_cmp.dtype), v_cmp)
        imp = p_c.sum(axis=2).reshape(bsz, hkv, Q_BLOCK, n_sel, ratio).sum(-1)
        cur = (t // SEL_BLOCK)[:, None]
        forced = (sel_ids[None, :] == 0) | (sel_ids[None, :] == cur)
        score = jnp.where(sel_ids[None, :] > cur, -1.0, jnp.where(forced, NSA_GROUP + 1.0, imp))
        _, idx = lax.top_k(score, top_k)
        gidx = idx.reshape(bsz, hkv, Q_BLOCK * top_k, 1, 1)
        k_sel = jnp.take_along_axis(k_slc_blk, gidx, axis=2).reshape(bsz, hkv, Q_BLOCK, top_k * SEL_BLOCK, hd)
        v_sel = jnp.take_along_axis(v_slc_blk, gidx, axis=2).reshape(bsz, hkv, Q_BLOCK, top_k * SEL_BLOCK, hd)
        pos_s = (idx[..., None] * SEL_BLOCK + jnp.arange(SEL_BLOCK)).reshape(bsz, hkv, Q_BLOCK, top_k * SEL_BLOCK)
        dist_s = (t[:, None] - pos_s)[:, :, None]
        s_s = jnp.einsum('bhgqd,bhqkd->bhgqk', qb, k_sel).astype(jnp.float32) * scale
        p_s = masked_softmax(s_s - slopes * dist_s.astype(jnp.float32), dist_s >= 0)
        o_s = jnp.einsum('bhgqk,bhqkd->bhgqd', p_s.astype(v_sel.dtype), v_sel)
        k_w = lax.dynamic_slice_in_dim(k_win_pad, q0, WINDOW + Q_BLOCK, axis=2)
        v_w = lax.dynamic_slice_in_dim(v_win_pad, q0, WINDOW + Q_BLOCK, axis=2)
        pos_w = q0 - WINDOW + jnp.arange(WINDOW + Q_BLOCK)
        dist_w = t[:, None] - pos_w[None, :]
        mask_w = (dist_w >= 0) & (dist_w < WINDOW) & (pos_w[None, :] >= 0)
        s_w = jnp.einsum('bhgqd,bhwd->bhgqw', qb, k_w).astype(jnp.float32) * scale
        p_w = masked_softmax(s_w - slopes * dist_w.astype(jnp.float32), mask_w)
        o_w = jnp.einsum('bhgqw,bhwd->bhgqd', p_w.astype(v_w.dtype), v_w)
        out = gb[..., 0:1] * o_c + gb[..., 1:2] * o_s + gb[..., 2:3] * o_w
        return out.astype(q.dtype)

    o = lax.map(block, jnp.arange(s_len // Q_BLOCK))
    return o.transpose(1, 0, 4, 2, 3, 5).reshape(bsz, s_len, NSA_WIDTH)


def conv_mixer(u, w_dw, b_dw, ln_g, ln_b):
    a, g = jnp.split(u, 2, axis=-1)
    h = causal_dwconv(a * jax.nn.sigmoid(g), w_dw, b_dw)
    return jax.nn.silu(layer_norm(h, ln_g, ln_b))


def fox_mixer(qkv, f_logit, b_f):
    bsz, s_len, _ = qkv.shape
    q, k, v = qkv.reshape(bsz, s_len, 3, FOX_HEADS, HEAD_DIM).transpose(2, 0, 3, 1, 4)
    log_f = jax.nn.log_sigmoid(f_logit.astype(jnp.float32) + b_f.astype(jnp.float32))
    c = jnp.cumsum(log_f, axis=1).transpose(0, 2, 1)
    pos = jnp.arange(s_len)
    scale = HEAD_DIM ** -0.5

    def block(i):
        q0 = i * Q_BLOCK
        qb = lax.dynamic_slice_in_dim(q, q0, Q_BLOCK, axis=2)
        cb = lax.dynamic_slice_in_dim(c, q0, Q_BLOCK, axis=2)
        t = q0 + jnp.arange(Q_BLOCK)
        s = jnp.einsum('bhqd,bhkd->bhqk', qb, k).astype(jnp.float32) * scale
        s = s + cb[..., None] - c[:, :, None, :]
        p = masked_softmax(s, t[:, None] >= pos[None, :])
        return jnp.einsum('bhqk,bhkd->bhqd', p.astype(v.dtype), v)

    o = lax.map(block, jnp.arange(s_len // Q_BLOCK))
    return o.transpose(1, 0, 3, 2, 4).reshape(bsz, s_len, FOX_WIDTH)


def setup_inputs(seed: int = 0) -> dict:
    key = jax.random.key(seed)
    ks = jax.random.split(key, 24)
    f32 = jnp.float32

    def nrm(k, shape, s):
        return jax.random.normal(k, shape, f32) * s

    return {
        'x': nrm(ks[0], (BATCH, SEQ, D_MODEL), 1.0),
        'ln_emb_g': 1.0 + nrm(ks[1], (D_MODEL,), 0.01),
        'ln_emb_b': nrm(ks[2], (D_MODEL,), 0.01),
        'w_in': nrm(ks[3], (DEPTH, D_MODEL, PROJ_WIDTH), D_MODEL ** -0.5),
        'b_f': jnp.linspace(1.0, 5.0, FOX_HEADS, dtype=f32) + nrm(ks[4], (DEPTH, FOX_HEADS), 0.1),
        'w_cmp1': nrm(ks[5], (DEPTH, 2, CMP_BLOCK * HEAD_DIM, CMP_HIDDEN), (CMP_BLOCK * HEAD_DIM) ** -0.5),
        'w_cmp2': nrm(ks[6], (DEPTH, 2, CMP_HIDDEN, HEAD_DIM), CMP_HIDDEN ** -0.5),
        'pe_cmp': nrm(ks[7], (DEPTH, 2, CMP_BLOCK, HEAD_DIM), 0.1),
        'w_dw': nrm(ks[8], (DEPTH, CONV_KERNEL, CONV_WIDTH), CONV_KERNEL ** -0.5),
        'b_dw': nrm(ks[9], (DEPTH, CONV_WIDTH), 0.01),
        'ln_conv_g': 1.0 + nrm(ks[10], (DEPTH, CONV_WIDTH), 0.01),
        'ln_conv_b': nrm(ks[11], (DEPTH, CONV_WIDTH), 0.01),
        'w_out': nrm(ks[12], (DEPTH, MIX_WIDTH, D_MODEL), MIX_WIDTH ** -0.5 * BETA),
        'ln1_g': 1.0 + nrm(ks[13], (DEPTH, D_MODEL), 0.01),
        'ln1_b': nrm(ks[14], (DEPTH, D_MODEL), 0.01),
        'w_ffn_in': nrm(ks[15], (DEPTH, D_MODEL, 2 * D_FF), D_MODEL ** -0.5),
        'w_ffn_conv': nrm(ks[16], (DEPTH, FFN_CONV, D_FF), FFN_CONV ** -0.5),
        'b_ffn_conv': nrm(ks[17], (DEPTH, D_FF), 0.01),
        'w_ffn_down': nrm(ks[18], (DEPTH, D_FF, D_MODEL), D_FF ** -0.5 * BETA),
        'ln2_g': 1.0 + nrm(ks[19], (DEPTH, D_MODEL), 0.01),
        'ln2_b': nrm(ks[20], (DEPTH, D_MODEL), 0.01),
    }


def reference(x, ln_emb_g, ln_emb_b, w_in, b_f, w_cmp1, w_cmp2, pe_cmp, w_dw, b_dw, ln_conv_g, ln_conv_b,
              w_out, ln1_g, ln1_b, w_ffn_in, w_ffn_conv, b_ffn_conv, w_ffn_down, ln2_g, ln2_b):
    x = layer_norm(x, ln_emb_g, ln_emb_b)
    for l in range(DEPTH):
        proj = x @ w_in[l]
        a_q, a_kv, a_g, b_u, c_qkv, c_f = jnp.split(proj, PROJ_SPLITS, axis=-1)
        o_a = nsa_mixer(a_q, a_kv, a_g, w_cmp1[l], w_cmp2[l], pe_cmp[l])
        o_b = conv_mixer(b_u, w_dw[l], b_dw[l], ln_conv_g[l], ln_conv_b[l])
        o_c = fox_mixer(c_qkv, c_f, b_f[l])
        mix = jnp.concatenate([o_a, o_b.astype(o_a.dtype), o_c], axis=-1) @ w_out[l]
        x = layer_norm(ALPHA * x + mix, ln1_g[l], ln1_b[l])
        gate_pre, up = jnp.split(x @ w_ffn_in[l], 2, axis=-1)
        h = jax.nn.silu(causal_dwconv(gate_pre, w_ffn_conv[l], b_ffn_conv[l])) * up
        x = layer_norm(ALPHA * x + h @ w_ffn_down[l], ln2_g[l], ln2_b[l])
    return x
```

```python
import contextlib
import math
import numpy as np
import concourse.bass as bass
import concourse.mybir as mybir
from concourse.bass_utils import run_bass_kernel_spmd

F32 = mybir.dt.float32
BF16 = mybir.dt.bfloat16
I32 = mybir.dt.int32
AF = mybir.ActivationFunctionType
ALU = mybir.AluOpType
AX = mybir.AxisListType

S = 8192
D = 1024
DEPTH = 2
NQT = S // 512
NKT = S // 128
PROJ_W = 2840
DFF = 2816
NF = DFF // 128
ALPHA = (2.0 * DEPTH) ** 0.25
LN_EPS = 1e-5
SLOPES = [2.0 ** (-8.0 * (i + 1) / 6) for i in range(6)]
NEGM = 240000.0
DBG_NTT = None
DBG_WHICH = "fsw"
DBG_HEADS = None
C_NQ, C_KCR, C_VCR, C_KSL, C_VSL, C_KWN, C_VWN, C_GATE, C_CA, C_CG, C_FQ, C_FK, C_FV, C_FF = (
    0, 384, 512, 640, 768, 896, 1024, 1152, 1170, 1426, 1682, 2066, 2450, 2834)


class Ev:
    __slots__ = ("sem", "val", "src")

    def __init__(self, sem, val, src):
        self.sem, self.val, self.src = sem, val, src


class Tracker:
    def __init__(self, nc, stack):
        self.nc = nc
        self.stack = stack
        self.eng = {"pe": nc.tensor, "act": nc.scalar, "dve": nc.vector, "pool": nc.gpsimd, "sp": nc.sync}
        self.sem = {}
        self.cnt = {}
        self.nsem = 0
        for e in self.eng:
            self._new_epoch(e)
        self.seen = {e: {} for e in self.eng}
        self.last_w = {}
        self.readers = {}
        self.last_ev = {}
        self.dma_sems = {}
        self.dma_rr = {}
        for q, n in (("sp", 20), ("pool", 12), ("act", 6)):
            self.dma_sems[q] = [[self._mk_sem(f"d{q}{i}"), 0] for i in range(n)]
            self.dma_rr[q] = 0
        self.ninst = 0
        self.excl = set()

    def _mk_sem(self, name):
        self.nsem += 1
        return self.stack.enter_context(self.nc.semaphore(f"{name}_{self.nsem}"))

    def _new_epoch(self, e):
        self.sem[e] = self._mk_sem(f"s{e}")
        self.cnt[e] = 0

    def wait(self, e, ev):
        k = id(ev.sem)
        if self.seen[e].get(k, 0) >= ev.val:
            return
        self.eng[e].wait_ge(ev.sem, ev.val)
        self.seen[e][k] = ev.val

    def _split(self, reads, writes):
        if self.excl:
            ex = [k for k in reads if k in self.excl]
            if ex:
                reads = [k for k in reads if k not in self.excl]
                writes = list(writes) + [k for k in ex if k not in writes]
        return reads, writes

    def _deps(self, e, reads, writes, srcid):
        deps = []
        for k in reads:
            w = self.last_w.get(k)
            if w is not None:
                deps.append(w)
        for k in writes:
            w = self.last_w.get(k)
            if w is not None and w.src != srcid:
                deps.append(w)
            for r in self.readers.get(k, {}).values():
                if r.src != srcid:
                    deps.append(r)
        for d in deps:
            self.wait(e, d)

    def _record(self, ev, reads, writes):
        for k in reads:
            self.readers.setdefault(k, {})[ev.src] = ev
        for k in writes:
            self.last_w[k] = ev
            self.readers[k] = {}

    def op(self, e, fn, reads=(), writes=()):
        reads, writes = self._split(reads, writes)
        self._deps(e, reads, writes, e)
        inst = fn()
        if self.cnt[e] >= 30000:
            self._new_epoch(e)
        self.cnt[e] += 1
        inst.then_inc(self.sem[e], 1)
        ev = Ev(self.sem[e], self.cnt[e], e)
        self.last_ev[e] = ev
        self._record(ev, reads, writes)
        self.ninst += 1
        return ev

    def group(self, e, fns, reads=(), writes=()):
        reads, writes = self._split(reads, writes)
        self._deps(e, reads, writes, e)
        inst = None
        for fn in fns:
            inst = fn()
            self.ninst += 1
        if self.cnt[e] >= 30000:
            self._new_epoch(e)
        self.cnt[e] += 1
        inst.then_inc(self.sem[e], 1)
        ev = Ev(self.sem[e], self.cnt[e], e)
        self.last_ev[e] = ev
        self._record(ev, reads, writes)
        return ev

    def dma(self, q, out, in_, reads=(), writes=(), **kw):
        pool = self.dma_sems[q]
        i = self.dma_rr[q]
        self.dma_rr[q] = (i + 1) % len(pool)
        ent = pool[i]
        if ent[1] >= 30000:
            ent[0] = self._mk_sem(f"d{q}")
            ent[1] = 0
        sem, val = ent
        srcid = ("dma", id(sem))
        if val > 0:
            self.wait(q, Ev(sem, val, srcid))
        self._deps(q, reads, writes, srcid)
        self.eng[q].dma_start(out=out, in_=in_, **kw).then_inc(sem, 16)
        ent[1] = val + 16
        ev = Ev(sem, ent[1], srcid)
        self._record(ev, reads, writes)
        self.ninst += 1
        return ev

    def barrier(self):
        evs = list(self.last_ev.values())
        for q in self.dma_sems:
            for sem, val in self.dma_sems[q]:
                if val > 0:
                    evs.append(Ev(sem, val, ("dma", id(sem))))
        for e in self.eng:
            for ev in evs:
                self.wait(e, ev)
        self.last_w.clear()
        self.readers.clear()


def build(dbg=(), phases="ABCDEFGH", layers=None):
    nc = bass.Bass("TRN2", target_bir_lowering=False)
    ES = contextlib.ExitStack()
    with ES:
        T = Tracker(nc, ES)
        V, A, P, G = nc.vector, nc.scalar, nc.tensor, nc.gpsimd

        def dram_in(name, shape):
            return nc.dram_tensor(name, list(shape), F32, kind="ExternalInput").ap()

        def dram(name, shape, dt):
            kind = "ExternalOutput" if name in dbg else "Internal"
            return nc.dram_tensor(name, list(shape), dt, kind=kind).ap()

        x_in = dram_in("x", [S, D])
        ln_emb_g = dram_in("ln_emb_g", [D]); ln_emb_b = dram_in("ln_emb_b", [D])
        w_in = dram_in("w_in", [DEPTH, D, PROJ_W]); b_f = dram_in("b_f", [DEPTH, 6])
        w_cmp1 = dram_in("w_cmp1", [DEPTH, 2, 2048, 256]); w_cmp2 = dram_in("w_cmp2", [DEPTH, 2, 256, 64])
        pe_cmp = dram_in("pe_cmp", [DEPTH, 2, 32, 64])
        w_dw = dram_in("w_dw", [DEPTH, 31, 256]); b_dw = dram_in("b_dw", [DEPTH, 256])
        ln_conv_g = dram_in("ln_conv_g", [DEPTH, 256]); ln_conv_b = dram_in("ln_conv_b", [DEPTH, 256])
        w_out = dram_in("w_out", [DEPTH, D, D])
        ln1_g = dram_in("ln1_g", [DEPTH, D]); ln1_b = dram_in("ln1_b", [DEPTH, D])
        w_ffn_in = dram_in("w_ffn_in", [DEPTH, D, 2 * DFF]); w_ffn_conv = dram_in("w_ffn_conv", [DEPTH, 3, DFF])
        b_ffn_conv = dram_in("b_ffn_conv", [DEPTH, DFF]); w_ffn_down = dram_in("w_ffn_down", [DEPTH, DFF, D])
        ln2_g = dram_in("ln2_g", [DEPTH, D]); ln2_b = dram_in("ln2_b", [DEPTH, D])
        y_out = nc.dram_tensor("y", [S, D], F32, kind="ExternalOutput").ap()

        XA = dram("XA", [S, D], F32); XB = dram("XB", [S, D], F32)
        QTN = dram("QTN", [384, S], BF16)
        KCR = dram("KCR", [128, S], BF16); VCR = dram("VCR", [128, S], BF16)
        KSL = dram("KSL", [128, S], BF16); KWN = dram("KWN", [128, S], BF16)
        VSL = dram("VSL", [S, 128], BF16); VWN = dram("VWN", [S, 128], BF16)
        GT = dram("GT", [18, S], F32); GTM = dram("GTM", [S, 18], F32)
        GLU = dram("GLU", [256, S], F32)
        FQT = dram("FQT", [384, S], BF16); FKT = dram("FKT", [384, S], BF16); FV = dram("FV", [S, 384], BF16)
        FLT = dram("FLT", [6, S], F32); CQ8 = dram("CQ8", [6, S], BF16); CFX = dram("CFX", [6, S], F32)
        NSACQ = dram("NSACQ", [6, S], BF16)
        MASKT = dram("MASKT", [6, 128, S], BF16)
        OCT = dram("OCT", [6, 64, S], F32)
        MIXT = dram("MIXT", [D, S], BF16)
        OST = dram("OST", [6, 64, S], F32)
        KCMP = dram("KCMP", [2, 64, 256], BF16); VCMP = dram("VCMP", [2, 256, 64], BF16)
        SCORE = dram("SCORE", [2, S, 128], F32); SELM = dram("SELM", [2, S, 128], F32)

        uniq = [0]

        def sb(name, shape, dt, st=ES):
            uniq[0] += 1
            return st.enter_context(nc.sbuf_tensor(f"{name}_{uniq[0]}", list(shape), dt))

        def ps(name, shape, dt, st=ES):
            uniq[0] += 1
            esz = 4 if dt == F32 else 2
            t = st.enter_context(nc.psum_tensor(f"{name}_{uniq[0]}", [128, 2048 // esz], dt))
            T.excl.add(name)
            free = 1
            for d_ in shape[1:]:
                free *= d_
            assert free * esz <= 2048, (name, shape)
            v = t[0:shape[0], 0:free]
            if len(shape) == 3:
                v = v.rearrange("p (a b) -> p a b", a=shape[1])
            return v

        ident_b = sb("ident_b", [128, 128], BF16)
        ident_f = sb("ident_f", [128, 128], F32)
        ones_b = sb("ones_b", [128, 128], BF16)
        ones_f = sb("ones_f", [128, 128], F32)
        tpos = sb("tpos", [128, 64], F32)
        nsakb = sb("nsakb", [128, 6, 64], F32)
        nsaqb = sb("nsaqb", [128, 6, 64], F32)
        T.op("pool", lambda: G.memset(ones_b[:], 1.0), writes=["ones_b"])
        T.op("pool", lambda: G.memset(ones_f[:], 1.0), writes=["ones_f"])
        T.op("pool", lambda: G.affine_select(out=ident_b[:], in_=ones_b[:], pattern=[[1, 128]], compare_op=ALU.is_equal,
                                             fill=0.0, base=0, channel_multiplier=-1), reads=["ones_b"], writes=["ident_b"])
        T.op("pool", lambda: G.affine_select(out=ident_f[:], in_=ones_f[:], pattern=[[1, 128]], compare_op=ALU.is_equal,
                                             fill=0.0, base=0, channel_multiplier=-1), reads=["ones_f"], writes=["ident_f"])
        T.op("pool", lambda: G.iota(tpos[:], pattern=[[128, 64]], base=0, channel_multiplier=1,
                                    allow_small_or_imprecise_dtypes=True), writes=["tpos"])
        for h in range(6):
            T.op("dve", lambda h=h: V.tensor_scalar(out=nsakb[:, h, :], in0=tpos[:], scalar1=SLOPES[h], scalar2=None,
                                                    op0=ALU.mult), reads=["tpos"], writes=["nsakb"])
            T.op("dve", lambda h=h: V.tensor_scalar(out=nsaqb[:, h, :], in0=tpos[:], scalar1=-8.0 * SLOPES[h], scalar2=-NEGM,
                                                    op0=ALU.mult, op1=ALU.add), reads=["tpos"], writes=["nsaqb"])
        with contextlib.ExitStack() as st:
            trow = sb("trow", [1, S], F32, st)
            crow = sb("crow", [1, S], BF16, st)
            T.op("pool", lambda: G.iota(trow[:], pattern=[[1, S]], base=0, channel_multiplier=0,
                                        allow_small_or_imprecise_dtypes=True), writes=["trow"])
            for h in range(6):
                T.op("dve", lambda h=h: V.tensor_scalar(out=crow[:], in0=trow[:], scalar1=-8.0 * SLOPES[h], scalar2=None,
                                                        op0=ALU.mult), reads=["trow"], writes=["crow"])
                T.dma("sp", NSACQ[h:h + 1, :], crow[:], reads=["crow"])
            T.barrier()

        cmptab = sb("cmptab", [128, 6, 512], F32)
        atab = sb("atab", [128, 256], F32)
        btab = sb("btab", [128, 256], F32)
        with contextlib.ExitStack() as st:
            dist = sb("dist", [128, 512], F32, st)
            dt_ = sb("dt_", [128, 256], F32, st)
            t1 = sb("t1", [128, 256], F32, st)
            T.op("pool", lambda: G.iota(dist[:], pattern=[[-32, 512]], base=8033, channel_multiplier=1,
                                        allow_small_or_imprecise_dtypes=True), writes=["dist"])
            for h in range(6):
                T.op("dve", lambda h=h: V.tensor_scalar(out=cmptab[:, h, :], in0=dist[:], scalar1=-SLOPES[h], scalar2=None,
                                                        op0=ALU.mult), reads=["dist"], writes=["cmptab"])
                T.op("pool", lambda h=h: G.affine_select(out=cmptab[:, h, :], in_=cmptab[:, h, :], pattern=[[-32, 512]],
                                                         compare_op=ALU.is_ge, fill=-30000.0, base=8033, channel_multiplier=1),
                     reads=["cmptab"], writes=["cmptab"])
            T.op("pool", lambda: G.iota(dt_[0:64, :], pattern=[[1, 256]], base=-126, channel_multiplier=0,
                                        allow_small_or_imprecise_dtypes=True), writes=["dt_"])
            T.op("pool", lambda: G.iota(dt_[64:128, :], pattern=[[1, 256]], base=-127, channel_multiplier=0,
                                        allow_small_or_imprecise_dtypes=True), writes=["dt_"])
            T.op("dve", lambda: V.tensor_scalar(out=atab[:], in0=dt_[:], scalar1=0.0, scalar2=None, op0=ALU.is_lt),
                 reads=["dt_"], writes=["atab"])
            T.op("dve", lambda: V.tensor_scalar(out=btab[:], in0=dt_[:], scalar1=0.0, scalar2=4.0, op0=ALU.is_equal, op1=ALU.mult),
                 reads=["dt_"], writes=["btab"])
            T.op("dve", lambda: V.tensor_scalar(out=t1[:], in0=dt_[:], scalar1=0.0, scalar2=-1.0, op0=ALU.is_gt, op1=ALU.mult),
                 reads=["dt_"], writes=["t1"])
            T.op("dve", lambda: V.tensor_tensor(out=btab[:], in0=btab[:], in1=t1[:], op=ALU.add),
                 reads=["btab", "t1"], writes=["btab"])
            T.barrier()

        def load_cast(st, name, dst, src_of_chunk, nchunks, width, stage):
            for c in range(nchunks):
                sg = stage[c % 2]
                T.dma("sp", sg[:, :width], src_of_chunk(c), writes=[f"{name}_stg{c % 2}"])
                eng = ("dve", "pool")[c % 2]
                E = V if eng == "dve" else G
                T.op(eng, lambda E=E, c=c, sg=sg: E.tensor_copy(out=dst(c), in_=sg[:, :width]),
                     reads=[f"{name}_stg{c % 2}"], writes=[name])

        def layer_norm_rows(st_tiles, ytile, ykey, gb, n=D):
            stats, mv, rstd = st_tiles
            nch = n // 512
            for c in range(nch):
                T.op("dve", lambda c=c: V.bn_stats(out=stats[:, c, :], in_=ytile[:, c * 512:(c + 1) * 512]),
                     reads=[ykey], writes=["ln_stats"])
            T.op("dve", lambda: V.bn_aggr(out=mv[:], in_=stats[:, 0:nch, :]), reads=["ln_stats"], writes=["ln_mv"])
            T.op("act", lambda: A.activation(out=rstd[:], in_=mv[:, 1:2], func=AF.Sqrt, bias=epsb[:, 0:1], scale=1.0),
                 reads=["ln_mv"], writes=["ln_rstd"])
            T.op("dve", lambda: V.reciprocal(out=rstd[:], in_=rstd[:]), reads=["ln_rstd"], writes=["ln_rstd"])
            T.op("dve", lambda: V.tensor_scalar(out=ytile[:], in0=ytile[:], scalar1=mv[:, 0:1], scalar2=rstd[:, 0:1],
                                                op0=ALU.subtract, op1=ALU.mult), reads=[ykey, "ln_mv", "ln_rstd"], writes=[ykey])
            T.op("pool", lambda: G.tensor_tensor(out=ytile[:], in0=ytile[:], in1=gb[0][:], op=ALU.mult),
                 reads=[ykey, "ln_gb"], writes=[ykey])
            T.op("pool", lambda: G.tensor_tensor(out=ytile[:], in0=ytile[:], in1=gb[1][:], op=ALU.add),
                 reads=[ykey, "ln_gb"], writes=[ykey])


        def transpose_load(name, src_ap, n, C, dst_fn, dup=1):
            with contextlib.ExitStack() as st_:
                sg = sb(f"{name}_tl", [n, C * 128], F32, st_)
                pt = ps(f"{name}_tp", [128, C * n], F32, st_)
                T.dma("sp", sg[:], src_ap, writes=[f"{name}_tl"])
                T.group("pe", [lambda c=c: P.transpose(out=pt[:, c * n:(c + 1) * n], in_=sg[0:n, c * 128:(c + 1) * 128], identity=ident_f[0:n, 0:n])
                               for c in range(C)], reads=[f"{name}_tl", "ident_f"], writes=[f"{name}_tp"])
                for c in range(C):
                    T.op("dve", lambda c=c: V.tensor_copy(out=dst_fn(c), in_=pt[:, c * n:(c + 1) * n]), reads=[f"{name}_tp"], writes=[f"{name}_dst"])
                T.barrier()

        epsb = sb("epsb", [128, 1], F32)
        T.op("pool", lambda: G.memset(epsb[:], LN_EPS), writes=["epsb"])
        ln_stats = sb("ln_stats", [128, 2, 6], F32)
        ln_mv = sb("ln_mv", [128, 2], F32)
        ln_rstd = sb("ln_rstd", [128, 1], F32)
        LNT = (ln_stats, ln_mv, ln_rstd)
        T.barrier()

        def load_gb(gt, bt, g_ap, b_ap):
            T.dma("sp", gt[:], g_ap.partition_broadcast(128), writes=["ln_gb"])
            T.dma("sp", bt[:], b_ap.partition_broadcast(128), writes=["ln_gb"])

        def phase_A(l):
            with contextlib.ExitStack() as st:
                win = sb("win", [128, 8, PROJ_W], BF16, st)
                stg = [sb(f"wstg{i}", [128, PROJ_W], F32, st) for i in range(2)]
                load_cast(st, "win", lambda c: win[:, c, :], lambda c: w_in[l, c * 128:(c + 1) * 128, :], 8, PROJ_W, stg)
                gt = sb("gA", [128, D], F32, st); bt = sb("bA", [128, D], F32, st)
                if l == 0:
                    load_gb(gt, bt, ln_emb_g, ln_emb_b)
                xin = [sb(f"xinA{i}", [128, D], F32, st) for i in range(3)]
                xbf = [sb(f"xbfA{i}", [128, D], BF16, st) for i in range(2)]
                xT = [sb(f"xTA{i}", [128, 8, 512], BF16, st) for i in range(2)]
                peT = sb("peT", [128, 2, 512], F32, st)
                negbf = sb("negbf", [6, 1], F32, st)
                ob = [sb(f"obA{i}", [128, 512], BF16, st) for i in range(4)]
                of = [sb(f"ofA{i}", [128, 512], F32, st) for i in range(3)]
                sg = [sb(f"sgA{i}", [128, 512], F32, st) for i in range(2)]
                tmv = [sb(f"tmvA{i}", [128, 640], BF16, st) for i in range(2)]
                tmg = [sb(f"tmgA{i}", [128, 18], F32, st) for i in range(2)]
                ptr = [ps(f"ptrA{i}", [128, 1024], BF16, st) for i in range(2)]
                pfm = [ps(f"pfmA{i}", [128, 512], F32, st) for i in range(3)]
                ptm = [ps(f"ptmA{i}", [128, 512], F32, st) for i in range(2)]
                with contextlib.ExitStack() as st_:
                    pes = sb("pesA", [32, 2, 128], F32, st_)
                    ptp = ps("ptpA", [128, 2, 32], F32, st_)
                    for i in range(2):
                        for kvh in range(2):
                            T.dma("sp", pes[:, i, kvh * 64:(kvh + 1) * 64], pe_cmp[l, i], writes=["pesA"])
                    T.group("pe", [lambda i=i: P.transpose(out=ptp[:, i, :], in_=pes[0:32, i, :], identity=ident_f[0:32, 0:32]) for i in range(2)],
                            reads=["pesA", "ident_f"], writes=["ptpA"])
                    for i in range(2):
                        T.op("dve", lambda i=i: V.tensor_copy(out=peT[:, i, 0:32], in_=ptp[:, i, :]), reads=["ptpA"], writes=["peT"])
                    T.barrier()
                for i in range(2):
                    w_ = 32
                    while w_ < 512:
                        T.op("dve", lambda i=i, w_=w_: V.tensor_copy(out=peT[:, i, w_:2 * w_], in_=peT[:, i, 0:w_]),
                             reads=["peT"], writes=["peT"])
                        w_ *= 2
                T.dma("sp", negbf[:], b_f[l].rearrange("(h o) -> h o", o=1), writes=["negbf"], allow_slow_non_contiguous=True)
                T.op("dve", lambda: V.tensor_scalar(out=negbf[:], in0=negbf[:], scalar1=-1.0, scalar2=None, op0=ALU.mult),
                     reads=["negbf"], writes=["negbf"])
                src = x_in if l == 0 else XA
                cnt = {"ob": 0, "of": 0, "pfm": 0, "ev": 0}

                def evac(eng, out, in_, rk, wk, **kw):
                    if eng == "act":
                        return T.op("act", lambda: A.activation(out=out, in_=in_, func=kw.get("func", AF.Copy)), reads=rk, writes=wk)
                    return T.op("dve", lambda: V.tensor_copy(out=out, in_=in_), reads=rk, writes=wk)

                for tt in range(NQT if DBG_NTT is None else DBG_NTT):
                    xt = xT[tt % 2]; xtk = f"xTA{tt % 2}"
                    for sub in range(4):
                        r0 = tt * 512 + sub * 128
                        n = tt * 4 + sub
                        xi = xin[n % 3]; xik = f"xinA{n % 3}"
                        xb = xbf[n % 2]; xbk = f"xbfA{n % 2}"
                        T.dma("sp", xi[:], src[r0:r0 + 128, :], writes=[xik])
                        if l == 0:
                            layer_norm_rows(LNT, xi, xik, (gt, bt))
                            T.dma("pool", XA[r0:r0 + 128, :], xi[:], reads=[xik])
                        T.op("dve", lambda xb=xb, xi=xi: V.tensor_copy(out=xb[:], in_=xi[:]), reads=[xik], writes=[xbk])
                        pt = ptr[n % 2]; ptk = f"ptrA{n % 2}"
                        T.group("pe", [lambda k=k, pt=pt, xb=xb: P.transpose(out=pt[:, k * 128:(k + 1) * 128], in_=xb[:, k * 128:(k + 1) * 128],
                                                                              identity=ident_b[:]) for k in range(8)],
                                reads=[xbk, "ident_b"], writes=[ptk])
                        T.op("act", lambda pt=pt, xt=xt, sub=sub: A.activation(
                            out=xt[:, :, sub * 128:(sub + 1) * 128], in_=pt[:].rearrange("p (k t) -> p k t", k=8), func=AF.Copy),
                            reads=[ptk], writes=[xtk])
                    q0 = tt * 512
                    def fm(col, width):
                        i = cnt["pfm"] % 3; cnt["pfm"] += 1
                        pp = pfm[i]
                        T.group("pe", [lambda k=k: P.matmul(pp[0:width, :], lhsT=win[:, k, col:col + width], rhs=xt[:, k, :],
                                                            start=(k == 0), stop=(k == 7)) for k in range(8)],
                                reads=["win", xtk], writes=[f"pfmA{i}"])
                        return pp, f"pfmA{i}"

                    def out_bf(pp, pk, width, dst):
                        i = cnt["ob"] % 4; cnt["ob"] += 1
                        o = ob[i]
                        cnt["ev"] += 1
                        evac(("act", "dve")[cnt["ev"] % 2], o[0:width, :], pp[0:width, :], [pk], [f"obA{i}"])
                        T.dma("pool", dst, o[0:width, :], reads=[f"obA{i}"])

                    for c in range(3):
                        pp, pk = fm(C_NQ + c * 128, 128)
                        out_bf(pp, pk, 128, QTN[c * 128:(c + 1) * 128, q0:q0 + 512])
                    for i2, (col, dst) in enumerate(((C_KCR, KCR), (C_VCR, VCR))):
                        pp, pk = fm(col, 128)
                        i = cnt["ob"] % 4; cnt["ob"] += 1
                        o = ob[i]
                        T.op("dve", lambda o=o, pp=pp, i2=i2: V.tensor_tensor(out=o[:], in0=pp[:], in1=peT[:, i2, :], op=ALU.add),
                             reads=[pk, "peT"], writes=[f"obA{i}"])
                        T.dma("pool", dst[:, q0:q0 + 512], o[:], reads=[f"obA{i}"])
                    for col, dst in ((C_KSL, KSL), (C_KWN, KWN)):
                        pp, pk = fm(col, 128)
                        out_bf(pp, pk, 128, dst[:, q0:q0 + 512])
                    pp, pk = fm(C_GATE, 18)
                    i = cnt["of"] % 3; cnt["of"] += 1
                    T.op("act", lambda pp=pp, i=i: A.activation(out=of[i][0:18, :], in_=pp[0:18, :], func=AF.Sigmoid),
                         reads=[pk], writes=[f"ofA{i}"])
                    T.dma("pool", GT[:, q0:q0 + 512], of[i][0:18, :], reads=[f"ofA{i}"])
                    for c in range(2):
                        pg, pgk = fm(C_CG + c * 128, 128)
                        s_ = sg[c % 2]
                        T.op("act", lambda pg=pg, s_=s_: A.activation(out=s_[:], in_=pg[:], func=AF.Sigmoid),
                             reads=[pgk], writes=[f"sgA{c % 2}"])
                        pa, pak = fm(C_CA + c * 128, 128)
                        i = cnt["of"] % 3; cnt["of"] += 1
                        T.op("dve", lambda pa=pa, s_=s_, i=i: V.tensor_tensor(out=of[i][:], in0=pa[:], in1=s_[:], op=ALU.mult),
                             reads=[pak, f"sgA{c % 2}"], writes=[f"ofA{i}"])
                        T.dma("pool", GLU[c * 128:(c + 1) * 128, q0:q0 + 512], of[i][:], reads=[f"ofA{i}"])
                    for c in range(3):
                        pp, pk = fm(C_FQ + c * 128, 128)
                        out_bf(pp, pk, 128, FQT[c * 128:(c + 1) * 128, q0:q0 + 512])
                    for c in range(3):
                        pp, pk = fm(C_FK + c * 128, 128)
                        out_bf(pp, pk, 128, FKT[c * 128:(c + 1) * 128, q0:q0 + 512])
                    pp, pk = fm(C_FF, 6)
                    i = cnt["of"] % 3; cnt["of"] += 1
                    T.op("act", lambda pp=pp, i=i: A.activation(out=of[i][0:6, :], in_=pp[0:6, :], func=AF.Exp, bias=negbf[:, 0:1], scale=-1.0),
                         reads=[pk, "negbf"], writes=[f"ofA{i}"])
                    T.op("act", lambda i=i: A.activation(out=of[i][0:6, :], in_=of[i][0:6, :], func=AF.Ln, bias=1.0, scale=1.0),
                         reads=[f"ofA{i}"], writes=[f"ofA{i}"])
                    T.op("dve", lambda i=i: V.tensor_scalar(out=of[i][0:6, :], in0=of[i][0:6, :], scalar1=-1.0, scalar2=None, op0=ALU.mult),
                         reads=[f"ofA{i}"], writes=[f"ofA{i}"])
                    T.dma("pool", FLT[:, q0:q0 + 512], of[i][0:6, :], reads=[f"ofA{i}"])
                    for sub in range(4):
                        r0 = q0 + sub * 128
                        n = tt * 4 + sub
                        pa_ = ptm[0]; pb_ = ptm[1]
                        lt = lambda k: xt[:, k, sub * 128:(sub + 1) * 128]
                        T.group("pe",
                                [lambda k=k: P.matmul(pa_[:, 0:128], lhsT=lt(k), rhs=win[:, k, C_VSL:C_VSL + 128], start=(k == 0), stop=(k == 7)) for k in range(8)] +
                                [lambda k=k: P.matmul(pa_[:, 128:256], lhsT=lt(k), rhs=win[:, k, C_VWN:C_VWN + 128], start=(k == 0), stop=(k == 7)) for k in range(8)] +
                                [lambda k=k: P.matmul(pa_[:, 256:274], lhsT=lt(k), rhs=win[:, k, C_GATE:C_GATE + 18], start=(k == 0), stop=(k == 7)) for k in range(8)],
                                reads=["win", xtk], writes=["ptmA0"])
                        T.group("pe",
                                [lambda k=k: P.matmul(pb_[:, 0:384], lhsT=lt(k), rhs=win[:, k, C_FV:C_FV + 384], start=(k == 0), stop=(k == 7)) for k in range(8)],
                                reads=["win", xtk], writes=["ptmA1"])
                        tv = tmv[n % 2]; tg = tmg[n % 2]
                        T.op("dve", lambda tv=tv: V.tensor_copy(out=tv[:, 0:256], in_=pa_[:, 0:256]), reads=["ptmA0"], writes=[f"tmvA{n % 2}"])
                        T.op("act", lambda tg=tg: A.activation(out=tg[:], in_=pa_[:, 256:274], func=AF.Sigmoid), reads=["ptmA0"], writes=[f"tmgA{n % 2}"])
                        T.op("act", lambda tv=tv: A.activation(out=tv[:, 256:640], in_=pb_[:, 0:384], func=AF.Copy), reads=["ptmA1"], writes=[f"tmvA{n % 2}"])
                        T.dma("pool", VSL[r0:r0 + 128, :], tv[:, 0:128], reads=[f"tmvA{n % 2}"])
                        T.dma("pool", VWN[r0:r0 + 128, :], tv[:, 128:256], reads=[f"tmvA{n % 2}"])
                        T.dma("pool", FV[r0:r0 + 128, :], tv[:, 256:640], reads=[f"tmvA{n % 2}"])
                        T.dma("pool", GTM[r0:r0 + 128, :], tg[:], reads=[f"tmgA{n % 2}"])
                T.barrier()

        negc = sb("negc", [128, 64, 6], F32)

        def phase_B(l):
            with contextlib.ExitStack() as st:
                lf = sb("lfB", [6, S], F32, st)
                on6 = sb("on6B", [6, S], F32, st)
                cs = sb("csB", [6, S], F32, st)
                cq = sb("cqB", [6, S], BF16, st)
                pst = ps("pstB", [128, 384], F32, st)
                T.dma("sp", lf[:], FLT[:, :], writes=["lfB"])
                T.op("pool", lambda: G.memset(on6[:], 1.0), writes=["on6B"])
                T.op("dve", lambda: V.tensor_tensor_scan(out=cs[:], data0=on6[:], data1=lf[:], initial=0.0, op0=ALU.mult, op1=ALU.add),
                     reads=["lfB", "on6B"], writes=["csB"])
                T.op("dve", lambda: V.tensor_scalar(out=cq[:], in0=cs[:], scalar1=8.0, scalar2=None, op0=ALU.mult),
                     reads=["csB"], writes=["cqB"])
                T.dma("sp", CQ8[:, :], cq[:], reads=["cqB"])
                if "CFX" in dbg:
                    T.dma("sp", CFX[:, :], cs[:], reads=["csB"])
                T.group("pe", [lambda kt=kt: P.transpose(out=pst[:, kt * 6:(kt + 1) * 6], in_=cs[0:6, kt * 128:(kt + 1) * 128],
                                                         identity=ident_f[0:6, 0:6]) for kt in range(64)],
                        reads=["csB", "ident_f"], writes=["pstB"])
                T.op("dve", lambda: V.tensor_scalar(out=negc[:].rearrange("p k h -> p (k h)"), in0=pst[:], scalar1=-1.0, scalar2=None, op0=ALU.mult),
                     reads=["pstB"], writes=["negc"])
                T.barrier()

        def phase_C(l):
            with contextlib.ExitStack() as st:
                w1 = [sb(f"w1C{i}", [128, 32, 256], BF16, st) for i in range(2)]
                w2 = [sb(f"w2C{i}", [128, 2, 64], BF16, st) for i in range(2)]
                kcmp = [sb(f"kcmpC{k}", [64, 256], BF16, st) for k in range(2)]
                vcmp = [sb(f"vcmpC{k}", [128, 2, 64], BF16, st) for k in range(2)]
                with contextlib.ExitStack() as st2:
                    stg = [sb(f"stgC{i}", [128, 8, 256], F32, st2) for i in range(2)]
                    stg2 = sb("stg2C", [128, 2, 64], F32, st2)
                    src = [sb(f"srcC{i}", [128, S], BF16, st2) for i in range(2)]
                    hT = [[sb(f"hTC{i}{k}", [128, 2, 256], BF16, st2) for k in range(2)] for i in range(2)]
                    tq = [sb(f"tqC{i}", [128, 256], F32, st2) for i in range(2)]
                    ph = [ps(f"phC{i}", [128, 256], F32, st2) for i in range(2)]
                    pk = ps("pkC", [128, 256], F32, st2)
                    n = 0
                    for i in range(2):
                        for t8 in range(4):
                            sg = stg[n % 2]; sk = f"stgC{n % 2}"; n += 1
                            srcap = w_cmp1[l, i, t8 * 512:(t8 + 1) * 512, :].rearrange("(t d) n -> d t n", d=64)
                            T.dma("sp", sg[0:64], srcap, writes=[sk])
                            T.dma("sp", sg[64:128], srcap, writes=[sk])
                            T.op("pool", lambda sg=sg, i=i, t8=t8: G.tensor_copy(out=w1[i][:, t8 * 8:(t8 + 1) * 8, :], in_=sg[:]),
                                 reads=[sk], writes=[f"w1C{i}"])
                        T.dma("sp", stg2[:], w_cmp2[l, i].rearrange("(c p) n -> p c n", p=128), writes=["stg2C"])
                        T.op("dve", lambda i=i: V.tensor_copy(out=w2[i][:], in_=stg2[:]), reads=["stg2C"], writes=[f"w2C{i}"])
                        T.dma("sp", src[i][:], (KCR, VCR)[i][:, :], writes=[f"srcC{i}"])
                    g = 0
                    for i in range(2):
                        for kvh in range(2):
                            sview = src[i][kvh * 64:(kvh + 1) * 64, :].rearrange("p (n t) -> p t n", t=32)
                            for hc in range(2):
                                pp = ph[g % 2]; ppk = f"phC{g % 2}"; tt_ = tq[g % 2]; tk = f"tqC{g % 2}"; g += 1
                                T.group("pe", [lambda t=t, pp=pp, sview=sview: P.matmul(
                                    pp[:], lhsT=w1[i][kvh * 64:(kvh + 1) * 64, t, hc * 128:(hc + 1) * 128], rhs=sview[:, t, :],
                                    start=(t == 0), stop=(t == 31)) for t in range(32)],
                                    reads=[f"w1C{i}", f"srcC{i}"], writes=[ppk])
                                T.op("act", lambda pp=pp, tt_=tt_: A.activation(out=tt_[:], in_=pp[:], func=AF.Square), reads=[ppk], writes=[tk])
                                T.op("dve", lambda tt_=tt_: V.tensor_scalar(out=tt_[:], in0=tt_[:], scalar1=0.044715, scalar2=1.0, op0=ALU.mult, op1=ALU.add),
                                     reads=[tk], writes=[tk])
                                T.op("dve", lambda pp=pp, tt_=tt_: V.tensor_tensor(out=tt_[:], in0=tt_[:], in1=pp[:], op=ALU.mult), reads=[tk, ppk], writes=[tk])
                                T.op("act", lambda tt_=tt_: A.activation(out=tt_[:], in_=tt_[:], func=AF.Sigmoid, scale=1.5957691216), reads=[tk], writes=[tk])
                                T.op("dve", lambda pp=pp, tt_=tt_, i=i, kvh=kvh, hc=hc: V.tensor_tensor(out=hT[i][kvh][:, hc, :], in0=tt_[:], in1=pp[:], op=ALU.mult),
                                     reads=[tk, ppk], writes=[f"hTC{i}{kvh}"])
                    for kvh in range(2):
                        T.group("pe", [lambda hc=hc: P.matmul(pk[0:64, :], lhsT=w2[0][:, hc, :], rhs=hT[0][kvh][:, hc, :], start=(hc == 0), stop=(hc == 1)) for hc in range(2)],
                                reads=["w2C0", f"hTC0{kvh}"], writes=["pkC"])
                        T.op("act", lambda kvh=kvh: A.activation(out=kcmp[kvh][:], in_=pk[0:64, :], func=AF.Copy), reads=["pkC"], writes=[f"kcmpC{kvh}"])
                        for c in range(2):
                            T.group("pe", [lambda hc=hc, c=c: P.matmul(pk[:, 0:64], lhsT=hT[1][kvh][:, hc, c * 128:(c + 1) * 128], rhs=w2[1][:, hc, :],
                                                                   start=(hc == 0), stop=(hc == 1)) for hc in range(2)],
                                    reads=["w2C1", f"hTC1{kvh}"], writes=["pkC"])
                            T.op("act", lambda kvh=kvh, c=c: A.activation(out=vcmp[kvh][:, c, :], in_=pk[:, 0:64], func=AF.Copy), reads=["pkC"], writes=[f"vcmpC{kvh}"])
                    if "KCMP" in dbg:
                        for kvh in range(2):
                            T.dma("sp", KCMP[kvh], kcmp[kvh][:], reads=[f"kcmpC{kvh}"])
                            T.dma("sp", VCMP[kvh].rearrange("(c p) d -> p c d", p=128), vcmp[kvh][:], reads=[f"vcmpC{kvh}"])
                    T.barrier()
                qc = [sb(f"qcC{i}", [64, 3, 128], BF16, st) for i in range(3)]
                gtm = [sb(f"gtmC{i}", [128, 18], F32, st) for i in range(2)]
                zt = [sb(f"zC{i}", [128, 256], F32, st) for i in range(2)]
                et = [sb(f"eC{i}", [128, 256], F32, st) for i in range(2)]
                pb = [sb(f"pbC{i}", [128, 256], BF16, st) for i in range(2)]
                pT = [sb(f"pTC{i}", [128, 2, 128], BF16, st) for i in range(2)]
                sm = [sb(f"smC{i}", [128, 4], F32, st) for i in range(2)]
                imp = [sb(f"impC{i}", [128, 256], F32, st) for i in range(2)]
                sc = [sb(f"scC{i}", [128, 128], F32, st) for i in range(2)]
                sc2 = [sb(f"sc2C{i}", [128, 128], F32, st) for i in range(2)]
                m8 = [sb(f"m8C{i}", [128, 16], F32, st) for i in range(2)]
                msk = [sb(f"mskC{i}", [128, 128], F32, st) for i in range(2)]
                mv = [sb(f"mvC{i}", [128, 128], F32, st) for i in range(3)]
                ocs = [[sb(f"ocsC{h}_{i}", [64, 512], F32, st) for i in range(2)] for h in range(6)]
                mts = [[sb(f"mtsC{h}_{i}", [128, 512], BF16, st) for i in range(2)] for h in range(6)]
                psS = [ps(f"psSC{i}", [128, 256], F32, st) for i in range(2)]
                psT = [ps(f"psTC{i}", [128, 2, 128], BF16, st) for i in range(2)]
                psO = [ps(f"psOC{i}", [64, 128], F32, st) for i in range(2)]
                psM = [ps(f"psMC{i}", [128, 128], F32, st) for i in range(2)]
                nq = S // 128
                it = 0
                for qi in range(nq):
                    q0 = qi * 128
                    gm = gtm[qi % 2]; gmk = f"gtmC{qi % 2}"
                    T.dma("sp", gm[:], GTM[q0:q0 + 128, :], writes=[gmk])
                    blk = qi // 4; sub = qi % 4; par = blk % 2
                    for kvh in range(2):
                        qn = qi * 2 + kvh
                        qq = qc[qn % 3]; qqk = f"qcC{qn % 3}"
                        T.dma("sp", qq[:], QTN[kvh * 192:(kvh + 1) * 192, q0:q0 + 128].rearrange("(g p) t -> p g t", p=64), writes=[qqk])
                        im = imp[qn % 2]; imk = f"impC{qn % 2}"
                        for g in range(3):
                            h = kvh * 3 + g
                            b = it % 2; it += 1
                            T.op("pe", lambda b=b, qq=qq, g=g: P.matmul(psS[b][:], lhsT=qq[:, g, :], rhs=kcmp[kvh][:], start=True, stop=True),
                                 reads=[qqk, f"kcmpC{kvh}"], writes=[f"psSC{b}"])
                            z = zt[b]; zk = f"zC{b}"; e = et[b]; ek = f"eC{b}"; s4 = sm[b]; sk = f"smC{b}"
                            T.op("dve", lambda b=b, z=z, h=h: V.scalar_tensor_tensor(out=z[:], in0=psS[b][:], scalar=0.125, in1=cmptab[:, h, 252 - 4 * qi:508 - 4 * qi],
                                                                                   op0=ALU.mult, op1=ALU.add), reads=[f"psSC{b}", "cmptab"], writes=[zk])
                            T.op("dve", lambda z=z, s4=s4: V.tensor_reduce(out=s4[:, 0:1], in_=z[:], axis=AX.X, op=ALU.max), reads=[zk], writes=[sk])
                            T.op("dve", lambda s4=s4: V.tensor_scalar(out=s4[:, 0:1], in0=s4[:, 0:1], scalar1=-1000.0, scalar2=-1.0, op0=ALU.max, op1=ALU.mult),
                                 reads=[sk], writes=[sk])
                            T.op("act", lambda z=z, e=e, s4=s4: A.activation(out=e[:], in_=z[:], func=AF.Exp, bias=s4[:, 0:1], scale=1.0, accum_out=s4[:, 1:2]),
                                 reads=[zk, sk], writes=[ek, sk])
                            T.op("dve", lambda s4=s4: V.tensor_scalar(out=s4[:, 1:2], in0=s4[:, 1:2], scalar1=1e-30, scalar2=None, op0=ALU.max), reads=[sk], writes=[sk])
                            T.op("dve", lambda s4=s4: V.reciprocal(out=s4[:, 2:3], in_=s4[:, 1:2]), reads=[sk], writes=[sk])
                            T.op("dve", lambda s4=s4, gm=gm, h=h: V.tensor_tensor(out=s4[:, 3:4], in0=s4[:, 2:3], in1=gm[:, h * 3:h * 3 + 1], op=ALU.mult),
                                 reads=[sk, gmk], writes=[sk])
                            T.op("dve", lambda e=e, s4=s4, b=b: V.tensor_scalar(out=pb[b][:], in0=e[:], scalar1=s4[:, 3:4], scalar2=None, op0=ALU.mult),
                                 reads=[ek, sk], writes=[f"pbC{b}"])
                            if g == 0:
                                T.op("dve", lambda e=e, s4=s4, im=im: V.tensor_scalar(out=im[:], in0=e[:], scalar1=s4[:, 2:3], scalar2=None, op0=ALU.mult),
                                     reads=[ek, sk], writes=[imk])
                            else:
                                T.op("dve", lambda e=e, s4=s4, im=im: V.scalar_tensor_tensor(out=im[:], in0=e[:], scalar=s4[:, 2:3], in1=im[:], op0=ALU.mult, op1=ALU.add),
                                     reads=[ek, sk, imk], writes=[imk])
                            T.group("pe", [lambda c=c, b=b: P.transpose(out=psT[b][:, c, :], in_=pb[b][:, c * 128:(c + 1) * 128], identity=ident_b[:]) for c in range(2)],
                                    reads=[f"pbC{b}", "ident_b"], writes=[f"psTC{b}"])
                            T.op("act", lambda b=b: A.activation(out=pT[b][:], in_=psT[b][:], func=AF.Copy), reads=[f"psTC{b}"], writes=[f"pTC{b}"])
                            T.group("pe", [lambda c=c, b=b: P.matmul(psO[b][:], lhsT=vcmp[kvh][:, c, :], rhs=pT[b][:, c, :], start=(c == 0), stop=(c == 1)) for c in range(2)],
                                    reads=[f"vcmpC{kvh}", f"pTC{b}"], writes=[f"psOC{b}"])
                            T.op("act", lambda b=b, h=h: A.activation(out=ocs[h][par][:, sub * 128:(sub + 1) * 128], in_=psO[b][:], func=AF.Copy),
                                 reads=[f"psOC{b}"], writes=[f"ocsC{h}_{par}"])
                            if sub == 3:
                                T.dma("pool", OCT[h][:, blk * 512:(blk + 1) * 512], ocs[h][par][:], reads=[f"ocsC{h}_{par}"])
                        k2 = qn % 2
                        s_ = sc[k2]; s_k = f"scC{k2}"; s2 = sc2[k2]; s2k = f"sc2C{k2}"; m_ = m8[k2]; mk = f"m8C{k2}"; ms = msk[k2]; msk_k = f"mskC{k2}"
                        imv = im[:].rearrange("p (n r) -> p n r", r=2)
                        T.op("dve", lambda s_=s_, imv=imv: V.tensor_tensor(out=s_[:], in0=imv[:, :, 0], in1=imv[:, :, 1], op=ALU.add), reads=[imk], writes=[s_k])
                        T.op("dve", lambda s_=s_: V.tensor_tensor(out=s_[:], in0=s_[:], in1=atab[:, 126 - 2 * qi:254 - 2 * qi], op=ALU.mult), reads=[s_k, "atab"], writes=[s_k])
                        T.op("dve", lambda s_=s_: V.tensor_tensor(out=s_[:], in0=s_[:], in1=btab[:, 126 - 2 * qi:254 - 2 * qi], op=ALU.add), reads=[s_k, "btab"], writes=[s_k])
                        T.op("dve", lambda s_=s_: V.memset(s_[:, 0:1], 4.0), reads=[s_k], writes=[s_k])
                        T.op("dve", lambda s_=s_, m_=m_: V.max(out=m_[:, 0:8], in_=s_[:]), reads=[s_k], writes=[mk])
                        T.op("dve", lambda s_=s_, m_=m_, s2=s2: V.match_replace(out=s2[:], in_to_replace=m_[:, 0:8], in_values=s_[:], imm_value=-1e9),
                             reads=[s_k, mk], writes=[s2k])
                        T.op("dve", lambda m_=m_, s2=s2: V.max(out=m_[:, 8:16], in_=s2[:]), reads=[s2k], writes=[mk])
                        T.op("dve", lambda s_=s_, m_=m_, ms=ms: V.tensor_scalar(out=ms[:], in0=s_[:], scalar1=m_[:, 15:16], scalar2=None, op0=ALU.is_ge),
                             reads=[s_k, mk], writes=[msk_k])
                        if "SCORE" in dbg:
                            T.dma("sp", SCORE[kvh][q0:q0 + 128, :], s_[:], reads=[s_k])
                            T.dma("sp", SELM[kvh][q0:q0 + 128, :], ms[:], reads=[msk_k])
                        for g in range(3):
                            h = kvh * 3 + g
                            mi = (qn * 3 + g) % 3
                            T.op("dve", lambda ms=ms, mi=mi, h=h: V.tensor_scalar(out=mv[mi][:], in0=ms[:], scalar1=NEGM, scalar2=nsaqb[:, h, qi:qi + 1], op0=ALU.mult, op1=ALU.add),
                                 reads=[msk_k, "nsaqb"], writes=[f"mvC{mi}"])
                            pm = (qn * 3 + g) % 2
                            T.op("pe", lambda pm=pm, mi=mi: P.transpose(out=psM[pm][:], in_=mv[mi][:], identity=ident_f[:]), reads=[f"mvC{mi}", "ident_f"], writes=[f"psMC{pm}"])
                            T.op("act", lambda pm=pm, h=h: A.activation(out=mts[h][par][:, sub * 128:(sub + 1) * 128], in_=psM[pm][:], func=AF.Copy),
                                 reads=[f"psMC{pm}"], writes=[f"mtsC{h}_{par}"])
                            if sub == 3:
                                T.dma("pool", MASKT[h][:, blk * 512:(blk + 1) * 512], mts[h][par][:], reads=[f"mtsC{h}_{par}"])
                T.barrier()

        def attn_pass(name, KR, heads, hkey, load_head, load_q, qgroups, bias_ap, ktasks, gate_row, prev, out_ap, out_bf16):
            with contextlib.ExitStack() as st:
                kaug = [sb(f"{name}K{i}", [KR, S], BF16, st) for i in range(2)]
                vaug = [sb(f"{name}V{i}", [128, NKT, 65], BF16, st) for i in range(2)]
                qaug = [sb(f"{name}Q{i}", [KR, 512], BF16, st) for i in range(3)]
                pbuf = [sb(f"{name}P{i}", [128, 512], BF16, st) for i in range(4)]
                rden = [sb(f"{name}rd{i}", [65, 512], F32, st) for i in range(2)]
                grow = [sb(f"{name}gr{i}", [65, 512], F32, st) for i in range(2)]
                rb = [sb(f"{name}rb{i}", [65, 512], BF16, st) for i in range(2)]
                bcs = [sb(f"{name}bc{i}", [64, 512], F32, st) for i in range(2)]
                prv = [sb(f"{name}pv{i}", [64, 512], F32, st) for i in range(2)]
                ot = [sb(f"{name}ot{i}", [64, 512], F32, st) for i in range(2)]
                otb = [sb(f"{name}ob{i}", [64, 512], BF16, st) for i in range(2)]
                sps = [ps(f"{name}s{i}", [128, 512], F32, st) for i in range(4)]
                acc = [ps(f"{name}a{i}", [128, 512], F32, st) for i in range(2)]
                bcp = ps(f"{name}bcp", [64, 512], F32, st)
                for i in range(2):
                    T.op("pool", lambda i=i: G.memset(vaug[i][:, :, 64:65], 1.0), writes=[f"{name}V{i}"])
                    if KR == 65:
                        T.op("pool", lambda i=i: G.memset(kaug[i][64:65, :], 1.0), writes=[f"{name}K{i}"])
                    else:
                        T.op("pool", lambda i=i: G.memset(kaug[i][64:128, :], 1.0), writes=[f"{name}K{i}"])
                        T.op("pool", lambda i=i: G.affine_select(out=kaug[i][64:128, :], in_=kaug[i][64:128, :], pattern=[[0, 2], [1, 64], [0, 64]],
                                                                 compare_op=ALU.is_equal, fill=0.0, base=0, channel_multiplier=-1),
                             reads=[f"{name}K{i}"], writes=[f"{name}K{i}"])
                units = []
                for h in heads:
                    for qt in range(NQT):
                        tl = ktasks(qt)
                        for g in range(qgroups):
                            sel = [t for t in tl if (qgroups == 1 or (t[0] // 32) == g)]
                            if sel:
                                units.append((h, qt, g, sel))
                flat = []
                for ui, (h, qt, g, sel) in enumerate(units):
                    first_of_hq = (ui == 0 or units[ui - 1][0] != h or units[ui - 1][1] != qt)
                    last_of_hq = (ui == len(units) - 1 or units[ui + 1][0] != h or units[ui + 1][1] != qt)
                    for j, t in enumerate(sel):
                        flat.append((ui, h, qt, t, first_of_hq and j == 0, last_of_hq and j == len(sel) - 1))
                state = {"hk": None, "hslot": -1, "uload": -1, "hq": -1}
                hk_list = []
                for h in heads:
                    if not hk_list or hk_list[-1] != hkey(h):
                        hk_list.append(hkey(h))

                def ensure_head(hk):
                    if state["hk"] == hk:
                        return
                    state["hk"] = hk
                    state["hslot"] += 1
                    sl = state["hslot"] % 2
                    load_head(hk, kaug[sl], f"{name}K{sl}", vaug[sl], f"{name}V{sl}")

                def ensure_unit(ui):
                    while state["uload"] < ui:
                        state["uload"] += 1
                        u = state["uload"]
                        h, qt, g, _ = units[u]
                        load_q(h, qt, g, qaug[u % 3], f"{name}Q{u % 3}")

                def emit_S(i):
                    ui, h, qt, (kt, c0, c1, masks), first, last = flat[i]
                    ensure_head(hkey(h))
                    ensure_unit(min(ui + 1, len(units) - 1) if False else ui)
                    sl = state["hslot"] % 2
                    b = i % 4
                    q = qaug[ui % 3]
                    T.op("pe", lambda: P.matmul(sps[b][:, c0:c1], lhsT=kaug[sl][0:KR, kt * 128:(kt + 1) * 128], rhs=q[0:KR, c0:c1], start=True, stop=True),
                         reads=[f"{name}K{sl}", f"{name}Q{ui % 3}"], writes=[f"{name}s{b}"])
                    T.op("act", lambda: A.activation(out=pbuf[b][:, c0:c1], in_=sps[b][:, c0:c1], func=AF.Exp, bias=bias_ap(h, kt), scale=0.125),
                         reads=[f"{name}s{b}"], writes=[f"{name}P{b}"])
                    for (mc, kind) in masks:
                        if kind == "diag":
                            T.op("pool", lambda mc=mc: G.affine_select(out=pbuf[b][:, mc:mc + 128], in_=pbuf[b][:, mc:mc + 128], pattern=[[1, 128]],
                                                                       compare_op=ALU.is_ge, fill=0.0, base=0, channel_multiplier=-1),
                                 reads=[f"{name}P{b}"], writes=[f"{name}P{b}"])
                        else:
                            T.op("pool", lambda mc=mc: G.affine_select(out=pbuf[b][:, mc:mc + 128], in_=pbuf[b][:, mc:mc + 128], pattern=[[-1, 128]],
                                                                       compare_op=ALU.is_ge, fill=0.0, base=-1, channel_multiplier=1),
                                 reads=[f"{name}P{b}"], writes=[f"{name}P{b}"])
                    return sl

                slot_of = {}

                def emit_PV(i):
                    ui, h, qt, (kt, c0, c1, masks), first, last = flat[i]
                    if first:
                        state["hq"] += 1
                    hq = state["hq"]
                    ab = hq % 2
                    b = i % 4
                    sl = slot_of[i]
                    T.op("pe", lambda: P.matmul(acc[ab][0:65, c0:c1], lhsT=vaug[sl][:, kt, :], rhs=pbuf[b][:, c0:c1], start=first, stop=last),
                         reads=[f"{name}V{sl}", f"{name}P{b}"], writes=[f"{name}a{ab}"])
                    if last:
                        finalize(h, qt, hq)

                def finalize(h, qt, hq):
                    ab = hq % 2
                    q0 = qt * 512
                    ak = f"{name}a{ab}"
                    r_ = rden[ab]; g_ = grow[ab]; rb_ = rb[ab]; bc_ = bcs[ab]; pv_ = prv[ab]; o_ = ot[ab]; ob_ = otb[ab]
                    if gate_row is not None:
                        T.dma("sp", g_[64:65, :], GT[gate_row(h):gate_row(h) + 1, q0:q0 + 512], writes=[f"{name}gr{ab}"])
                        T.op("dve", lambda: V.reciprocal(out=r_[64:65, :], in_=acc[ab][64:65, :]), reads=[ak], writes=[f"{name}rd{ab}"])
                        T.op("dve", lambda: V.tensor_tensor(out=rb_[64:65, :], in0=r_[64:65, :], in1=g_[64:65, :], op=ALU.mult),
                             reads=[f"{name}rd{ab}", f"{name}gr{ab}"], writes=[f"{name}rb{ab}"])
                    else:
                        T.op("dve", lambda: V.reciprocal(out=r_[64:65, :], in_=acc[ab][64:65, :]), reads=[ak], writes=[f"{name}rd{ab}"])
                        T.op("dve", lambda: V.tensor_copy(out=rb_[64:65, :], in_=r_[64:65, :]), reads=[f"{name}rd{ab}"], writes=[f"{name}rb{ab}"])
                    if prev is not None:
                        T.dma("sp", pv_[:], prev(h)[:, q0:q0 + 512], writes=[f"{name}pv{ab}"])
                    T.op("pe", lambda: P.matmul(bcp[:], lhsT=ones_b[64:65, 0:64], rhs=rb_[64:65, :], start=True, stop=True),
                         reads=[f"{name}rb{ab}", "ones_b"], writes=[f"{name}bcp"])
                    T.op("act", lambda: A.activation(out=bc_[:], in_=bcp[:], func=AF.Copy), reads=[f"{name}bcp"], writes=[f"{name}bc{ab}"])
                    if prev is None:
                        tgt, tk = (ob_, f"{name}ob{ab}") if out_bf16 else (o_, f"{name}ot{ab}")
                        T.op("dve", lambda: V.tensor_tensor(out=tgt[:], in0=acc[ab][0:64, :], in1=bc_[:], op=ALU.mult),
                             reads=[ak, f"{name}bc{ab}"], writes=[tk])
                    else:
                        T.op("dve", lambda: V.tensor_tensor(out=o_[:], in0=acc[ab][0:64, :], in1=bc_[:], op=ALU.mult),
                             reads=[ak, f"{name}bc{ab}"], writes=[f"{name}ot{ab}"])
                        tgt, tk = (ob_, f"{name}ob{ab}") if out_bf16 else (o_, f"{name}ot{ab}")
                        T.op("pool", lambda: G.tensor_tensor(out=tgt[:], in0=o_[:], in1=pv_[:], op=ALU.add),
                             reads=[f"{name}ot{ab}", f"{name}pv{ab}"], writes=[tk])
                    T.dma("pool", out_ap(h)[:, q0:q0 + 512], tgt[:], reads=[tk])

                LA = 2
                n = len(flat)
                for i in range(n + LA):
                    if i < n:
                        slot_of[i] = emit_S(i)
                    if i >= LA:
                        emit_PV(i - LA)
                T.barrier()

        def kt_causal(qt):
            tl = [(kt, 0, 512, []) for kt in range(4 * qt)]
            for j in range(4):
                tl.append((4 * qt + j, 128 * j, 512, [(128 * j, "diag")]))
            return tl

        def kt_window(qt):
            tl = []
            for m in (0, 1, 2, 3):
                tl.append((4 * qt + m, 128 * m, 512, [(128 * m, "diag")]))
            for m in (-1, -2, -3, -4):
                kt = 4 * qt + m
                if kt < 0:
                    continue
                tl.append((kt, 0, 128 * (m + 5), [(128 * (m + 4), "wtri")]))
            return tl

        def phase_D(l, which="fsw"):
            def vload(src_ap, vt, vk):
                sv = src_ap.rearrange("(k p) d -> p k d", p=128)
                for k0 in range(0, NKT, 4):
                    T.dma("sp", vt[:, k0:k0 + 4, 0:64], sv[:, k0:k0 + 4, :], writes=[vk])

            if "f" in which:
                def lh(hk, kt_, kk, vt, vk):
                    T.dma("sp", kt_[0:64, :], FKT[hk * 64:(hk + 1) * 64, :], writes=[kk])
                    vload(FV[:, hk * 64:(hk + 1) * 64], vt, vk)

                def lq(h, qt, g, qt_, qk):
                    T.dma("sp", qt_[0:64, :], FQT[h * 64:(h + 1) * 64, qt * 512:(qt + 1) * 512], writes=[qk])
                    T.dma("sp", qt_[64:65, :], CQ8[h:h + 1, qt * 512:(qt + 1) * 512], writes=[qk])
                attn_pass("fx", 65, DBG_HEADS or list(range(6)), lambda h: h, lh, lq, 1, lambda h, kt: negc[:, kt, h:h + 1], kt_causal,
                          None, None, lambda h: MIXT[640 + h * 64:640 + (h + 1) * 64, :], True)
            if "s" in which:
                def lh(hk, kt_, kk, vt, vk):
                    T.dma("sp", kt_[0:64, :], KSL[hk * 64:(hk + 1) * 64, :], writes=[kk])
                    vload(VSL[:, hk * 64:(hk + 1) * 64], vt, vk)

                def lq(h, qt, g, qt_, qk):
                    T.dma("sp", qt_[0:64, :], QTN[h * 64:(h + 1) * 64, qt * 512:(qt + 1) * 512], writes=[qk])
                    T.dma("sp", qt_[64:128, :], MASKT[h][g * 64:(g + 1) * 64, qt * 512:(qt + 1) * 512], writes=[qk])
                attn_pass("sl", 128, DBG_HEADS or list(range(6)), lambda h: h // 3, lh, lq, 2, lambda h, kt: nsakb[:, h, kt:kt + 1], kt_causal,
                          lambda h: h * 3 + 1, lambda h: OCT[h], lambda h: OST[h], False)
            if "w" in which:
                def lh(hk, kt_, kk, vt, vk):
                    T.dma("sp", kt_[0:64, :], KWN[hk * 64:(hk + 1) * 64, :], writes=[kk])
                    vload(VWN[:, hk * 64:(hk + 1) * 64], vt, vk)

                def lq(h, qt, g, qt_, qk):
                    T.dma("sp", qt_[0:64, :], QTN[h * 64:(h + 1) * 64, qt * 512:(qt + 1) * 512], writes=[qk])
                    T.dma("sp", qt_[64:65, :], NSACQ[h:h + 1, qt * 512:(qt + 1) * 512], writes=[qk])
                attn_pass("wn", 65, DBG_HEADS or list(range(6)), lambda h: h // 3, lh, lq, 1, lambda h, kt: nsakb[:, h, kt:kt + 1], kt_window,
                          lambda h: h * 3 + 2, lambda h: OST[h], lambda h: MIXT[h * 64:(h + 1) * 64, :], True)

        def phase_E(l):
            with contextlib.ExitStack() as st:
                wdw = sb("wdwE", [128, 2, 31], F32, st)
                bdw = sb("bdwE", [128, 2], F32, st)
                lng = sb("lngE", [128, 2], F32, st)
                lnb = sb("lnbE", [128, 2], F32, st)
                U = [sb(f"UE{i}", [128, 542], F32, st) for i in range(4)]
                Hh = [sb(f"HE{i}", [128, 512], F32, st) for i in range(4)]
                Hq = [sb(f"HqE{i}", [128, 512], F32, st) for i in range(2)]
                mean = sb("meanE", [128, 512], F32, st)
                msq = sb("msqE", [128, 512], F32, st)
                rstd = sb("rstdE", [128, 512], F32, st)
                xh = [sb(f"xhE{i}", [128, 512], F32, st) for i in range(2)]
                ob = [sb(f"obE{i}", [128, 512], BF16, st) for i in range(2)]
                p1 = ps("p1E", [128, 512], F32, st)
                p2 = ps("p2E", [128, 512], F32, st)
                transpose_load("wdwE", w_dw[l], 31, 2, lambda c: wdw[:, c, :])
                transpose_load("bdwE", b_dw[l].rearrange("(o n) -> o n", o=1), 1, 2, lambda c: bdw[:, c:c + 1])
                transpose_load("lngE", ln_conv_g[l].rearrange("(o n) -> o n", o=1), 1, 2, lambda c: lng[:, c:c + 1])
                transpose_load("lnbE", ln_conv_b[l].rearrange("(o n) -> o n", o=1), 1, 2, lambda c: lnb[:, c:c + 1])
                n = 0
                for tt in range(NQT):
                    q0 = tt * 512
                    hs = []
                    for c in range(2):
                        u = U[n % 4]; uk = f"UE{n % 4}"; hh = Hh[n % 4]; hk = f"HE{n % 4}"; n += 1
                        if tt == 0:
                            T.op("pool", lambda u=u: G.memset(u[:, 0:30], 0.0), writes=[uk])
                            T.dma("sp", u[:, 30:542], GLU[c * 128:(c + 1) * 128, 0:512], writes=[uk])
                        else:
                            T.dma("sp", u[:], GLU[c * 128:(c + 1) * 128, q0 - 30:q0 + 512], writes=[uk])
                        T.op("dve", lambda u=u, hh=hh, c=c: V.tensor_scalar(out=hh[:], in0=u[:, 0:512], scalar1=wdw[:, c, 0:1], scalar2=bdw[:, c:c + 1],
                                                                            op0=ALU.mult, op1=ALU.add), reads=[uk, "wdwE", "bdwE"], writes=[hk])
                        for k in range(1, 31):
                            T.op("dve", lambda u=u, hh=hh, c=c, k=k: V.scalar_tensor_tensor(out=hh[:], in0=u[:, k:k + 512], scalar=wdw[:, c, k:k + 1], in1=hh[:],
                                                                                          op0=ALU.mult, op1=ALU.add), reads=[uk, "wdwE", hk], writes=[hk])
                        hq = Hq[c]
                        T.op("act", lambda hh=hh, hq=hq: A.activation(out=hq[:], in_=hh[:], func=AF.Square), reads=[hk], writes=[f"HqE{c}"])
                        hs.append((hh, hk))
                    T.group("pe", [lambda c=c: P.matmul(p1[:], lhsT=ones_f[:], rhs=hs[c][0][:], start=(c == 0), stop=(c == 1)) for c in range(2)],
                            reads=["ones_f", hs[0][1], hs[1][1]], writes=["p1E"])
                    T.group("pe", [lambda c=c: P.matmul(p2[:], lhsT=ones_f[:], rhs=Hq[c][:], start=(c == 0), stop=(c == 1)) for c in range(2)],
                            reads=["ones_f", "HqE0", "HqE1"], writes=["p2E"])
                    T.op("dve", lambda: V.tensor_scalar(out=mean[:], in0=p1[:], scalar1=1.0 / 256, scalar2=None, op0=ALU.mult), reads=["p1E"], writes=["meanE"])
                    T.op("dve", lambda: V.tensor_tensor(out=msq[:], in0=mean[:], in1=mean[:], op=ALU.mult), reads=["meanE"], writes=["msqE"])
                    T.op("dve", lambda: V.scalar_tensor_tensor(out=rstd[:], in0=p2[:], scalar=1.0 / 256, in1=msq[:], op0=ALU.mult, op1=ALU.subtract),
                         reads=["p2E", "msqE"], writes=["rstdE"])
                    T.op("act", lambda: A.activation(out=rstd[:], in_=rstd[:], func=AF.Sqrt, bias=epsb[:, 0:1], scale=1.0), reads=["rstdE", "epsb"], writes=["rstdE"])
                    T.op("dve", lambda: V.reciprocal(out=rstd[:], in_=rstd[:]), reads=["rstdE"], writes=["rstdE"])
                    for c in range(2):
                        hh, hk = hs[c]
                        x_ = xh[c]; xk = f"xhE{c}"
                        T.op("dve", lambda hh=hh, x_=x_: V.tensor_tensor(out=x_[:], in0=hh[:], in1=mean[:], op=ALU.subtract), reads=[hk, "meanE"], writes=[xk])
                        T.op("dve", lambda x_=x_: V.tensor_tensor(out=x_[:], in0=x_[:], in1=rstd[:], op=ALU.mult), reads=[xk, "rstdE"], writes=[xk])
                        T.op("dve", lambda x_=x_, c=c: V.tensor_scalar(out=x_[:], in0=x_[:], scalar1=lng[:, c:c + 1], scalar2=lnb[:, c:c + 1], op0=ALU.mult, op1=ALU.add),
                             reads=[xk, "lngE", "lnbE"], writes=[xk])
                        o_ = ob[c]
                        T.op("act", lambda x_=x_, o_=o_: A.activation(out=o_[:], in_=x_[:], func=AF.Silu), reads=[xk], writes=[f"obE{c}"])
                        T.dma("pool", MIXT[384 + c * 128:384 + (c + 1) * 128, q0:q0 + 512], o_[:], reads=[f"obE{c}"])
                T.barrier()

        def phase_F(l):
            with contextlib.ExitStack() as st:
                wo = sb("woF", [128, 8, D], BF16, st)
                stg = [sb(f"stgF{i}", [128, D], F32, st) for i in range(2)]
                load_cast(st, "woF", lambda c: wo[:, c, :], lambda c: w_out[l, c * 128:(c + 1) * 128, :], 8, D, stg)
                gt = sb("gF", [128, D], F32, st); bt = sb("bF", [128, D], F32, st)
                load_gb(gt, bt, ln1_g[l], ln1_b[l])
                mt = [sb(f"mtF{i}", [128, 8, 128], BF16, st) for i in range(3)]
                xi = [sb(f"xiF{i}", [128, D], F32, st) for i in range(3)]
                pp = [ps(f"ppF{i}", [128, 512], F32, st) for i in range(4)]
                for ti in range(S // 128):
                    r0 = ti * 128
                    m = mt[ti % 3]; mk = f"mtF{ti % 3}"; x_ = xi[ti % 3]; xk = f"xiF{ti % 3}"
                    for c in range(8):
                        T.dma("sp", m[:, c, :], MIXT[c * 128:(c + 1) * 128, r0:r0 + 128], writes=[mk])
                    T.dma("sp", x_[:], XA[r0:r0 + 128, :], writes=[xk])
                    for nh in range(2):
                        pb = pp[(ti * 2 + nh) % 4]; pk = f"ppF{(ti * 2 + nh) % 4}"
                        T.group("pe", [lambda k=k, pb=pb, nh=nh, m=m: P.matmul(pb[:], lhsT=m[:, k, :], rhs=wo[:, k, nh * 512:(nh + 1) * 512], start=(k == 0), stop=(k == 7))
                                       for k in range(8)], reads=[mk, "woF"], writes=[pk])
                        T.op("dve", lambda pb=pb, nh=nh, x_=x_: V.scalar_tensor_tensor(out=x_[:, nh * 512:(nh + 1) * 512], in0=x_[:, nh * 512:(nh + 1) * 512], scalar=ALPHA,
                                                                                   in1=pb[:], op0=ALU.mult, op1=ALU.add), reads=[pk, xk], writes=[xk])
                    layer_norm_rows(LNT, x_, xk, (gt, bt))
                    T.dma("pool", XB[r0:r0 + 128, :], x_[:], reads=[xk])
                T.barrier()

        def phase_G(l, last):
            TT = 256
            with contextlib.ExitStack() as st:
                wfi = sb("wfiG", [128, 8, 2 * DFF], BF16, st)
                wfd = sb("wfdG", [128, NF, D], BF16, st)
                with contextlib.ExitStack() as st2:
                    stg = [sb(f"stgG{i}", [128, 2048], F32, st2) for i in range(2)]
                    n = 0
                    for c in range(8):
                        for q in range(3):
                            c0 = q * 2048; w_ = min(2048, 2 * DFF - c0)
                            sg = stg[n % 2]; sk = f"stgG{n % 2}"; n += 1
                            T.dma("sp", sg[:, :w_], w_ffn_in[l, c * 128:(c + 1) * 128, c0:c0 + w_], writes=[sk])
                            eng = ("dve", "pool")[n % 2]; E = V if eng == "dve" else G
                            T.op(eng, lambda E=E, sg=sg, c=c, c0=c0, w_=w_: E.tensor_copy(out=wfi[:, c, c0:c0 + w_], in_=sg[:, :w_]), reads=[sk], writes=["wfiG"])
                    for c in range(NF):
                        sg = stg[n % 2]; sk = f"stgG{n % 2}"; n += 1
                        T.dma("sp", sg[:, :D], w_ffn_down[l, c * 128:(c + 1) * 128, :], writes=[sk])
                        eng = ("dve", "pool")[n % 2]; E = V if eng == "dve" else G
                        T.op(eng, lambda E=E, sg=sg, c=c: E.tensor_copy(out=wfd[:, c, :], in_=sg[:, :D]), reads=[sk], writes=["wfdG"])
                    T.barrier()
                gt = sb("gG", [128, D], F32, st); bt = sb("bG", [128, D], F32, st)
                load_gb(gt, bt, ln2_g[l], ln2_b[l])
                wcv = sb("wcvG", [128, NF, 3], F32, st)
                bcv = sb("bcvG", [128, NF], F32, st)
                transpose_load("wcvG", w_ffn_conv[l], 3, NF, lambda c: wcv[:, c, :])
                transpose_load("bcvG", b_ffn_conv[l].rearrange("(o n) -> o n", o=1), 1, NF, lambda c: bcv[:, c:c + 1])
                halo = sb("haloG", [128, NF, 2], F32, st)
                T.op("pool", lambda: G.memset(halo[:], 0.0), writes=["haloG"])
                xi = [sb(f"xiG{i}", [128, D], F32, st) for i in range(3)]
                xb = [sb(f"xbG{i}", [128, D], BF16, st) for i in range(2)]
                xT = [sb(f"xTG{i}", [128, 8, TT], BF16, st) for i in range(2)]
                hT = [sb(f"hTG{i}", [128, NF, TT], BF16, st) for i in range(1)]
                Gs = [sb(f"GsG{i}", [128, TT + 2], F32, st) for i in range(2)]
                t1 = [sb(f"t1G{i}", [128, TT], F32, st) for i in range(2)]
                sgl = [sb(f"sgG{i}", [128, TT], F32, st) for i in range(2)]
                ptr = [ps(f"ptrG{i}", [128, 1024], BF16, st) for i in range(2)]
                pg = [ps(f"pgG{i}", [128, TT], F32, st) for i in range(2)]
                pu = [ps(f"puG{i}", [128, TT], F32, st) for i in range(2)]
                pd = [ps(f"pdG{i}", [128, 512], F32, st) for i in range(2)]
                nsub = TT // 128
                dst = y_out if last else XA
                it = 0
                for tt in range(S // TT):
                    xt = xT[tt % 2]; xtk = f"xTG{tt % 2}"; ht = hT[0]; htk = "hTG0"
                    xs = []
                    for sub in range(nsub):
                        nn = tt * nsub + sub
                        r0 = nn * 128
                        x_ = xi[nn % 3]; xk = f"xiG{nn % 3}"; b_ = xb[nn % 2]; bk = f"xbG{nn % 2}"
                        T.dma("sp", x_[:], XB[r0:r0 + 128, :], writes=[xk])
                        T.op("pool", lambda b_=b_, x_=x_: G.tensor_copy(out=b_[:], in_=x_[:]), reads=[xk], writes=[bk])
                        pt = ptr[nn % 2]; ptk = f"ptrG{nn % 2}"
                        T.group("pe", [lambda k=k, pt=pt, b_=b_: P.transpose(out=pt[:, k * 128:(k + 1) * 128], in_=b_[:, k * 128:(k + 1) * 128], identity=ident_b[:]) for k in range(8)],
                                reads=[bk, "ident_b"], writes=[ptk])
                        T.op("act", lambda pt=pt, xt=xt, sub=sub: A.activation(out=xt[:, :, sub * 128:(sub + 1) * 128], in_=pt[:].rearrange("p (k t) -> p k t", k=8), func=AF.Copy),
                             reads=[ptk], writes=[xtk])
                        xs.append((x_, xk, r0))
                    for fc in range(NF):
                        b = it % 2; it += 1
                        T.group("pe", [lambda k=k, b=b, fc=fc: P.matmul(pg[b][:], lhsT=wfi[:, k, fc * 128:(fc + 1) * 128], rhs=xt[:, k, :], start=(k == 0), stop=(k == 7)) for k in range(8)],
                                reads=["wfiG", xtk], writes=[f"pgG{b}"])
                        T.group("pe", [lambda k=k, b=b, fc=fc: P.matmul(pu[b][:], lhsT=wfi[:, k, DFF + fc * 128:DFF + (fc + 1) * 128], rhs=xt[:, k, :], start=(k == 0), stop=(k == 7)) for k in range(8)],
                                reads=["wfiG", xtk], writes=[f"puG{b}"])
                        gs = Gs[b]; gk = f"GsG{b}"; t_ = t1[b]; tk = f"t1G{b}"; s_ = sgl[b]; sk = f"sgG{b}"
                        T.op("pool", lambda gs=gs, fc=fc: G.tensor_copy(out=gs[:, 0:2], in_=halo[:, fc, :]), reads=["haloG"], writes=[gk])
                        T.op("act", lambda gs=gs, b=b: A.activation(out=gs[:, 2:TT + 2], in_=pg[b][:], func=AF.Copy), reads=[f"pgG{b}"], writes=[gk])
                        T.op("pool", lambda gs=gs, fc=fc: G.tensor_copy(out=halo[:, fc, :], in_=gs[:, TT:TT + 2]), reads=[gk], writes=["haloG"])
                        T.op("dve", lambda gs=gs, t_=t_, fc=fc: V.tensor_scalar(out=t_[:], in0=gs[:, 0:TT], scalar1=wcv[:, fc, 0:1], scalar2=bcv[:, fc:fc + 1], op0=ALU.mult, op1=ALU.add),
                             reads=[gk, "wcvG", "bcvG"], writes=[tk])
                        T.op("dve", lambda gs=gs, t_=t_, fc=fc: V.scalar_tensor_tensor(out=t_[:], in0=gs[:, 1:TT + 1], scalar=wcv[:, fc, 1:2], in1=t_[:], op0=ALU.mult, op1=ALU.add),
                             reads=[gk, "wcvG", tk], writes=[tk])
                        T.op("dve", lambda gs=gs, t_=t_, fc=fc: V.scalar_tensor_tensor(out=t_[:], in0=gs[:, 2:TT + 2], scalar=wcv[:, fc, 2:3], in1=t_[:], op0=ALU.mult, op1=ALU.add),
                             reads=[gk, "wcvG", tk], writes=[tk])
                        T.op("act", lambda t_=t_, s_=s_: A.activation(out=s_[:], in_=t_[:], func=AF.Silu), reads=[tk], writes=[sk])
                        T.op("dve", lambda s_=s_, b=b, fc=fc: V.tensor_tensor(out=ht[:, fc, :], in0=s_[:], in1=pu[b][:], op=ALU.mult), reads=[sk, f"puG{b}"], writes=[htk])
                    for sub in range(nsub):
                        x_, xk, r0 = xs[sub]
                        for nh in range(2):
                            pb = pd[nh]; pk = f"pdG{nh}"
                            T.group("pe", [lambda fc=fc, pb=pb, nh=nh, sub=sub: P.matmul(pb[:], lhsT=ht[:, fc, sub * 128:(sub + 1) * 128], rhs=wfd[:, fc, nh * 512:(nh + 1) * 512],
                                                                                     start=(fc == 0), stop=(fc == NF - 1)) for fc in range(NF)],
                                    reads=[htk, "wfdG"], writes=[pk])
                            T.op("dve", lambda pb=pb, nh=nh, x_=x_: V.scalar_tensor_tensor(out=x_[:, nh * 512:(nh + 1) * 512], in0=x_[:, nh * 512:(nh + 1) * 512], scalar=ALPHA,
                                                                                       in1=pb[:], op0=ALU.mult, op1=ALU.add), reads=[pk, xk], writes=[xk])
                        layer_norm_rows(LNT, x_, xk, (gt, bt))
                        T.dma("pool", dst[r0:r0 + 128, :], x_[:], reads=[xk])
                T.barrier()

        for l in range(DEPTH):
            if layers is not None and l not in layers:
                continue
            if "A" in phases:
                phase_A(l)
            if "B" in phases:
                phase_B(l)
            if "C" in phases:
                phase_C(l)
            if "D" in phases:
                phase_D(l, DBG_WHICH)
            if "E" in phases:
                phase_E(l)
            if "F" in phases:
                phase_F(l)
            if "G" in phases:
                phase_G(l, l == DEPTH - 1)
        T.barrier()
        print("ninst", T.ninst, "nsem", T.nsem)
    return nc


_NC = None


def kernel(**inputs):
    global _NC
    if _NC is None:
        _NC = build()
    x = np.asarray(inputs["x"], dtype=np.float32)
    nb = x.shape[0]
    shared = {k: np.ascontiguousarray(np.asarray(v, dtype=np.float32)) for k, v in inputs.items() if k != "x"}
    in_maps = []
    for b in range(nb):
        m = dict(shared)
        m["x"] = np.ascontiguousarray(x[b])
        in_maps.append(m)
    res = run_bass_kernel_spmd(_NC, in_maps, core_ids=list(range(nb)))
    return np.stack([np.asarray(r["y"], dtype=np.float32) for r in res.results], axis=0)
```

```python
import contextlib
import math
import numpy as np
import concourse.bass as bass
import concourse.mybir as mybir
from concourse.bass_utils import run_bass_kernel_spmd

F32 = mybir.dt.float32
BF16 = mybir.dt.bfloat16
I32 = mybir.dt.int32
AF = mybir.ActivationFunctionType
ALU = mybir.AluOpType
AX = mybir.AxisListType

S = 8192
D = 1024
DEPTH = 2
NQT = S // 512
NKT = S // 128
PROJ_W = 2840
DFF = 2816
NF = DFF // 128
ALPHA = (2.0 * DEPTH) ** 0.25
LN_EPS = 1e-5
SLOPES = [2.0 ** (-8.0 * (i + 1) / 6) for i in range(6)]
NEGM = 240000.0
DBG_NTT = None
DBG_WHICH = "fsw"
DBG_HEADS = None
C_NQ, C_KCR, C_VCR, C_KSL, C_VSL, C_KWN, C_VWN, C_GATE, C_CA, C_CG, C_FQ, C_FK, C_FV, C_FF = (
    0, 384, 512, 640, 768, 896, 1024, 1152, 1170, 1426, 1682, 2066, 2450, 2834)


class Ev:
    __slots__ = ("sem", "val", "src")

    def __init__(self, sem, val, src):
        self.sem, self.val, self.src = sem, val, src


class Tracker:
    def __init__(self, nc, stack):
        self.nc = nc
        self.stack = stack
        self.eng = {"pe": nc.tensor, "act": nc.scalar, "dve": nc.vector, "pool": nc.gpsimd, "sp": nc.sync}
        self.sem = {}
        self.cnt = {}
        self.nsem = 0
        for e in self.eng:
            self._new_epoch(e)
        self.seen = {e: {} for e in self.eng}
        self.last_w = {}
        self.readers = {}
        self.last_ev = {}
        self.dma_sems = {}
        self.dma_rr = {}
        for q, n in (("sp", 20), ("pool", 12), ("act", 6)):
            self.dma_sems[q] = [[self._mk_sem(f"d{q}{i}"), 0] for i in range(n)]
            self.dma_rr[q] = 0
        self.ninst = 0
        self.excl = set()

    def _mk_sem(self, name):
        self.nsem += 1
        return self.stack.enter_context(self.nc.semaphore(f"{name}_{self.nsem}"))

    def _new_epoch(self, e):
        self.sem[e] = self._mk_sem(f"s{e}")
        self.cnt[e] = 0

    def wait(self, e, ev):
        k = id(ev.sem)
        if self.seen[e].get(k, 0) >= ev.val:
            return
        self.eng[e].wait_ge(ev.sem, ev.val)
        self.seen[e][k] = ev.val

    def _split(self, reads, writes):
        if self.excl:
            ex = [k for k in reads if k in self.excl]
            if ex:
                reads = [k for k in reads if k not in self.excl]
                writes = list(writes) + [k for k in ex if k not in writes]
        return reads, writes

    def _deps(self, e, reads, writes, srcid):
        deps = []
        for k in reads:
            w = self.last_w.get(k)
            if w is not None:
                deps.append(w)
        for k in writes:
            w = self.last_w.get(k)
            if w is not None and w.src != srcid:
                deps.append(w)
            for r in self.readers.get(k, {}).values():
                if r.src != srcid:
                    deps.append(r)
        for d in deps:
            self.wait(e, d)

    def _record(self, ev, reads, writes):
        for k in reads:
            self.readers.setdefault(k, {})[ev.src] = ev
        for k in writes:
            self.last_w[k] = ev
            self.readers[k] = {}

    def op(self, e, fn, reads=(), writes=()):
        reads, writes = self._split(reads, writes)
        self._deps(e, reads, writes, e)
        inst = fn()
        if self.cnt[e] >= 30000:
            self._new_epoch(e)
        self.cnt[e] += 1
        inst.then_inc(self.sem[e], 1)
        ev = Ev(self.sem[e], self.cnt[e], e)
        self.last_ev[e] = ev
        self._record(ev, reads, writes)
        self.ninst += 1
        return ev

    def group(self, e, fns, reads=(), writes=()):
        reads, writes = self._split(reads, writes)
        self._deps(e, reads, writes, e)
        inst = None
        for fn in fns:
            inst = fn()
            self.ninst += 1
        if self.cnt[e] >= 30000:
            self._new_epoch(e)
        self.cnt[e] += 1
        inst.then_inc(self.sem[e], 1)
        ev = Ev(self.sem[e], self.cnt[e], e)
        self.last_ev[e] = ev
        self._record(ev, reads, writes)
        return ev

    def dma(self, q, out, in_, reads=(), writes=(), **kw):
        pool = self.dma_sems[q]
        i = self.dma_rr[q]
        self.dma_rr[q] = (i + 1) % len(pool)
        ent = pool[i]
        if ent[1] >= 30000:
            ent[0] = self._mk_sem(f"d{q}")
            ent[1] = 0
        sem, val = ent
        srcid = ("dma", id(sem))
        if val > 0:
            self.wait(q, Ev(sem, val, srcid))
        self._deps(q, reads, writes, srcid)
        self.eng[q].dma_start(out=out, in_=in_, **kw).then_inc(sem, 16)
        ent[1] = val + 16
        ev = Ev(sem, ent[1], srcid)
        self._record(ev, reads, writes)
        self.ninst += 1
        return ev

    def barrier(self):
        evs = list(self.last_ev.values())
        for q in self.dma_sems:
            for sem, val in self.dma_sems[q]:
                if val > 0:
                    evs.append(Ev(sem, val, ("dma", id(sem))))
        for e in self.eng:
            for ev in evs:
                self.wait(e, ev)
        self.last_w.clear()
        self.readers.clear()


def build(dbg=(), phases="ABCDEFGH", layers=None):
    nc = bass.Bass("TRN2", target_bir_lowering=False)
    ES = contextlib.ExitStack()
    with ES:
        T = Tracker(nc, ES)
        V, A, P, G = nc.vector, nc.scalar, nc.tensor, nc.gpsimd

        def dram_in(name, shape):
            return nc.dram_tensor(name, list(shape), F32, kind="ExternalInput").ap()

        def dram(name, shape, dt):
            kind = "ExternalOutput" if name in dbg else "Internal"
            return nc.dram_tensor(name, list(shape), dt, kind=kind).ap()

        x_in = dram_in("x", [S, D])
        ln_emb_g = dram_in("ln_emb_g", [D]); ln_emb_b = dram_in("ln_emb_b", [D])
        w_in = dram_in("w_in", [DEPTH, D, PROJ_W]); b_f = dram_in("b_f", [DEPTH, 6])
        w_cmp1 = dram_in("w_cmp1", [DEPTH, 2, 2048, 256]); w_cmp2 = dram_in("w_cmp2", [DEPTH, 2, 256, 64])
        pe_cmp = dram_in("pe_cmp", [DEPTH, 2, 32, 64])
        w_dw = dram_in("w_dw", [DEPTH, 31, 256]); b_dw = dram_in("b_dw", [DEPTH, 256])
        ln_conv_g = dram_in("ln_conv_g", [DEPTH, 256]); ln_conv_b = dram_in("ln_conv_b", [DEPTH, 256])
        w_out = dram_in("w_out", [DEPTH, D, D])
        ln1_g = dram_in("ln1_g", [DEPTH, D]); ln1_b = dram_in("ln1_b", [DEPTH, D])
        w_ffn_in = dram_in("w_ffn_in", [DEPTH, D, 2 * DFF]); w_ffn_conv = dram_in("w_ffn_conv", [DEPTH, 3, DFF])
        b_ffn_conv = dram_in("b_ffn_conv", [DEPTH, DFF]); w_ffn_down = dram_in("w_ffn_down", [DEPTH, DFF, D])
        ln2_g = dram_in("ln2_g", [DEPTH, D]); ln2_b = dram_in("ln2_b", [DEPTH, D])
        y_out = nc.dram_tensor("y", [S, D], F32, kind="ExternalOutput").ap()

        XA = dram("XA", [S, D], F32); XB = dram("XB", [S, D], F32)
        QTN = dram("QTN", [384, S], BF16)
        KCR = dram("KCR", [128, S], BF16); VCR = dram("VCR", [128, S], BF16)
        KSL = dram("KSL", [128, S], BF16); KWN = dram("KWN", [128, S], BF16)
        VSL = dram("VSL", [S, 128], BF16); VWN = dram("VWN", [S, 128], BF16)
        GT = dram("GT", [18, S], F32); GTM = dram("GTM", [S, 18], F32)
        GLU = dram("GLU", [256, S], F32)
        FQT = dram("FQT", [384, S], BF16); FKT = dram("FKT", [384, S], BF16); FV = dram("FV", [S, 384], BF16)
        FLT = dram("FLT", [6, S], F32); CQ8 = dram("CQ8", [6, S], BF16); CFX = dram("CFX", [6, S], F32)
        NSACQ = dram("NSACQ", [6, S], BF16)
        MASKT = dram("MASKT", [6, 128, S], BF16)
        OCT = dram("OCT", [6, 64, S], F32)
        MIXT = dram("MIXT", [D, S], BF16)
        OST = dram("OST", [6, 64, S], F32)
        KCMP = dram("KCMP", [2, 64, 256], BF16); VCMP = dram("VCMP", [2, 256, 64], BF16)
        SCORE = dram("SCORE", [2, S, 128], F32); SELM = dram("SELM", [2, S, 128], F32)

        uniq = [0]

        def sb(name, shape, dt, st=ES):
            uniq[0] += 1
            return st.enter_context(nc.sbuf_tensor(f"{name}_{uniq[0]}", list(shape), dt))

        def ps(name, shape, dt, st=ES):
            uniq[0] += 1
            esz = 4 if dt == F32 else 2
            t = st.enter_context(nc.psum_tensor(f"{name}_{uniq[0]}", [128, 2048 // esz], dt))
            T.excl.add(name)
            free = 1
            for d_ in shape[1:]:
                free *= d_
            assert free * esz <= 2048, (name, shape)
            v = t[0:shape[0], 0:free]
            if len(shape) == 3:
                v = v.rearrange("p (a b) -> p a b", a=shape[1])
            return v

        ident_b = sb("ident_b", [128, 128], BF16)
        ident_f = sb("ident_f", [128, 128], F32)
        ones_b = sb("ones_b", [128, 128], BF16)
        ones_f = sb("ones_f", [128, 128], F32)
        tpos = sb("tpos", [128, 64], F32)
        nsakb = sb("nsakb", [128, 6, 64], F32)
        nsaqb = sb("nsaqb", [128, 6, 64], F32)
        T.op("pool", lambda: G.memset(ones_b[:], 1.0), writes=["ones_b"])
        T.op("pool", lambda: G.memset(ones_f[:], 1.0), writes=["ones_f"])
        T.op("pool", lambda: G.affine_select(out=ident_b[:], in_=ones_b[:], pattern=[[1, 128]], compare_op=ALU.is_equal,
                                             fill=0.0, base=0, channel_multiplier=-1), reads=["ones_b"], writes=["ident_b"])
        T.op("pool", lambda: G.affine_select(out=ident_f[:], in_=ones_f[:], pattern=[[1, 128]], compare_op=ALU.is_equal,
                                             fill=0.0, base=0, channel_multiplier=-1), reads=["ones_f"], writes=["ident_f"])
        T.op("pool", lambda: G.iota(tpos[:], pattern=[[128, 64]], base=0, channel_multiplier=1,
                                    allow_small_or_imprecise_dtypes=True), writes=["tpos"])
        for h in range(6):
            T.op("dve", lambda h=h: V.tensor_scalar(out=nsakb[:, h, :], in0=tpos[:], scalar1=SLOPES[h], scalar2=None,
                                                    op0=ALU.mult), reads=["tpos"], writes=["nsakb"])
            T.op("dve", lambda h=h: V.tensor_scalar(out=nsaqb[:, h, :], in0=tpos[:], scalar1=-8.0 * SLOPES[h], scalar2=-NEGM,
                                                    op0=ALU.mult, op1=ALU.add), reads=["tpos"], writes=["nsaqb"])
        with contextlib.ExitStack() as st:
            trow = sb("trow", [1, S], F32, st)
            crow = sb("crow", [1, S], BF16, st)
            T.op("pool", lambda: G.iota(trow[:], pattern=[[1, S]], base=0, channel_multiplier=0,
                                        allow_small_or_imprecise_dtypes=True), writes=["trow"])
            for h in range(6):
                T.op("dve", lambda h=h: V.tensor_scalar(out=crow[:], in0=trow[:], scalar1=-8.0 * SLOPES[h], scalar2=None,
                                                        op0=ALU.mult), reads=["trow"], writes=["crow"])
                T.dma("sp", NSACQ[h:h + 1, :], crow[:], reads=["crow"])
            T.barrier()

        cmptab = sb("cmptab", [128, 6, 512], F32)
        atab = sb("atab", [128, 256], F32)
        btab = sb("btab", [128, 256], F32)
        with contextlib.ExitStack() as st:
            dist = sb("dist", [128, 512], F32, st)
            dt_ = sb("dt_", [128, 256], F32, st)
            t1 = sb("t1", [128, 256], F32, st)
            T.op("pool", lambda: G.iota(dist[:], pattern=[[-32, 512]], base=8033, channel_multiplier=1,
                                        allow_small_or_imprecise_dtypes=True), writes=["dist"])
            for h in range(6):
                T.op("dve", lambda h=h: V.tensor_scalar(out=cmptab[:, h, :], in0=dist[:], scalar1=-SLOPES[h], scalar2=None,
                                                        op0=ALU.mult), reads=["dist"], writes=["cmptab"])
                T.op("pool", lambda h=h: G.affine_select(out=cmptab[:, h, :], in_=cmptab[:, h, :], pattern=[[-32, 512]],
                                                         compare_op=ALU.is_ge, fill=-30000.0, base=8033, channel_multiplier=1),
                     reads=["cmptab"], writes=["cmptab"])
            T.op("pool", lambda: G.iota(dt_[0:64, :], pattern=[[1, 256]], base=-126, channel_multiplier=0,
                                        allow_small_or_imprecise_dtypes=True), writes=["dt_"])
            T.op("pool", lambda: G.iota(dt_[64:128, :], pattern=[[1, 256]], base=-127, channel_multiplier=0,
                                        allow_small_or_imprecise_dtypes=True), writes=["dt_"])
            T.op("dve", lambda: V.tensor_scalar(out=atab[:], in0=dt_[:], scalar1=0.0, scalar2=None, op0=ALU.is_lt),
                 reads=["dt_"], writes=["atab"])
            T.op("dve", lambda: V.tensor_scalar(out=btab[:], in0=dt_[:], scalar1=0.0, scalar2=4.0, op0=ALU.is_equal, op1=ALU.mult),
                 reads=["dt_"], writes=["btab"])
            T.op("dve", lambda: V.tensor_scalar(out=t1[:], in0=dt_[:], scalar1=0.0, scalar2=-1.0, op0=ALU.is_gt, op1=ALU.mult),
                 reads=["dt_"], writes=["t1"])
            T.op("dve", lambda: V.tensor_tensor(out=btab[:], in0=btab[:], in1=t1[:], op=ALU.add),
                 reads=["btab", "t1"], writes=["btab"])
            T.barrier()

        def load_cast(st, name, dst, src_of_chunk, nchunks, width, stage):
            for c in range(nchunks):
                sg = stage[c % 2]
                T.dma("sp", sg[:, :width], src_of_chunk(c), writes=[f"{name}_stg{c % 2}"])
                eng = ("dve", "pool")[c % 2]
                E = V if eng == "dve" else G
                T.op(eng, lambda E=E, c=c, sg=sg: E.tensor_copy(out=dst(c), in_=sg[:, :width]),
                     reads=[f"{name}_stg{c % 2}"], writes=[name])

        def layer_norm_rows(st_tiles, ytile, ykey, gb, n=D):
            stats, mv, rstd = st_tiles
            nch = n // 512
            for c in range(nch):
                T.op("dve", lambda c=c: V.bn_stats(out=stats[:, c, :], in_=ytile[:, c * 512:(c + 1) * 512]),
                     reads=[ykey], writes=["ln_stats"])
            T.op("dve", lambda: V.bn_aggr(out=mv[:], in_=stats[:, 0:nch, :]), reads=["ln_stats"], writes=["ln_mv"])
            T.op("act", lambda: A.activation(out=rstd[:], in_=mv[:, 1:2], func=AF.Sqrt, bias=epsb[:, 0:1], scale=1.0),
                 reads=["ln_mv"], writes=["ln_rstd"])
            T.op("dve", lambda: V.reciprocal(out=rstd[:], in_=rstd[:]), reads=["ln_rstd"], writes=["ln_rstd"])
            T.op("dve", lambda: V.tensor_scalar(out=mv[:, 1:2], in0=mv[:, 0:1], scalar1=-1.0, scalar2=rstd[:, 0:1], op0=ALU.mult, op1=ALU.mult),
                 reads=["ln_mv", "ln_rstd"], writes=["ln_mv"])
            T.op("act", lambda: A.activation(out=ytile[:], in_=ytile[:], func=AF.Identity, bias=mv[:, 1:2], scale=rstd[:, 0:1]),
                 reads=[ykey, "ln_mv", "ln_rstd"], writes=[ykey])
            T.op("dve", lambda: V.tensor_tensor(out=ytile[:], in0=ytile[:], in1=gb[0][:], op=ALU.mult),
                 reads=[ykey, "ln_gb"], writes=[ykey])
            T.op("dve", lambda: V.tensor_tensor(out=ytile[:], in0=ytile[:], in1=gb[1][:], op=ALU.add),
                 reads=[ykey, "ln_gb"], writes=[ykey])


        def transpose_load(name, src_ap, n, C, dst_fn, dup=1):
            with contextlib.ExitStack() as st_:
                sg = sb(f"{name}_tl", [n, C * 128], F32, st_)
                pt = ps(f"{name}_tp", [128, C * n], F32, st_)
                T.dma("sp", sg[:], src_ap, writes=[f"{name}_tl"])
                T.group("pe", [lambda c=c: P.transpose(out=pt[:, c * n:(c + 1) * n], in_=sg[0:n, c * 128:(c + 1) * 128], identity=ident_f[0:n, 0:n])
                               for c in range(C)], reads=[f"{name}_tl", "ident_f"], writes=[f"{name}_tp"])
                for c in range(C):
                    T.op("dve", lambda c=c: V.tensor_copy(out=dst_fn(c), in_=pt[:, c * n:(c + 1) * n]), reads=[f"{name}_tp"], writes=[f"{name}_dst"])
                T.barrier()

        epsb = sb("epsb", [128, 1], F32)
        T.op("pool", lambda: G.memset(epsb[:], LN_EPS), writes=["epsb"])
        ln_stats = sb("ln_stats", [128, 2, 6], F32)
        ln_mv = sb("ln_mv", [128, 2], F32)
        ln_rstd = sb("ln_rstd", [128, 1], F32)
        LNT = (ln_stats, ln_mv, ln_rstd)
        T.barrier()

        def load_gb(gt, bt, g_ap, b_ap):
            T.dma("sp", gt[:], g_ap.partition_broadcast(128), writes=["ln_gb"])
            T.dma("sp", bt[:], b_ap.partition_broadcast(128), writes=["ln_gb"])

        def phase_A(l):
            with contextlib.ExitStack() as st:
                win = sb("win", [128, 8, PROJ_W], BF16, st)
                stg = [sb(f"wstg{i}", [128, PROJ_W], F32, st) for i in range(2)]
                load_cast(st, "win", lambda c: win[:, c, :], lambda c: w_in[l, c * 128:(c + 1) * 128, :], 8, PROJ_W, stg)
                gt = sb("gA", [128, D], F32, st); bt = sb("bA", [128, D], F32, st)
                if l == 0:
                    load_gb(gt, bt, ln_emb_g, ln_emb_b)
                xin = [sb(f"xinA{i}", [128, D], F32, st) for i in range(6)]
                xbf = [sb(f"xbfA{i}", [128, D], BF16, st) for i in range(2)]
                xT = [sb(f"xTA{i}", [128, 8, 512], BF16, st) for i in range(2)]
                peT = sb("peT", [128, 2, 512], F32, st)
                negbf = sb("negbf", [6, 1], F32, st)
                ob = [sb(f"obA{i}", [128, 512], BF16, st) for i in range(4)]
                of = [sb(f"ofA{i}", [128, 512], F32, st) for i in range(3)]
                sg = [sb(f"sgA{i}", [128, 512], F32, st) for i in range(2)]
                tmv = [sb(f"tmvA{i}", [128, 640], BF16, st) for i in range(2)]
                tmg = [sb(f"tmgA{i}", [128, 18], F32, st) for i in range(2)]
                ptr = [ps(f"ptrA{i}", [128, 1024], BF16, st) for i in range(2)]
                pfm = [ps(f"pfmA{i}", [128, 512], F32, st) for i in range(3)]
                ptm = [ps(f"ptmA{i}", [128, 512], F32, st) for i in range(2)]
                with contextlib.ExitStack() as st_:
                    pes = sb("pesA", [32, 2, 128], F32, st_)
                    ptp = ps("ptpA", [128, 2, 32], F32, st_)
                    for i in range(2):
                        for kvh in range(2):
                            T.dma("sp", pes[:, i, kvh * 64:(kvh + 1) * 64], pe_cmp[l, i], writes=["pesA"])
                    T.group("pe", [lambda i=i: P.transpose(out=ptp[:, i, :], in_=pes[0:32, i, :], identity=ident_f[0:32, 0:32]) for i in range(2)],
                            reads=["pesA", "ident_f"], writes=["ptpA"])
                    for i in range(2):
                        T.op("dve", lambda i=i: V.tensor_copy(out=peT[:, i, 0:32], in_=ptp[:, i, :]), reads=["ptpA"], writes=["peT"])
                    T.barrier()
                for i in range(2):
                    w_ = 32
                    while w_ < 512:
                        T.op("dve", lambda i=i, w_=w_: V.tensor_copy(out=peT[:, i, w_:2 * w_], in_=peT[:, i, 0:w_]),
                             reads=["peT"], writes=["peT"])
                        w_ *= 2
                T.dma("sp", negbf[:], b_f[l].rearrange("(h o) -> h o", o=1), writes=["negbf"], allow_slow_non_contiguous=True)
                T.op("dve", lambda: V.tensor_scalar(out=negbf[:], in0=negbf[:], scalar1=-1.0, scalar2=None, op0=ALU.mult),
                     reads=["negbf"], writes=["negbf"])
                src = x_in if l == 0 else XA
                cnt = {"ob": 0, "of": 0, "pfm": 0, "ev": 0}

                def evac(eng, out, in_, rk, wk, **kw):
                    if eng == "act":
                        return T.op("act", lambda: A.activation(out=out, in_=in_, func=kw.get("func", AF.Copy)), reads=rk, writes=wk)
                    return T.op("dve", lambda: V.tensor_copy(out=out, in_=in_), reads=rk, writes=wk)

                def prep(tt):
                    xt = xT[tt % 2]; xtk = f"xTA{tt % 2}"
                    for sub in range(4):
                        r0 = tt * 512 + sub * 128
                        n = tt * 4 + sub
                        T.dma("sp", xin[n % 6][:], src[r0:r0 + 128, :], writes=[f"xinA{n % 6}"])
                    for sub in range(4):
                        r0 = tt * 512 + sub * 128
                        n = tt * 4 + sub
                        xi = xin[n % 6]; xik = f"xinA{n % 6}"
                        xb = xbf[n % 2]; xbk = f"xbfA{n % 2}"
                        if l == 0:
                            layer_norm_rows(LNT, xi, xik, (gt, bt))
                            T.dma("pool", XA[r0:r0 + 128, :], xi[:], reads=[xik])
                        T.op("dve", lambda xb=xb, xi=xi: V.tensor_copy(out=xb[:], in_=xi[:]), reads=[xik], writes=[xbk])
                        pt = ptr[n % 2]; ptk = f"ptrA{n % 2}"
                        T.group("pe", [lambda k=k, pt=pt, xb=xb: P.transpose(out=pt[:, k * 128:(k + 1) * 128], in_=xb[:, k * 128:(k + 1) * 128],
                                                                              identity=ident_b[:]) for k in range(8)],
                                reads=[xbk, "ident_b"], writes=[ptk])
                        T.op("act", lambda pt=pt, xt=xt, sub=sub: A.activation(
                            out=xt[:, :, sub * 128:(sub + 1) * 128], in_=pt[:].rearrange("p (k t) -> p k t", k=8), func=AF.Copy),
                            reads=[ptk], writes=[xtk])
                ntt = NQT if DBG_NTT is None else DBG_NTT
                prep(0)
                for tt in range(ntt):
                    if tt + 1 < ntt:
                        prep(tt + 1)
                    xt = xT[tt % 2]; xtk = f"xTA{tt % 2}"
                    q0 = tt * 512
                    def fm(col, width):
                        i = cnt["pfm"] % 3; cnt["pfm"] += 1
                        pp = pfm[i]
                        T.group("pe", [lambda k=k: P.matmul(pp[0:width, :], lhsT=win[:, k, col:col + width], rhs=xt[:, k, :],
                                                            start=(k == 0), stop=(k == 7)) for k in range(8)],
                                reads=["win", xtk], writes=[f"pfmA{i}"])
                        return pp, f"pfmA{i}"

                    def out_bf(pp, pk, width, dst):
                        i = cnt["ob"] % 4; cnt["ob"] += 1
                        o = ob[i]
                        cnt["ev"] += 1
                        evac(("act", "dve")[cnt["ev"] % 2], o[0:width, :], pp[0:width, :], [pk], [f"obA{i}"])
                        T.dma("sp", dst, o[0:width, :], reads=[f"obA{i}"])

                    for c in range(3):
                        pp, pk = fm(C_NQ + c * 128, 128)
                        out_bf(pp, pk, 128, QTN[c * 128:(c + 1) * 128, q0:q0 + 512])
                    for i2, (col, dst) in enumerate(((C_KCR, KCR), (C_VCR, VCR))):
                        pp, pk = fm(col, 128)
                        i = cnt["ob"] % 4; cnt["ob"] += 1
                        o = ob[i]
                        T.op("dve", lambda o=o, pp=pp, i2=i2: V.tensor_tensor(out=o[:], in0=pp[:], in1=peT[:, i2, :], op=ALU.add),
                             reads=[pk, "peT"], writes=[f"obA{i}"])
                        T.dma("sp", dst[:, q0:q0 + 512], o[:], reads=[f"obA{i}"])
                    for col, dst in ((C_KSL, KSL), (C_KWN, KWN)):
                        pp, pk = fm(col, 128)
                        out_bf(pp, pk, 128, dst[:, q0:q0 + 512])
                    pp, pk = fm(C_GATE, 18)
                    i = cnt["of"] % 3; cnt["of"] += 1
                    T.op("act", lambda pp=pp, i=i: A.activation(out=of[i][0:18, :], in_=pp[0:18, :], func=AF.Sigmoid),
                         reads=[pk], writes=[f"ofA{i}"])
                    T.dma("sp", GT[:, q0:q0 + 512], of[i][0:18, :], reads=[f"ofA{i}"])
                    for c in range(2):
                        pg, pgk = fm(C_CG + c * 128, 128)
                        s_ = sg[c % 2]
                        T.op("act", lambda pg=pg, s_=s_: A.activation(out=s_[:], in_=pg[:], func=AF.Sigmoid),
                             reads=[pgk], writes=[f"sgA{c % 2}"])
                        pa, pak = fm(C_CA + c * 128, 128)
                        i = cnt["of"] % 3; cnt["of"] += 1
                        T.op("dve", lambda pa=pa, s_=s_, i=i: V.tensor_tensor(out=of[i][:], in0=pa[:], in1=s_[:], op=ALU.mult),
                             reads=[pak, f"sgA{c % 2}"], writes=[f"ofA{i}"])
                        T.dma("sp", GLU[c * 128:(c + 1) * 128, q0:q0 + 512], of[i][:], reads=[f"ofA{i}"])
                    for c in range(3):
                        pp, pk = fm(C_FQ + c * 128, 128)
                        out_bf(pp, pk, 128, FQT[c * 128:(c + 1) * 128, q0:q0 + 512])
                    for c in range(3):
                        pp, pk = fm(C_FK + c * 128, 128)
                        out_bf(pp, pk, 128, FKT[c * 128:(c + 1) * 128, q0:q0 + 512])
                    pp, pk = fm(C_FF, 6)
                    i = cnt["of"] % 3; cnt["of"] += 1
                    T.op("act", lambda pp=pp, i=i: A.activation(out=of[i][0:6, :], in_=pp[0:6, :], func=AF.Exp, bias=negbf[:, 0:1], scale=-1.0),
                         reads=[pk, "negbf"], writes=[f"ofA{i}"])
                    T.op("act", lambda i=i: A.activation(out=of[i][0:6, :], in_=of[i][0:6, :], func=AF.Ln, bias=1.0, scale=1.0),
                         reads=[f"ofA{i}"], writes=[f"ofA{i}"])
                    T.op("dve", lambda i=i: V.tensor_scalar(out=of[i][0:6, :], in0=of[i][0:6, :], scalar1=-1.0, scalar2=None, op0=ALU.mult),
                         reads=[f"ofA{i}"], writes=[f"ofA{i}"])
                    T.dma("sp", FLT[:, q0:q0 + 512], of[i][0:6, :], reads=[f"ofA{i}"])
                    for sub in range(4):
                        r0 = q0 + sub * 128
                        n = tt * 4 + sub
                        pa_ = ptm[0]; pb_ = ptm[1]
                        lt = lambda k: xt[:, k, sub * 128:(sub + 1) * 128]
                        T.group("pe",
                                [lambda k=k: P.matmul(pa_[:, 0:128], lhsT=lt(k), rhs=win[:, k, C_VSL:C_VSL + 128], start=(k == 0), stop=(k == 7)) for k in range(8)] +
                                [lambda k=k: P.matmul(pa_[:, 128:256], lhsT=lt(k), rhs=win[:, k, C_VWN:C_VWN + 128], start=(k == 0), stop=(k == 7)) for k in range(8)] +
                                [lambda k=k: P.matmul(pa_[:, 256:274], lhsT=lt(k), rhs=win[:, k, C_GATE:C_GATE + 18], start=(k == 0), stop=(k == 7)) for k in range(8)],
                                reads=["win", xtk], writes=["ptmA0"])
                        T.group("pe",
                                [lambda k=k: P.matmul(pb_[:, 0:384], lhsT=lt(k), rhs=win[:, k, C_FV:C_FV + 384], start=(k == 0), stop=(k == 7)) for k in range(8)],
                                reads=["win", xtk], writes=["ptmA1"])
                        tv = tmv[n % 2]; tg = tmg[n % 2]
                        T.op("dve", lambda tv=tv: V.tensor_copy(out=tv[:, 0:256], in_=pa_[:, 0:256]), reads=["ptmA0"], writes=[f"tmvA{n % 2}"])
                        T.op("act", lambda tg=tg: A.activation(out=tg[:], in_=pa_[:, 256:274], func=AF.Sigmoid), reads=["ptmA0"], writes=[f"tmgA{n % 2}"])
                        T.op("act", lambda tv=tv: A.activation(out=tv[:, 256:640], in_=pb_[:, 0:384], func=AF.Copy), reads=["ptmA1"], writes=[f"tmvA{n % 2}"])
                        T.dma("sp", VSL[r0:r0 + 128, :], tv[:, 0:128], reads=[f"tmvA{n % 2}"])
                        T.dma("sp", VWN[r0:r0 + 128, :], tv[:, 128:256], reads=[f"tmvA{n % 2}"])
                        T.dma("sp", FV[r0:r0 + 128, :], tv[:, 256:640], reads=[f"tmvA{n % 2}"])
                        T.dma("sp", GTM[r0:r0 + 128, :], tg[:], reads=[f"tmgA{n % 2}"])
                T.barrier()

        negc = sb("negc", [128, 64, 6], F32)

        def phase_B(l):
            with contextlib.ExitStack() as st:
                lf = sb("lfB", [6, S], F32, st)
                on6 = sb("on6B", [6, S], F32, st)
                cs = sb("csB", [6, S], F32, st)
                cq = sb("cqB", [6, S], BF16, st)
                pst = ps("pstB", [128, 384], F32, st)
                T.dma("sp", lf[:], FLT[:, :], writes=["lfB"])
                T.op("pool", lambda: G.memset(on6[:], 1.0), writes=["on6B"])
                T.op("dve", lambda: V.tensor_tensor_scan(out=cs[:], data0=on6[:], data1=lf[:], initial=0.0, op0=ALU.mult, op1=ALU.add),
                     reads=["lfB", "on6B"], writes=["csB"])
                T.op("dve", lambda: V.tensor_scalar(out=cq[:], in0=cs[:], scalar1=8.0, scalar2=None, op0=ALU.mult),
                     reads=["csB"], writes=["cqB"])
                T.dma("sp", CQ8[:, :], cq[:], reads=["cqB"])
                if "CFX" in dbg:
                    T.dma("sp", CFX[:, :], cs[:], reads=["csB"])
                T.group("pe", [lambda kt=kt: P.transpose(out=pst[:, kt * 6:(kt + 1) * 6], in_=cs[0:6, kt * 128:(kt + 1) * 128],
                                                         identity=ident_f[0:6, 0:6]) for kt in range(64)],
                        reads=["csB", "ident_f"], writes=["pstB"])
                T.op("dve", lambda: V.tensor_scalar(out=negc[:].rearrange("p k h -> p (k h)"), in0=pst[:], scalar1=-1.0, scalar2=None, op0=ALU.mult),
                     reads=["pstB"], writes=["negc"])
                T.barrier()

        def phase_C(l):
            with contextlib.ExitStack() as st:
                w1 = [sb(f"w1C{i}", [128, 32, 256], BF16, st) for i in range(2)]
                w2 = [sb(f"w2C{i}", [128, 2, 64], BF16, st) for i in range(2)]
                kcmp = [sb(f"kcmpC{k}", [64, 256], BF16, st) for k in range(2)]
                vcmp = [sb(f"vcmpC{k}", [128, 2, 64], BF16, st) for k in range(2)]
                with contextlib.ExitStack() as st2:
                    stg = [sb(f"stgC{i}", [128, 8, 256], F32, st2) for i in range(2)]
                    stg2 = sb("stg2C", [128, 2, 64], F32, st2)
                    src = [sb(f"srcC{i}", [128, S], BF16, st2) for i in range(2)]
                    hT = [[sb(f"hTC{i}{k}", [128, 2, 256], BF16, st2) for k in range(2)] for i in range(2)]
                    tq = [sb(f"tqC{i}", [128, 256], F32, st2) for i in range(2)]
                    ph = [ps(f"phC{i}", [128, 256], F32, st2) for i in range(2)]
                    pk = ps("pkC", [128, 256], F32, st2)
                    n = 0
                    for i in range(2):
                        for t8 in range(4):
                            sg = stg[n % 2]; sk = f"stgC{n % 2}"; n += 1
                            srcap = w_cmp1[l, i, t8 * 512:(t8 + 1) * 512, :].rearrange("(t d) n -> d t n", d=64)
                            T.dma("sp", sg[0:64], srcap, writes=[sk])
                            T.dma("sp", sg[64:128], srcap, writes=[sk])
                            T.op("pool", lambda sg=sg, i=i, t8=t8: G.tensor_copy(out=w1[i][:, t8 * 8:(t8 + 1) * 8, :], in_=sg[:]),
                                 reads=[sk], writes=[f"w1C{i}"])
                        T.dma("sp", stg2[:], w_cmp2[l, i].rearrange("(c p) n -> p c n", p=128), writes=["stg2C"])
                        T.op("dve", lambda i=i: V.tensor_copy(out=w2[i][:], in_=stg2[:]), reads=["stg2C"], writes=[f"w2C{i}"])
                        T.dma("sp", src[i][:], (KCR, VCR)[i][:, :], writes=[f"srcC{i}"])
                    g = 0
                    for i in range(2):
                        for kvh in range(2):
                            sview = src[i][kvh * 64:(kvh + 1) * 64, :].rearrange("p (n t) -> p t n", t=32)
                            for hc in range(2):
                                pp = ph[g % 2]; ppk = f"phC{g % 2}"; tt_ = tq[g % 2]; tk = f"tqC{g % 2}"; g += 1
                                T.group("pe", [lambda t=t, pp=pp, sview=sview: P.matmul(
                                    pp[:], lhsT=w1[i][kvh * 64:(kvh + 1) * 64, t, hc * 128:(hc + 1) * 128], rhs=sview[:, t, :],
                                    start=(t == 0), stop=(t == 31)) for t in range(32)],
                                    reads=[f"w1C{i}", f"srcC{i}"], writes=[ppk])
                                T.op("act", lambda pp=pp, tt_=tt_: A.activation(out=tt_[:], in_=pp[:], func=AF.Square), reads=[ppk], writes=[tk])
                                T.op("dve", lambda tt_=tt_: V.tensor_scalar(out=tt_[:], in0=tt_[:], scalar1=0.044715, scalar2=1.0, op0=ALU.mult, op1=ALU.add),
                                     reads=[tk], writes=[tk])
                                T.op("dve", lambda pp=pp, tt_=tt_: V.tensor_tensor(out=tt_[:], in0=tt_[:], in1=pp[:], op=ALU.mult), reads=[tk, ppk], writes=[tk])
                                T.op("act", lambda tt_=tt_: A.activation(out=tt_[:], in_=tt_[:], func=AF.Sigmoid, scale=1.5957691216), reads=[tk], writes=[tk])
                                T.op("dve", lambda pp=pp, tt_=tt_, i=i, kvh=kvh, hc=hc: V.tensor_tensor(out=hT[i][kvh][:, hc, :], in0=tt_[:], in1=pp[:], op=ALU.mult),
                                     reads=[tk, ppk], writes=[f"hTC{i}{kvh}"])
                    for kvh in range(2):
                        T.group("pe", [lambda hc=hc: P.matmul(pk[0:64, :], lhsT=w2[0][:, hc, :], rhs=hT[0][kvh][:, hc, :], start=(hc == 0), stop=(hc == 1)) for hc in range(2)],
                                reads=["w2C0", f"hTC0{kvh}"], writes=["pkC"])
                        T.op("act", lambda kvh=kvh: A.activation(out=kcmp[kvh][:], in_=pk[0:64, :], func=AF.Copy), reads=["pkC"], writes=[f"kcmpC{kvh}"])
                        for c in range(2):
                            T.group("pe", [lambda hc=hc, c=c: P.matmul(pk[:, 0:64], lhsT=hT[1][kvh][:, hc, c * 128:(c + 1) * 128], rhs=w2[1][:, hc, :],
                                                                   start=(hc == 0), stop=(hc == 1)) for hc in range(2)],
                                    reads=["w2C1", f"hTC1{kvh}"], writes=["pkC"])
                            T.op("act", lambda kvh=kvh, c=c: A.activation(out=vcmp[kvh][:, c, :], in_=pk[:, 0:64], func=AF.Copy), reads=["pkC"], writes=[f"vcmpC{kvh}"])
                    if "KCMP" in dbg:
                        for kvh in range(2):
                            T.dma("sp", KCMP[kvh], kcmp[kvh][:], reads=[f"kcmpC{kvh}"])
                            T.dma("sp", VCMP[kvh].rearrange("(c p) d -> p c d", p=128), vcmp[kvh][:], reads=[f"vcmpC{kvh}"])
                    T.barrier()
                qc = [sb(f"qcC{i}", [64, 3, 128], BF16, st) for i in range(4)]
                gtm = [sb(f"gtmC{i}", [128, 18], F32, st) for i in range(2)]
                zt = [sb(f"zC{i}", [128, 256], F32, st) for i in range(6)]
                et = [sb(f"eC{i}", [128, 256], F32, st) for i in range(6)]
                pb = [sb(f"pbC{i}", [128, 256], BF16, st) for i in range(6)]
                pT = [sb(f"pTC{i}", [128, 2, 128], BF16, st) for i in range(6)]
                sm = [sb(f"smC{i}", [128, 4], F32, st) for i in range(6)]
                imp = [sb(f"impC{i}", [128, 256], F32, st) for i in range(2)]
                sc = [sb(f"scC{i}", [128, 128], F32, st) for i in range(2)]
                sc2 = [sb(f"sc2C{i}", [128, 128], F32, st) for i in range(2)]
                m8 = [sb(f"m8C{i}", [128, 16], F32, st) for i in range(2)]
                msk = [sb(f"mskC{i}", [128, 128], F32, st) for i in range(2)]
                mv = [sb(f"mvC{i}", [128, 128], F32, st) for i in range(6)]
                ocs = [[sb(f"ocsC{h}_{i}", [64, 512], F32, st) for i in range(2)] for h in range(6)]
                mts = [[sb(f"mtsC{h}_{i}", [128, 512], BF16, st) for i in range(2)] for h in range(6)]
                psS = [ps(f"psSC{i}", [128, 256], F32, st) for i in range(3)]
                psT = [ps(f"psTC{i}", [128, 2, 128], BF16, st) for i in range(2)]
                psO = [ps(f"psOC{i}", [64, 128], F32, st) for i in range(2)]
                psM = [ps(f"psMC{i}", [128, 128], F32, st) for i in range(1)]
                nq = S // 128

                def load_qi(qi):
                    q0 = qi * 128
                    T.dma("sp", gtm[qi % 2][:], GTM[q0:q0 + 128, :], writes=[f"gtmC{qi % 2}"])
                    for kvh in range(2):
                        qn = qi * 2 + kvh
                        T.dma("sp", qc[qn % 4][:], QTN[kvh * 192:(kvh + 1) * 192, q0:q0 + 128].rearrange("(g p) t -> p g t", p=64),
                              writes=[f"qcC{qn % 4}"])

                load_qi(0)
                cnt2 = [0]
                for qi in range(nq):
                    if qi + 1 < nq:
                        load_qi(qi + 1)
                    q0 = qi * 128
                    gm = gtm[qi % 2]; gmk = f"gtmC{qi % 2}"
                    blk = qi // 4; sub = qi % 4; par = blk % 2
                    for kvh in range(2):
                        qn = qi * 2 + kvh
                        qq = qc[qn % 4]; qqk = f"qcC{qn % 4}"
                        im = imp[kvh]; imk = f"impC{kvh}"
                        HS = [(g, kvh * 3 + g) for g in range(3)]
                        for g, h in HS:
                            T.op("pe", lambda g=g: P.matmul(psS[g][:], lhsT=qq[:, g, :], rhs=kcmp[kvh][:], start=True, stop=True),
                                 reads=[qqk, f"kcmpC{kvh}"], writes=[f"psSC{g}"])
                        for g, h in HS:
                            T.op("dve", lambda g=g, h=h: V.scalar_tensor_tensor(out=zt[h][:], in0=psS[g][:], scalar=0.125, in1=cmptab[:, h, 252 - 4 * qi:508 - 4 * qi],
                                                                               op0=ALU.mult, op1=ALU.add), reads=[f"psSC{g}", "cmptab"], writes=[f"zC{h}"])
                        for g, h in HS:
                            T.op("dve", lambda h=h: V.tensor_reduce(out=sm[h][:, 0:1], in_=zt[h][:], axis=AX.X, op=ALU.max), reads=[f"zC{h}"], writes=[f"smC{h}"])
                        for g, h in HS:
                            T.op("dve", lambda h=h: V.tensor_scalar(out=sm[h][:, 0:1], in0=sm[h][:, 0:1], scalar1=-1000.0, scalar2=-1.0, op0=ALU.max, op1=ALU.mult),
                                 reads=[f"smC{h}"], writes=[f"smC{h}"])
                        for g, h in HS:
                            T.op("act", lambda h=h: A.activation(out=et[h][:], in_=zt[h][:], func=AF.Exp, bias=sm[h][:, 0:1], scale=1.0, accum_out=sm[h][:, 1:2]),
                                 reads=[f"zC{h}", f"smC{h}"], writes=[f"eC{h}", f"smC{h}"])
                        for g, h in HS:
                            T.op("dve", lambda h=h: V.tensor_scalar(out=sm[h][:, 1:2], in0=sm[h][:, 1:2], scalar1=1e-30, scalar2=None, op0=ALU.max),
                                 reads=[f"smC{h}"], writes=[f"smC{h}"])
                        for g, h in HS:
                            T.op("dve", lambda h=h: V.reciprocal(out=sm[h][:, 2:3], in_=sm[h][:, 1:2]), reads=[f"smC{h}"], writes=[f"smC{h}"])
                        for g, h in HS:
                            T.op("dve", lambda h=h: V.tensor_tensor(out=sm[h][:, 3:4], in0=sm[h][:, 2:3], in1=gm[:, h * 3:h * 3 + 1], op=ALU.mult),
                                 reads=[f"smC{h}", gmk], writes=[f"smC{h}"])
                        for g, h in HS:
                            T.op("dve", lambda h=h: V.tensor_scalar(out=pb[h][:], in0=et[h][:], scalar1=sm[h][:, 3:4], scalar2=None, op0=ALU.mult),
                                 reads=[f"eC{h}", f"smC{h}"], writes=[f"pbC{h}"])
                        for g, h in HS:
                            if g == 0:
                                T.op("dve", lambda h=h: V.tensor_scalar(out=im[:], in0=et[h][:], scalar1=sm[h][:, 2:3], scalar2=None, op0=ALU.mult),
                                     reads=[f"eC{h}", f"smC{h}"], writes=[imk])
                            else:
                                T.op("dve", lambda h=h: V.scalar_tensor_tensor(out=im[:], in0=et[h][:], scalar=sm[h][:, 2:3], in1=im[:], op0=ALU.mult, op1=ALU.add),
                                     reads=[f"eC{h}", f"smC{h}", imk], writes=[imk])
                        for g, h in HS:
                            b = cnt2[0] % 2; cnt2[0] += 1
                            T.group("pe", [lambda c=c, b=b, h=h: P.transpose(out=psT[b][:, c, :], in_=pb[h][:, c * 128:(c + 1) * 128], identity=ident_b[:]) for c in range(2)],
                                    reads=[f"pbC{h}", "ident_b"], writes=[f"psTC{b}"])
                            T.op("act", lambda b=b, h=h: A.activation(out=pT[h][:], in_=psT[b][:], func=AF.Copy), reads=[f"psTC{b}"], writes=[f"pTC{h}"])
                        for g, h in HS:
                            b = cnt2[0] % 2; cnt2[0] += 1
                            T.group("pe", [lambda c=c, b=b, h=h: P.matmul(psO[b][:], lhsT=vcmp[kvh][:, c, :], rhs=pT[h][:, c, :], start=(c == 0), stop=(c == 1)) for c in range(2)],
                                    reads=[f"vcmpC{kvh}", f"pTC{h}"], writes=[f"psOC{b}"])
                            T.op("act", lambda b=b, h=h: A.activation(out=ocs[h][par][:, sub * 128:(sub + 1) * 128], in_=psO[b][:], func=AF.Copy),
                                 reads=[f"psOC{b}"], writes=[f"ocsC{h}_{par}"])
                            if sub == 3:
                                T.dma("pool", OCT[h][:, blk * 512:(blk + 1) * 512], ocs[h][par][:], reads=[f"ocsC{h}_{par}"])
                    chains = []
                    for kvh in range(2):
                        im = imp[kvh]; imk = f"impC{kvh}"
                        s_ = sc[kvh]; s_k = f"scC{kvh}"; s2 = sc2[kvh]; s2k = f"sc2C{kvh}"; m_ = m8[kvh]; mk = f"m8C{kvh}"; ms = msk[kvh]; msk_k = f"mskC{kvh}"
                        imv = im[:].rearrange("p (n r) -> p n r", r=2)
                        ch = [
                            ("dve", lambda s_=s_, imv=imv: V.tensor_tensor(out=s_[:], in0=imv[:, :, 0], in1=imv[:, :, 1], op=ALU.add), [imk], [s_k]),
                            ("dve", lambda s_=s_: V.tensor_tensor(out=s_[:], in0=s_[:], in1=atab[:, 126 - 2 * qi:254 - 2 * qi], op=ALU.mult), [s_k, "atab"], [s_k]),
                            ("dve", lambda s_=s_: V.tensor_tensor(out=s_[:], in0=s_[:], in1=btab[:, 126 - 2 * qi:254 - 2 * qi], op=ALU.add), [s_k, "btab"], [s_k]),
                            ("dve", lambda s_=s_: V.memset(s_[:, 0:1], 4.0), [s_k], [s_k]),
                            ("dve", lambda s_=s_, m_=m_: V.max(out=m_[:, 0:8], in_=s_[:]), [s_k], [mk]),
                            ("dve", lambda s_=s_, m_=m_, s2=s2: V.match_replace(out=s2[:], in_to_replace=m_[:, 0:8], in_values=s_[:], imm_value=-1e9), [s_k, mk], [s2k]),
                            ("dve", lambda m_=m_, s2=s2: V.max(out=m_[:, 8:16], in_=s2[:]), [s2k], [mk]),
                            ("dve", lambda s_=s_, m_=m_, ms=ms: V.tensor_scalar(out=ms[:], in0=s_[:], scalar1=m_[:, 15:16], scalar2=None, op0=ALU.is_ge), [s_k, mk], [msk_k]),
                        ]
                        for g in range(3):
                            h = kvh * 3 + g
                            ch.append(("dve", lambda ms=ms, h=h: V.tensor_scalar(out=mv[h][:], in0=ms[:], scalar1=NEGM, scalar2=nsaqb[:, h, qi:qi + 1], op0=ALU.mult, op1=ALU.add),
                                       [msk_k, "nsaqb"], [f"mvC{h}"]))
                        chains.append(ch)
                    for opa, opb in zip(chains[0], chains[1]):
                        for (e_, fn_, r_, w_) in (opa, opb):
                            T.op(e_, fn_, reads=r_, writes=w_)
                    if "SCORE" in dbg:
                        for kvh in range(2):
                            T.dma("sp", SCORE[kvh][q0:q0 + 128, :], sc[kvh][:], reads=[f"scC{kvh}"])
                            T.dma("sp", SELM[kvh][q0:q0 + 128, :], msk[kvh][:], reads=[f"mskC{kvh}"])
                    for h in range(6):
                        T.op("pe", lambda h=h: P.transpose(out=psM[0][:], in_=mv[h][:], identity=ident_f[:]), reads=[f"mvC{h}", "ident_f"], writes=["psMC0"])
                        T.op("act", lambda h=h: A.activation(out=mts[h][par][:, sub * 128:(sub + 1) * 128], in_=psM[0][:], func=AF.Copy),
                             reads=["psMC0"], writes=[f"mtsC{h}_{par}"])
                        if sub == 3:
                            T.dma("pool", MASKT[h][:, blk * 512:(blk + 1) * 512], mts[h][par][:], reads=[f"mtsC{h}_{par}"])
                T.barrier()

        def attn_pass(name, KR, heads, hkey, load_head, load_q, qgroups, bias_ap, ktasks, gate_row, prev, out_ap, out_bf16):
            with contextlib.ExitStack() as st:
                kaug = [sb(f"{name}K{i}", [KR, S], BF16, st) for i in range(2)]
                vaug = [sb(f"{name}V{i}", [128, NKT, 65], BF16, st) for i in range(2)]
                qaug = [sb(f"{name}Q{i}", [KR, 512], BF16, st) for i in range(3)]
                pbuf = [sb(f"{name}P{i}", [128, 512], BF16, st) for i in range(4)]
                rden = [sb(f"{name}rd{i}", [65, 512], F32, st) for i in range(2)]
                grow = [sb(f"{name}gr{i}", [65, 512], F32, st) for i in range(2)]
                rb = [sb(f"{name}rb{i}", [65, 512], BF16, st) for i in range(2)]
                bcs = [sb(f"{name}bc{i}", [64, 512], F32, st) for i in range(2)]
                prv = [sb(f"{name}pv{i}", [64, 512], F32, st) for i in range(2)]
                ot = [sb(f"{name}ot{i}", [64, 512], F32, st) for i in range(2)]
                otb = [sb(f"{name}ob{i}", [64, 512], BF16, st) for i in range(2)]
                sps = [ps(f"{name}s{i}", [128, 512], F32, st) for i in range(4)]
                acc = [ps(f"{name}a{i}", [128, 512], F32, st) for i in range(3)]
                bcp = ps(f"{name}bcp", [64, 512], F32, st)
                for i in range(2):
                    T.op("pool", lambda i=i: G.memset(vaug[i][:, :, 64:65], 1.0), writes=[f"{name}V{i}"])
                    if KR == 65:
                        T.op("pool", lambda i=i: G.memset(kaug[i][64:65, :], 1.0), writes=[f"{name}K{i}"])
                    else:
                        T.op("pool", lambda i=i: G.memset(kaug[i][64:128, :], 1.0), writes=[f"{name}K{i}"])
                        T.op("pool", lambda i=i: G.affine_select(out=kaug[i][64:128, :], in_=kaug[i][64:128, :], pattern=[[0, 2], [1, 64], [0, 64]],
                                                                 compare_op=ALU.is_equal, fill=0.0, base=0, channel_multiplier=-1),
                             reads=[f"{name}K{i}"], writes=[f"{name}K{i}"])
                units = []
                for h in heads:
                    for qt in range(NQT):
                        tl = ktasks(qt)
                        for g in range(qgroups):
                            sel = [t for t in tl if (qgroups == 1 or (t[0] // 32) == g)]
                            if sel:
                                units.append((h, qt, g, sel))
                flat = []
                for ui, (h, qt, g, sel) in enumerate(units):
                    first_of_hq = (ui == 0 or units[ui - 1][0] != h or units[ui - 1][1] != qt)
                    last_of_hq = (ui == len(units) - 1 or units[ui + 1][0] != h or units[ui + 1][1] != qt)
                    for j, t in enumerate(sel):
                        flat.append((ui, h, qt, t, first_of_hq and j == 0, last_of_hq and j == len(sel) - 1))
                state = {"hk": None, "hslot": -1, "uload": -1, "hq": -1}
                hk_list = []
                for h in heads:
                    if not hk_list or hk_list[-1] != hkey(h):
                        hk_list.append(hkey(h))

                def ensure_head(hk):
                    if state["hk"] == hk:
                        return
                    state["hk"] = hk
                    state["hslot"] += 1
                    sl = state["hslot"] % 2
                    load_head(hk, kaug[sl], f"{name}K{sl}", vaug[sl], f"{name}V{sl}")

                def ensure_unit(ui):
                    while state["uload"] < ui:
                        state["uload"] += 1
                        u = state["uload"]
                        h, qt, g, _ = units[u]
                        load_q(h, qt, g, qaug[u % 3], f"{name}Q{u % 3}")

                def emit_S(i):
                    ui, h, qt, (kt, c0, c1, masks), first, last = flat[i]
                    ensure_head(hkey(h))
                    ensure_unit(min(ui + 1, len(units) - 1) if False else ui)
                    sl = state["hslot"] % 2
                    b = i % 4
                    q = qaug[ui % 3]
                    T.op("pe", lambda: P.matmul(sps[b][:, c0:c1], lhsT=kaug[sl][0:KR, kt * 128:(kt + 1) * 128], rhs=q[0:KR, c0:c1], start=True, stop=True),
                         reads=[f"{name}K{sl}", f"{name}Q{ui % 3}"], writes=[f"{name}s{b}"])
                    T.op("act", lambda: A.activation(out=pbuf[b][:, c0:c1], in_=sps[b][:, c0:c1], func=AF.Exp, bias=bias_ap(h, kt), scale=0.125),
                         reads=[f"{name}s{b}"], writes=[f"{name}P{b}"])
                    for (mc, kind) in masks:
                        if kind == "diag":
                            T.op("pool", lambda mc=mc: G.affine_select(out=pbuf[b][:, mc:mc + 128], in_=pbuf[b][:, mc:mc + 128], pattern=[[1, 128]],
                                                                       compare_op=ALU.is_ge, fill=0.0, base=0, channel_multiplier=-1),
                                 reads=[f"{name}P{b}"], writes=[f"{name}P{b}"])
                        else:
                            T.op("pool", lambda mc=mc: G.affine_select(out=pbuf[b][:, mc:mc + 128], in_=pbuf[b][:, mc:mc + 128], pattern=[[-1, 128]],
                                                                       compare_op=ALU.is_ge, fill=0.0, base=-1, channel_multiplier=1),
                                 reads=[f"{name}P{b}"], writes=[f"{name}P{b}"])
                    return sl

                slot_of = {}

                def emit_PV(i):
                    ui, h, qt, (kt, c0, c1, masks), first, last = flat[i]
                    if first:
                        state["hq"] += 1
                    hq = state["hq"]
                    ab = hq % 3
                    b = i % 4
                    sl = slot_of[i]
                    T.op("pe", lambda: P.matmul(acc[ab][0:65, c0:c1], lhsT=vaug[sl][:, kt, :], rhs=pbuf[b][:, c0:c1], start=first, stop=last),
                         reads=[f"{name}V{sl}", f"{name}P{b}"], writes=[f"{name}a{ab}"])
                    if last:
                        if pending:
                            finalize(*pending.pop(0))
                        pending.append((h, qt, hq))

                pending = []

                def finalize(h, qt, hq):
                    a3 = hq % 3
                    ab = hq % 2
                    q0 = qt * 512
                    ak = f"{name}a{a3}"
                    r_ = rden[ab]; g_ = grow[ab]; rb_ = rb[ab]; bc_ = bcs[ab]; pv_ = prv[ab]; o_ = ot[ab]; ob_ = otb[ab]
                    if gate_row is not None:
                        T.dma("sp", g_[64:65, :], GT[gate_row(h):gate_row(h) + 1, q0:q0 + 512], writes=[f"{name}gr{ab}"])
                        T.op("dve", lambda: V.reciprocal(out=r_[64:65, :], in_=acc[a3][64:65, :]), reads=[ak], writes=[f"{name}rd{ab}"])
                        T.op("dve", lambda: V.tensor_tensor(out=rb_[64:65, :], in0=r_[64:65, :], in1=g_[64:65, :], op=ALU.mult),
                             reads=[f"{name}rd{ab}", f"{name}gr{ab}"], writes=[f"{name}rb{ab}"])
                    else:
                        T.op("dve", lambda: V.reciprocal(out=r_[64:65, :], in_=acc[a3][64:65, :]), reads=[ak], writes=[f"{name}rd{ab}"])
                        T.op("dve", lambda: V.tensor_copy(out=rb_[64:65, :], in_=r_[64:65, :]), reads=[f"{name}rd{ab}"], writes=[f"{name}rb{ab}"])
                    if prev is not None:
                        T.dma("sp", pv_[:], prev(h)[:, q0:q0 + 512], writes=[f"{name}pv{ab}"])
                    T.op("pe", lambda: P.matmul(bcp[:], lhsT=ones_b[64:65, 0:64], rhs=rb_[64:65, :], start=True, stop=True),
                         reads=[f"{name}rb{ab}", "ones_b"], writes=[f"{name}bcp"])
                    T.op("act", lambda: A.activation(out=bc_[:], in_=bcp[:], func=AF.Copy), reads=[f"{name}bcp"], writes=[f"{name}bc{ab}"])
                    if prev is None:
                        tgt, tk = (ob_, f"{name}ob{ab}") if out_bf16 else (o_, f"{name}ot{ab}")
                        T.op("dve", lambda: V.tensor_tensor(out=tgt[:], in0=acc[a3][0:64, :], in1=bc_[:], op=ALU.mult),
                             reads=[ak, f"{name}bc{ab}"], writes=[tk])
                    else:
                        T.op("dve", lambda: V.tensor_tensor(out=o_[:], in0=acc[a3][0:64, :], in1=bc_[:], op=ALU.mult),
                             reads=[ak, f"{name}bc{ab}"], writes=[f"{name}ot{ab}"])
                        tgt, tk = (ob_, f"{name}ob{ab}") if out_bf16 else (o_, f"{name}ot{ab}")
                        T.op("pool", lambda: G.tensor_tensor(out=tgt[:], in0=o_[:], in1=pv_[:], op=ALU.add),
                             reads=[f"{name}ot{ab}", f"{name}pv{ab}"], writes=[tk])
                    T.dma("pool", out_ap(h)[:, q0:q0 + 512], tgt[:], reads=[tk])

                LA = 2
                n = len(flat)
                for i in range(n + LA):
                    if i < n:
                        slot_of[i] = emit_S(i)
                    if i >= LA:
                        emit_PV(i - LA)
                while pending:
                    finalize(*pending.pop(0))
                T.barrier()

        def kt_causal(qt):
            tl = [(kt, 0, 512, []) for kt in range(4 * qt)]
            for j in range(4):
                tl.append((4 * qt + j, 128 * j, 512, [(128 * j, "diag")]))
            return tl

        def kt_window(qt):
            tl = []
            for m in (0, 1, 2, 3):
                tl.append((4 * qt + m, 128 * m, 512, [(128 * m, "diag")]))
            for m in (-1, -2, -3, -4):
                kt = 4 * qt + m
                if kt < 0:
                    continue
                tl.append((kt, 0, 128 * (m + 5), [(128 * (m + 4), "wtri")]))
            return tl

        def phase_D(l, which="fsw"):
            def vload(src_ap, vt, vk):
                sv = src_ap.rearrange("(k p) d -> p k d", p=128)
                for k0 in range(0, NKT, 4):
                    T.dma("sp", vt[:, k0:k0 + 4, 0:64], sv[:, k0:k0 + 4, :], writes=[vk])

            if "f" in which:
                def lh(hk, kt_, kk, vt, vk):
                    T.dma("sp", kt_[0:64, :], FKT[hk * 64:(hk + 1) * 64, :], writes=[kk])
                    vload(FV[:, hk * 64:(hk + 1) * 64], vt, vk)

                def lq(h, qt, g, qt_, qk):
                    T.dma("sp", qt_[0:64, :], FQT[h * 64:(h + 1) * 64, qt * 512:(qt + 1) * 512], writes=[qk])
                    T.dma("sp", qt_[64:65, :], CQ8[h:h + 1, qt * 512:(qt + 1) * 512], writes=[qk])
                attn_pass("fx", 65, DBG_HEADS or list(range(6)), lambda h: h, lh, lq, 1, lambda h, kt: negc[:, kt, h:h + 1], kt_causal,
                          None, None, lambda h: MIXT[640 + h * 64:640 + (h + 1) * 64, :], True)
            if "s" in which:
                def lh(hk, kt_, kk, vt, vk):
                    T.dma("sp", kt_[0:64, :], KSL[hk * 64:(hk + 1) * 64, :], writes=[kk])
                    vload(VSL[:, hk * 64:(hk + 1) * 64], vt, vk)

                def lq(h, qt, g, qt_, qk):
                    T.dma("sp", qt_[0:64, :], QTN[h * 64:(h + 1) * 64, qt * 512:(qt + 1) * 512], writes=[qk])
                    T.dma("sp", qt_[64:128, :], MASKT[h][g * 64:(g + 1) * 64, qt * 512:(qt + 1) * 512], writes=[qk])
                attn_pass("sl", 128, DBG_HEADS or list(range(6)), lambda h: h // 3, lh, lq, 2, lambda h, kt: nsakb[:, h, kt:kt + 1], kt_causal,
                          lambda h: h * 3 + 1, lambda h: OCT[h], lambda h: OST[h], False)
            if "w" in which:
                def lh(hk, kt_, kk, vt, vk):
                    T.dma("sp", kt_[0:64, :], KWN[hk * 64:(hk + 1) * 64, :], writes=[kk])
                    vload(VWN[:, hk * 64:(hk + 1) * 64], vt, vk)

                def lq(h, qt, g, qt_, qk):
                    T.dma("sp", qt_[0:64, :], QTN[h * 64:(h + 1) * 64, qt * 512:(qt + 1) * 512], writes=[qk])
                    T.dma("sp", qt_[64:65, :], NSACQ[h:h + 1, qt * 512:(qt + 1) * 512], writes=[qk])
                attn_pass("wn", 65, DBG_HEADS or list(range(6)), lambda h: h // 3, lh, lq, 1, lambda h, kt: nsakb[:, h, kt:kt + 1], kt_window,
                          lambda h: h * 3 + 2, lambda h: OST[h], lambda h: MIXT[h * 64:(h + 1) * 64, :], True)

        def phase_E(l):
            with contextlib.ExitStack() as st:
                wdw = sb("wdwE", [128, 2, 31], F32, st)
                bdw = sb("bdwE", [128, 2], F32, st)
                lng = sb("lngE", [128, 2], F32, st)
                lnb = sb("lnbE", [128, 2], F32, st)
                U = [sb(f"UE{i}", [128, 542], F32, st) for i in range(4)]
                Hh = [sb(f"HE{i}", [128, 512], F32, st) for i in range(4)]
                Hq = [sb(f"HqE{i}", [128, 512], F32, st) for i in range(2)]
                mean = sb("meanE", [128, 512], F32, st)
                msq = sb("msqE", [128, 512], F32, st)
                rstd = sb("rstdE", [128, 512], F32, st)
                xh = [sb(f"xhE{i}", [128, 512], F32, st) for i in range(2)]
                ob = [sb(f"obE{i}", [128, 512], BF16, st) for i in range(2)]
                p1 = ps("p1E", [128, 512], F32, st)
                p2 = ps("p2E", [128, 512], F32, st)
                transpose_load("wdwE", w_dw[l], 31, 2, lambda c: wdw[:, c, :])
                transpose_load("bdwE", b_dw[l].rearrange("(o n) -> o n", o=1), 1, 2, lambda c: bdw[:, c:c + 1])
                transpose_load("lngE", ln_conv_g[l].rearrange("(o n) -> o n", o=1), 1, 2, lambda c: lng[:, c:c + 1])
                transpose_load("lnbE", ln_conv_b[l].rearrange("(o n) -> o n", o=1), 1, 2, lambda c: lnb[:, c:c + 1])
                n = 0
                for tt in range(NQT):
                    q0 = tt * 512
                    hs = []
                    for c in range(2):
                        u = U[n % 4]; uk = f"UE{n % 4}"; hh = Hh[n % 4]; hk = f"HE{n % 4}"; n += 1
                        if tt == 0:
                            T.op("pool", lambda u=u: G.memset(u[:, 0:30], 0.0), writes=[uk])
                            T.dma("sp", u[:, 30:542], GLU[c * 128:(c + 1) * 128, 0:512], writes=[uk])
                        else:
                            T.dma("sp", u[:], GLU[c * 128:(c + 1) * 128, q0 - 30:q0 + 512], writes=[uk])
                        T.op("dve", lambda u=u, hh=hh, c=c: V.tensor_scalar(out=hh[:], in0=u[:, 0:512], scalar1=wdw[:, c, 0:1], scalar2=bdw[:, c:c + 1],
                                                                            op0=ALU.mult, op1=ALU.add), reads=[uk, "wdwE", "bdwE"], writes=[hk])
                        for k in range(1, 31):
                            T.op("dve", lambda u=u, hh=hh, c=c, k=k: V.scalar_tensor_tensor(out=hh[:], in0=u[:, k:k + 512], scalar=wdw[:, c, k:k + 1], in1=hh[:],
                                                                                          op0=ALU.mult, op1=ALU.add), reads=[uk, "wdwE", hk], writes=[hk])
                        hq = Hq[c]
                        T.op("act", lambda hh=hh, hq=hq: A.activation(out=hq[:], in_=hh[:], func=AF.Square), reads=[hk], writes=[f"HqE{c}"])
                        hs.append((hh, hk))
                    T.group("pe", [lambda c=c: P.matmul(p1[:], lhsT=ones_f[:], rhs=hs[c][0][:], start=(c == 0), stop=(c == 1)) for c in range(2)],
                            reads=["ones_f", hs[0][1], hs[1][1]], writes=["p1E"])
                    T.group("pe", [lambda c=c: P.matmul(p2[:], lhsT=ones_f[:], rhs=Hq[c][:], start=(c == 0), stop=(c == 1)) for c in range(2)],
                            reads=["ones_f", "HqE0", "HqE1"], writes=["p2E"])
                    T.op("dve", lambda: V.tensor_scalar(out=mean[:], in0=p1[:], scalar1=1.0 / 256, scalar2=None, op0=ALU.mult), reads=["p1E"], writes=["meanE"])
                    T.op("dve", lambda: V.tensor_tensor(out=msq[:], in0=mean[:], in1=mean[:], op=ALU.mult), reads=["meanE"], writes=["msqE"])
                    T.op("dve", lambda: V.scalar_tensor_tensor(out=rstd[:], in0=p2[:], scalar=1.0 / 256, in1=msq[:], op0=ALU.mult, op1=ALU.subtract),
                         reads=["p2E", "msqE"], writes=["rstdE"])
                    T.op("act", lambda: A.activation(out=rstd[:], in_=rstd[:], func=AF.Sqrt, bias=epsb[:, 0:1], scale=1.0), reads=["rstdE", "epsb"], writes=["rstdE"])
                    T.op("dve", lambda: V.reciprocal(out=rstd[:], in_=rstd[:]), reads=["rstdE"], writes=["rstdE"])
                    for c in range(2):
                        hh, hk = hs[c]
                        x_ = xh[c]; xk = f"xhE{c}"
                        T.op("dve", lambda hh=hh, x_=x_: V.tensor_tensor(out=x_[:], in0=hh[:], in1=mean[:], op=ALU.subtract), reads=[hk, "meanE"], writes=[xk])
                        T.op("dve", lambda x_=x_: V.tensor_tensor(out=x_[:], in0=x_[:], in1=rstd[:], op=ALU.mult), reads=[xk, "rstdE"], writes=[xk])
                        T.op("dve", lambda x_=x_, c=c: V.tensor_scalar(out=x_[:], in0=x_[:], scalar1=lng[:, c:c + 1], scalar2=lnb[:, c:c + 1], op0=ALU.mult, op1=ALU.add),
                             reads=[xk, "lngE", "lnbE"], writes=[xk])
                        o_ = ob[c]
                        T.op("act", lambda x_=x_, o_=o_: A.activation(out=o_[:], in_=x_[:], func=AF.Silu), reads=[xk], writes=[f"obE{c}"])
                        T.dma("pool", MIXT[384 + c * 128:384 + (c + 1) * 128, q0:q0 + 512], o_[:], reads=[f"obE{c}"])
                T.barrier()

        def phase_F(l):
            with contextlib.ExitStack() as st:
                wo = sb("woF", [128, 8, D], BF16, st)
                stg = [sb(f"stgF{i}", [128, D], F32, st) for i in range(2)]
                load_cast(st, "woF", lambda c: wo[:, c, :], lambda c: w_out[l, c * 128:(c + 1) * 128, :], 8, D, stg)
                gt = sb("gF", [128, D], F32, st); bt = sb("bF", [128, D], F32, st)
                load_gb(gt, bt, ln1_g[l], ln1_b[l])
                mt = [sb(f"mtF{i}", [128, 8, 128], BF16, st) for i in range(3)]
                xi = [sb(f"xiF{i}", [128, D], F32, st) for i in range(3)]
                pp = [ps(f"ppF{i}", [128, 512], F32, st) for i in range(4)]
                for ti in range(S // 128):
                    r0 = ti * 128
                    m = mt[ti % 3]; mk = f"mtF{ti % 3}"; x_ = xi[ti % 3]; xk = f"xiF{ti % 3}"
                    for c in range(8):
                        T.dma("sp", m[:, c, :], MIXT[c * 128:(c + 1) * 128, r0:r0 + 128], writes=[mk])
                    T.dma("sp", x_[:], XA[r0:r0 + 128, :], writes=[xk])
                    for nh in range(2):
                        pb = pp[(ti * 2 + nh) % 4]; pk = f"ppF{(ti * 2 + nh) % 4}"
                        T.group("pe", [lambda k=k, pb=pb, nh=nh, m=m: P.matmul(pb[:], lhsT=m[:, k, :], rhs=wo[:, k, nh * 512:(nh + 1) * 512], start=(k == 0), stop=(k == 7))
                                       for k in range(8)], reads=[mk, "woF"], writes=[pk])
                        T.op("dve", lambda pb=pb, nh=nh, x_=x_: V.scalar_tensor_tensor(out=x_[:, nh * 512:(nh + 1) * 512], in0=x_[:, nh * 512:(nh + 1) * 512], scalar=ALPHA,
                                                                                   in1=pb[:], op0=ALU.mult, op1=ALU.add), reads=[pk, xk], writes=[xk])
                    layer_norm_rows(LNT, x_, xk, (gt, bt))
                    T.dma("pool", XB[r0:r0 + 128, :], x_[:], reads=[xk])
                T.barrier()

        def phase_G(l, last):
            TT = 256
            with contextlib.ExitStack() as st:
                wfi = sb("wfiG", [128, 8, 2 * DFF], BF16, st)
                wfd = sb("wfdG", [128, NF, D], BF16, st)
                with contextlib.ExitStack() as st2:
                    stg = [sb(f"stgG{i}", [128, 2048], F32, st2) for i in range(2)]
                    n = 0
                    for c in range(8):
                        for q in range(3):
                            c0 = q * 2048; w_ = min(2048, 2 * DFF - c0)
                            sg = stg[n % 2]; sk = f"stgG{n % 2}"; n += 1
                            T.dma("sp", sg[:, :w_], w_ffn_in[l, c * 128:(c + 1) * 128, c0:c0 + w_], writes=[sk])
                            eng = ("dve", "pool")[n % 2]; E = V if eng == "dve" else G
                            T.op(eng, lambda E=E, sg=sg, c=c, c0=c0, w_=w_: E.tensor_copy(out=wfi[:, c, c0:c0 + w_], in_=sg[:, :w_]), reads=[sk], writes=["wfiG"])
                    for c in range(NF):
                        sg = stg[n % 2]; sk = f"stgG{n % 2}"; n += 1
                        T.dma("sp", sg[:, :D], w_ffn_down[l, c * 128:(c + 1) * 128, :], writes=[sk])
                        eng = ("dve", "pool")[n % 2]; E = V if eng == "dve" else G
                        T.op(eng, lambda E=E, sg=sg, c=c: E.tensor_copy(out=wfd[:, c, :], in_=sg[:, :D]), reads=[sk], writes=["wfdG"])
                    T.barrier()
                gt = sb("gG", [128, D], F32, st); bt = sb("bG", [128, D], F32, st)
                load_gb(gt, bt, ln2_g[l], ln2_b[l])
                wcv = sb("wcvG", [128, NF, 3], F32, st)
                bcv = sb("bcvG", [128, NF], F32, st)
                transpose_load("wcvG", w_ffn_conv[l], 3, NF, lambda c: wcv[:, c, :])
                transpose_load("bcvG", b_ffn_conv[l].rearrange("(o n) -> o n", o=1), 1, NF, lambda c: bcv[:, c:c + 1])
                halo = sb("haloG", [128, NF, 2], F32, st)
                T.op("pool", lambda: G.memset(halo[:], 0.0), writes=["haloG"])
                xi = [sb(f"xiG{i}", [128, D], F32, st) for i in range(3)]
                xb = [sb(f"xbG{i}", [128, D], BF16, st) for i in range(2)]
                xT = [sb(f"xTG{i}", [128, 8, TT], BF16, st) for i in range(2)]
                hT = [sb(f"hTG{i}", [128, NF, TT], BF16, st) for i in range(1)]
                Gs = [sb(f"GsG{i}", [128, 2, TT + 2], F32, st) for i in range(2)]
                t1 = [sb(f"t1G{i}", [128, 2, TT], F32, st) for i in range(2)]
                sgl = [sb(f"sgG{i}", [128, 2, TT], F32, st) for i in range(1)]
                ptr = [ps(f"ptrG{i}", [128, 1024], BF16, st) for i in range(2)]
                pg = [ps(f"pgG{i}", [128, 2, TT], F32, st) for i in range(2)]
                pu = [ps(f"puG{i}", [128, 2, TT], F32, st) for i in range(2)]
                pd = [ps(f"pdG{i}", [128, 512], F32, st) for i in range(2)]
                nsub = TT // 128
                dst = y_out if last else XA
                it = 0
                for tt in range(S // TT):
                    xt = xT[tt % 2]; xtk = f"xTG{tt % 2}"; ht = hT[0]; htk = "hTG0"
                    xs = []
                    for sub in range(nsub):
                        nn = tt * nsub + sub
                        r0 = nn * 128
                        x_ = xi[nn % 3]; xk = f"xiG{nn % 3}"; b_ = xb[nn % 2]; bk = f"xbG{nn % 2}"
                        T.dma("sp", x_[:], XB[r0:r0 + 128, :], writes=[xk])
                        T.op("pool", lambda b_=b_, x_=x_: G.tensor_copy(out=b_[:], in_=x_[:]), reads=[xk], writes=[bk])
                        pt = ptr[nn % 2]; ptk = f"ptrG{nn % 2}"
                        T.group("pe", [lambda k=k, pt=pt, b_=b_: P.transpose(out=pt[:, k * 128:(k + 1) * 128], in_=b_[:, k * 128:(k + 1) * 128], identity=ident_b[:]) for k in range(8)],
                                reads=[bk, "ident_b"], writes=[ptk])
                        T.op("act", lambda pt=pt, xt=xt, sub=sub: A.activation(out=xt[:, :, sub * 128:(sub + 1) * 128], in_=pt[:].rearrange("p (k t) -> p k t", k=8), func=AF.Copy),
                             reads=[ptk], writes=[xtk])
                        xs.append((x_, xk, r0))
                    for fp in range(NF // 2):
                        b = it % 2; it += 1
                        f0 = 2 * fp
                        T.group("pe", [lambda k=k, b=b, j=j, f0=f0: P.matmul(pg[b][:, j, :], lhsT=wfi[:, k, (f0 + j) * 128:(f0 + j + 1) * 128], rhs=xt[:, k, :],
                                                                            start=(k == 0), stop=(k == 7)) for j in range(2) for k in range(8)],
                                reads=["wfiG", xtk], writes=[f"pgG{b}"])
                        T.group("pe", [lambda k=k, b=b, j=j, f0=f0: P.matmul(pu[b][:, j, :], lhsT=wfi[:, k, DFF + (f0 + j) * 128:DFF + (f0 + j + 1) * 128], rhs=xt[:, k, :],
                                                                            start=(k == 0), stop=(k == 7)) for j in range(2) for k in range(8)],
                                reads=["wfiG", xtk], writes=[f"puG{b}"])
                        gs = Gs[b]; gk = f"GsG{b}"; t_ = t1[b]; s_ = sgl[0]; sk = "sgG0"
                        T.op("pool", lambda gs=gs, f0=f0: G.tensor_copy(out=gs[:, :, 0:2], in_=halo[:, f0:f0 + 2, :]), reads=["haloG"], writes=[gk])
                        T.op("act", lambda gs=gs, b=b: A.activation(out=gs[:, :, 2:TT + 2], in_=pg[b][:], func=AF.Copy), reads=[f"pgG{b}"], writes=[gk])
                        T.op("pool", lambda gs=gs, f0=f0: G.tensor_copy(out=halo[:, f0:f0 + 2, :], in_=gs[:, :, TT:TT + 2]), reads=[gk], writes=["haloG"])
                        for j in range(2):
                            T.op("dve", lambda gs=gs, t_=t_, j=j, f0=f0: V.tensor_scalar(out=t_[:, j, :], in0=gs[:, j, 0:TT], scalar1=wcv[:, f0 + j, 0:1], scalar2=bcv[:, f0 + j:f0 + j + 1],
                                                                                   op0=ALU.mult, op1=ALU.add), reads=[gk, "wcvG", "bcvG"], writes=[f"t1G{b}_{j}"])
                        for j in range(2):
                            T.op("dve", lambda gs=gs, t_=t_, j=j, f0=f0: V.scalar_tensor_tensor(out=t_[:, j, :], in0=gs[:, j, 1:TT + 1], scalar=wcv[:, f0 + j, 1:2], in1=t_[:, j, :],
                                                                                          op0=ALU.mult, op1=ALU.add), reads=[gk, "wcvG", f"t1G{b}_{j}"], writes=[f"t1G{b}_{j}"])
                        for j in range(2):
                            T.op("dve", lambda gs=gs, t_=t_, j=j, f0=f0: V.scalar_tensor_tensor(out=t_[:, j, :], in0=gs[:, j, 2:TT + 2], scalar=wcv[:, f0 + j, 2:3], in1=t_[:, j, :],
                                                                                          op0=ALU.mult, op1=ALU.add), reads=[gk, "wcvG", f"t1G{b}_{j}"], writes=[f"t1G{b}_{j}"])
                        T.op("act", lambda t_=t_, s_=s_: A.activation(out=s_[:], in_=t_[:], func=AF.Silu), reads=[f"t1G{b}_0", f"t1G{b}_1"], writes=[sk])
                        T.op("dve", lambda s_=s_, b=b, f0=f0: V.tensor_tensor(out=ht[:, f0:f0 + 2, :], in0=s_[:], in1=pu[b][:], op=ALU.mult), reads=[sk, f"puG{b}"], writes=[htk])
                    for sub in range(nsub):
                        x_, xk, r0 = xs[sub]
                        for nh in range(2):
                            pb = pd[nh]; pk = f"pdG{nh}"
                            T.group("pe", [lambda fc=fc, pb=pb, nh=nh, sub=sub: P.matmul(pb[:], lhsT=ht[:, fc, sub * 128:(sub + 1) * 128], rhs=wfd[:, fc, nh * 512:(nh + 1) * 512],
                                                                                     start=(fc == 0), stop=(fc == NF - 1)) for fc in range(NF)],
                                    reads=[htk, "wfdG"], writes=[pk])
                            T.op("dve", lambda pb=pb, nh=nh, x_=x_: V.scalar_tensor_tensor(out=x_[:, nh * 512:(nh + 1) * 512], in0=x_[:, nh * 512:(nh + 1) * 512], scalar=ALPHA,
                                                                                       in1=pb[:], op0=ALU.mult, op1=ALU.add), reads=[pk, xk], writes=[xk])
                        layer_norm_rows(LNT, x_, xk, (gt, bt))
                        T.dma("pool", dst[r0:r0 + 128, :], x_[:], reads=[xk])
                T.barrier()

        for l in range(DEPTH):
            if layers is not None and l not in layers:
                continue
            if "A" in phases:
                phase_A(l)
            if "B" in phases:
                phase_B(l)
            if "C" in phases:
                phase_C(l)
            if "D" in phases:
                phase_D(l, DBG_WHICH)
            if "E" in phases:
                phase_E(l)
            if "F" in phases:
                phase_F(l)
            if "G" in phases:
                phase_G(l, l == DEPTH - 1)
        T.barrier()
        print("ninst", T.ninst, "nsem", T.nsem)
    return nc


_NC = None


def kernel(**inputs):
    global _NC
    if _NC is None:
        _NC = build()
    x = np.asarray(inputs["x"], dtype=np.float32)
    nb = x.shape[0]
    shared = {k: np.ascontiguousarray(np.asarray(v, dtype=np.float32)) for k, v in inputs.items() if k != "x"}
    in_maps = []
    for b in range(nb):
        m = dict(shared)
        m["x"] = np.ascontiguousarray(x[b])
        in_maps.append(m)
    res = run_bass_kernel_spmd(_NC, in_maps, core_ids=list(range(nb)))
    return np.stack([np.asarray(r["y"], dtype=np.float32) for r in res.results], axis=0)
```

```python
import contextlib
import math
import numpy as np
import concourse.bass as bass
import concourse.mybir as mybir
from concourse.bass_utils import run_bass_kernel_spmd

F32 = mybir.dt.float32
BF16 = mybir.dt.bfloat16
I32 = mybir.dt.int32
AF = mybir.ActivationFunctionType
ALU = mybir.AluOpType
AX = mybir.AxisListType

S = 8192
D = 1024
DEPTH = 2
NQT = S // 512
NKT = S // 128
PROJ_W = 2840
DFF = 2816
NF = DFF // 128
ALPHA = (2.0 * DEPTH) ** 0.25
LN_EPS = 1e-5
SLOPES = [2.0 ** (-8.0 * (i + 1) / 6) for i in range(6)]
NEGM = 240000.0
DBG_NTT = None
DBG_WHICH = "fsw"
DBG_HEADS = None
C_NQ, C_KCR, C_VCR, C_KSL, C_VSL, C_KWN, C_VWN, C_GATE, C_CA, C_CG, C_FQ, C_FK, C_FV, C_FF = (
    0, 384, 512, 640, 768, 896, 1024, 1152, 1170, 1426, 1682, 2066, 2450, 2834)


class Ev:
    __slots__ = ("sem", "val", "src")

    def __init__(self, sem, val, src):
        self.sem, self.val, self.src = sem, val, src


class Tracker:
    def __init__(self, nc, stack):
        self.nc = nc
        self.stack = stack
        self.eng = {"pe": nc.tensor, "act": nc.scalar, "dve": nc.vector, "pool": nc.gpsimd, "sp": nc.sync}
        self.sem = {}
        self.cnt = {}
        self.nsem = 0
        for e in self.eng:
            self._new_epoch(e)
        self.seen = {e: {} for e in self.eng}
        self.last_w = {}
        self.readers = {}
        self.last_ev = {}
        self.dma_sems = {}
        self.dma_rr = {}
        for q, n in (("sp", 20), ("pool", 12), ("act", 6)):
            self.dma_sems[q] = [[self._mk_sem(f"d{q}{i}"), 0] for i in range(n)]
            self.dma_rr[q] = 0
        self.ninst = 0
        self.excl = set()

    def _mk_sem(self, name):
        self.nsem += 1
        return self.stack.enter_context(self.nc.semaphore(f"{name}_{self.nsem}"))

    def _new_epoch(self, e):
        self.sem[e] = self._mk_sem(f"s{e}")
        self.cnt[e] = 0

    def wait(self, e, ev):
        k = id(ev.sem)
        if self.seen[e].get(k, 0) >= ev.val:
            return
        self.eng[e].wait_ge(ev.sem, ev.val)
        self.seen[e][k] = ev.val

    def _split(self, reads, writes):
        if self.excl:
            ex = [k for k in reads if k in self.excl]
            if ex:
                reads = [k for k in reads if k not in self.excl]
                writes = list(writes) + [k for k in ex if k not in writes]
        return reads, writes

    def _deps(self, e, reads, writes, srcid):
        deps = []
        for k in reads:
            w = self.last_w.get(k)
            if w is not None:
                deps.append(w)
        for k in writes:
            w = self.last_w.get(k)
            if w is not None and w.src != srcid:
                deps.append(w)
            for r in self.readers.get(k, {}).values():
                if r.src != srcid:
                    deps.append(r)
        for d in deps:
            self.wait(e, d)

    def _record(self, ev, reads, writes):
        for k in reads:
            self.readers.setdefault(k, {})[ev.src] = ev
        for k in writes:
            self.last_w[k] = ev
            self.readers[k] = {}

    def op(self, e, fn, reads=(), writes=()):
        reads, writes = self._split(reads, writes)
        self._deps(e, reads, writes, e)
        inst = fn()
        if self.cnt[e] >= 30000:
            self._new_epoch(e)
        self.cnt[e] += 1
        inst.then_inc(self.sem[e], 1)
        ev = Ev(self.sem[e], self.cnt[e], e)
        self.last_ev[e] = ev
        self._record(ev, reads, writes)
        self.ninst += 1
        return ev

    def group(self, e, fns, reads=(), writes=()):
        reads, writes = self._split(reads, writes)
        self._deps(e, reads, writes, e)
        inst = None
        for fn in fns:
            inst = fn()
            self.ninst += 1
        if self.cnt[e] >= 30000:
            self._new_epoch(e)
        self.cnt[e] += 1
        inst.then_inc(self.sem[e], 1)
        ev = Ev(self.sem[e], self.cnt[e], e)
        self.last_ev[e] = ev
        self._record(ev, reads, writes)
        return ev

    def dma(self, q, out, in_, reads=(), writes=(), **kw):
        pool = self.dma_sems[q]
        i = self.dma_rr[q]
        self.dma_rr[q] = (i + 1) % len(pool)
        ent = pool[i]
        if ent[1] >= 30000:
            ent[0] = self._mk_sem(f"d{q}")
            ent[1] = 0
        sem, val = ent
        srcid = ("dma", id(sem))
        if val > 0:
            self.wait(q, Ev(sem, val, srcid))
        self._deps(q, reads, writes, srcid)
        self.eng[q].dma_start(out=out, in_=in_, **kw).then_inc(sem, 16)
        ent[1] = val + 16
        ev = Ev(sem, ent[1], srcid)
        self._record(ev, reads, writes)
        self.ninst += 1
        return ev

    def barrier(self):
        evs = list(self.last_ev.values())
        for q in self.dma_sems:
            for sem, val in self.dma_sems[q]:
                if val > 0:
                    evs.append(Ev(sem, val, ("dma", id(sem))))
        for e in self.eng:
            for ev in evs:
                self.wait(e, ev)
        self.last_w.clear()
        self.readers.clear()


def build(dbg=(), phases="ABCDEFGH", layers=None):
    nc = bass.Bass("TRN2", target_bir_lowering=False)
    ES = contextlib.ExitStack()
    with ES:
        T = Tracker(nc, ES)
        V, A, P, G = nc.vector, nc.scalar, nc.tensor, nc.gpsimd

        def dram_in(name, shape):
            return nc.dram_tensor(name, list(shape), F32, kind="ExternalInput").ap()

        def dram(name, shape, dt):
            kind = "ExternalOutput" if name in dbg else "Internal"
            return nc.dram_tensor(name, list(shape), dt, kind=kind).ap()

        x_in = dram_in("x", [S, D])
        ln_emb_g = dram_in("ln_emb_g", [D]); ln_emb_b = dram_in("ln_emb_b", [D])
        w_in = dram_in("w_in", [DEPTH, D, PROJ_W]); b_f = dram_in("b_f", [DEPTH, 6])
        w_cmp1 = dram_in("w_cmp1", [DEPTH, 2, 2048, 256]); w_cmp2 = dram_in("w_cmp2", [DEPTH, 2, 256, 64])
        pe_cmp = dram_in("pe_cmp", [DEPTH, 2, 32, 64])
        w_dw = dram_in("w_dw", [DEPTH, 31, 256]); b_dw = dram_in("b_dw", [DEPTH, 256])
        ln_conv_g = dram_in("ln_conv_g", [DEPTH, 256]); ln_conv_b = dram_in("ln_conv_b", [DEPTH, 256])
        w_out = dram_in("w_out", [DEPTH, D, D])
        ln1_g = dram_in("ln1_g", [DEPTH, D]); ln1_b = dram_in("ln1_b", [DEPTH, D])
        w_ffn_in = dram_in("w_ffn_in", [DEPTH, D, 2 * DFF]); w_ffn_conv = dram_in("w_ffn_conv", [DEPTH, 3, DFF])
        b_ffn_conv = dram_in("b_ffn_conv", [DEPTH, DFF]); w_ffn_down = dram_in("w_ffn_down", [DEPTH, DFF, D])
        ln2_g = dram_in("ln2_g", [DEPTH, D]); ln2_b = dram_in("ln2_b", [DEPTH, D])
        y_out = nc.dram_tensor("y", [S, D], F32, kind="ExternalOutput").ap()

        XA = dram("XA", [S, D], F32); XB = dram("XB", [S, D], F32)
        QTN = dram("QTN", [384, S], BF16)
        KCR = dram("KCR", [128, S], BF16); VCR = dram("VCR", [128, S], BF16)
        KSL = dram("KSL", [128, S], BF16); KWN = dram("KWN", [128, S], BF16)
        VSL = dram("VSL", [S, 128], BF16); VWN = dram("VWN", [S, 128], BF16)
        GT = dram("GT", [18, S], F32); GTM = dram("GTM", [S, 18], F32)
        GLU = dram("GLU", [256, S], F32)
        FQT = dram("FQT", [384, S], BF16); FKT = dram("FKT", [384, S], BF16); FV = dram("FV", [S, 384], BF16)
        FLT = dram("FLT", [6, S], F32); CQ8 = dram("CQ8", [6, S], BF16); CFX = dram("CFX", [6, S], F32)
        NSACQ = dram("NSACQ", [6, S], BF16)
        MASKT = dram("MASKT", [6, 128, S], BF16)
        OCT = dram("OCT", [6, 64, S], F32)
        MIXT = dram("MIXT", [D, S], BF16)
        OST = dram("OST", [6, 64, S], F32)
        KCMP = dram("KCMP", [2, 64, 256], BF16); VCMP = dram("VCMP", [2, 256, 64], BF16)
        SCORE = dram("SCORE", [2, S, 128], F32); SELM = dram("SELM", [2, S, 128], F32)

        uniq = [0]

        def sb(name, shape, dt, st=ES):
            uniq[0] += 1
            return st.enter_context(nc.sbuf_tensor(f"{name}_{uniq[0]}", list(shape), dt))

        def ps(name, shape, dt, st=ES):
            uniq[0] += 1
            esz = 4 if dt == F32 else 2
            t = st.enter_context(nc.psum_tensor(f"{name}_{uniq[0]}", [128, 2048 // esz], dt))
            T.excl.add(name)
            free = 1
            for d_ in shape[1:]:
                free *= d_
            assert free * esz <= 2048, (name, shape)
            v = t[0:shape[0], 0:free]
            if len(shape) == 3:
                v = v.rearrange("p (a b) -> p a b", a=shape[1])
            return v

        ident_b = sb("ident_b", [128, 128], BF16)
        ident_f = sb("ident_f", [128, 128], F32)
        ones_b = sb("ones_b", [128, 128], BF16)
        ones_f = sb("ones_f", [128, 128], F32)
        tpos = sb("tpos", [128, 64], F32)
        nsakb = sb("nsakb", [128, 6, 64], F32)
        nsaqb = sb("nsaqb", [128, 6, 64], F32)
        T.op("pool", lambda: G.memset(ones_b[:], 1.0), writes=["ones_b"])
        T.op("pool", lambda: G.memset(ones_f[:], 1.0), writes=["ones_f"])
        T.op("pool", lambda: G.affine_select(out=ident_b[:], in_=ones_b[:], pattern=[[1, 128]], compare_op=ALU.is_equal,
                                             fill=0.0, base=0, channel_multiplier=-1), reads=["ones_b"], writes=["ident_b"])
        T.op("pool", lambda: G.affine_select(out=ident_f[:], in_=ones_f[:], pattern=[[1, 128]], compare_op=ALU.is_equal,
                                             fill=0.0, base=0, channel_multiplier=-1), reads=["ones_f"], writes=["ident_f"])
        T.op("pool", lambda: G.iota(tpos[:], pattern=[[128, 64]], base=0, channel_multiplier=1,
                                    allow_small_or_imprecise_dtypes=True), writes=["tpos"])
        for h in range(6):
            T.op("dve", lambda h=h: V.tensor_scalar(out=nsakb[:, h, :], in0=tpos[:], scalar1=SLOPES[h], scalar2=None,
                                                    op0=ALU.mult), reads=["tpos"], writes=["nsakb"])
            T.op("dve", lambda h=h: V.tensor_scalar(out=nsaqb[:, h, :], in0=tpos[:], scalar1=-8.0 * SLOPES[h], scalar2=-NEGM,
                                                    op0=ALU.mult, op1=ALU.add), reads=["tpos"], writes=["nsaqb"])
        with contextlib.ExitStack() as st:
            trow = sb("trow", [1, S], F32, st)
            crow = sb("crow", [1, S], BF16, st)
            T.op("pool", lambda: G.iota(trow[:], pattern=[[1, S]], base=0, channel_multiplier=0,
                                        allow_small_or_imprecise_dtypes=True), writes=["trow"])
            for h in range(6):
                T.op("dve", lambda h=h: V.tensor_scalar(out=crow[:], in0=trow[:], scalar1=-8.0 * SLOPES[h], scalar2=None,
                                                        op0=ALU.mult), reads=["trow"], writes=["crow"])
                T.dma("sp", NSACQ[h:h + 1, :], crow[:], reads=["crow"])
            T.barrier()

        cmptab = sb("cmptab", [128, 6, 512], F32)
        atab = sb("atab", [128, 256], F32)
        btab = sb("btab", [128, 256], F32)
        with contextlib.ExitStack() as st:
            dist = sb("dist", [128, 512], F32, st)
            dt_ = sb("dt_", [128, 256], F32, st)
            t1 = sb("t1", [128, 256], F32, st)
            T.op("pool", lambda: G.iota(dist[:], pattern=[[-32, 512]], base=8033, channel_multiplier=1,
                                        allow_small_or_imprecise_dtypes=True), writes=["dist"])
            for h in range(6):
                T.op("dve", lambda h=h: V.tensor_scalar(out=cmptab[:, h, :], in0=dist[:], scalar1=-SLOPES[h], scalar2=None,
                                                        op0=ALU.mult), reads=["dist"], writes=["cmptab"])
                T.op("pool", lambda h=h: G.affine_select(out=cmptab[:, h, :], in_=cmptab[:, h, :], pattern=[[-32, 512]],
                                                         compare_op=ALU.is_ge, fill=-30000.0, base=8033, channel_multiplier=1),
                     reads=["cmptab"], writes=["cmptab"])
            T.op("pool", lambda: G.iota(dt_[0:64, :], pattern=[[1, 256]], base=-126, channel_multiplier=0,
                                        allow_small_or_imprecise_dtypes=True), writes=["dt_"])
            T.op("pool", lambda: G.iota(dt_[64:128, :], pattern=[[1, 256]], base=-127, channel_multiplier=0,
                                        allow_small_or_imprecise_dtypes=True), writes=["dt_"])
            T.op("dve", lambda: V.tensor_scalar(out=atab[:], in0=dt_[:], scalar1=0.0, scalar2=None, op0=ALU.is_lt),
                 reads=["dt_"], writes=["atab"])
            T.op("dve", lambda: V.tensor_scalar(out=btab[:], in0=dt_[:], scalar1=0.0, scalar2=4.0, op0=ALU.is_equal, op1=ALU.mult),
                 reads=["dt_"], writes=["btab"])
            T.op("dve", lambda: V.tensor_scalar(out=t1[:], in0=dt_[:], scalar1=0.0, scalar2=-1.0, op0=ALU.is_gt, op1=ALU.mult),
                 reads=["dt_"], writes=["t1"])
            T.op("dve", lambda: V.tensor_tensor(out=btab[:], in0=btab[:], in1=t1[:], op=ALU.add),
                 reads=["btab", "t1"], writes=["btab"])
            T.barrier()

        def load_cast(st, name, dst, src_of_chunk, nchunks, width, stage):
            for c in range(nchunks):
                sg = stage[c % 2]
                T.dma("sp", sg[:, :width], src_of_chunk(c), writes=[f"{name}_stg{c % 2}"])
                eng = ("dve", "pool")[c % 2]
                E = V if eng == "dve" else G
                T.op(eng, lambda E=E, c=c, sg=sg: E.tensor_copy(out=dst(c), in_=sg[:, :width]),
                     reads=[f"{name}_stg{c % 2}"], writes=[name])

        def layer_norm_multi(items, gb, n=D):
            nch = n // 512
            its = [(y, k, j) for j, (y, k) in enumerate(items)]
            for y, k, j in its:
                for c in range(nch):
                    T.op("dve", lambda y=y, c=c, j=j: V.bn_stats(out=lnS[j][:, c, :], in_=y[:, c * 512:(c + 1) * 512]), reads=[k], writes=[f"lnS{j}"])
            for y, k, j in its:
                T.op("dve", lambda j=j: V.bn_aggr(out=lnM[j][:, 0:2], in_=lnS[j][:, 0:nch, :]), reads=[f"lnS{j}"], writes=[f"lnM{j}"])
            for y, k, j in its:
                T.op("act", lambda j=j: A.activation(out=lnM[j][:, 2:3], in_=lnM[j][:, 1:2], func=AF.Sqrt, bias=epsb[:, 0:1], scale=1.0),
                     reads=[f"lnM{j}"], writes=[f"lnR{j}"])
            for y, k, j in its:
                T.op("dve", lambda j=j: V.reciprocal(out=lnM[j][:, 2:3], in_=lnM[j][:, 2:3]), reads=[f"lnR{j}"], writes=[f"lnR{j}"])
            for y, k, j in its:
                T.op("dve", lambda j=j: V.tensor_scalar(out=lnM[j][:, 3:4], in0=lnM[j][:, 0:1], scalar1=-1.0, scalar2=lnM[j][:, 2:3], op0=ALU.mult, op1=ALU.mult),
                     reads=[f"lnM{j}", f"lnR{j}"], writes=[f"lnN{j}"])
            for y, k, j in its:
                T.op("act", lambda y=y, j=j: A.activation(out=y[:], in_=y[:], func=AF.Identity, bias=lnM[j][:, 3:4], scale=lnM[j][:, 2:3]),
                     reads=[k, f"lnN{j}", f"lnR{j}"], writes=[k])
            for y, k, j in its:
                T.op("dve", lambda y=y: V.tensor_tensor(out=y[:], in0=y[:], in1=gb[0][:], op=ALU.mult), reads=[k, "ln_gb"], writes=[k])
            for y, k, j in its:
                T.op("dve", lambda y=y: V.tensor_tensor(out=y[:], in0=y[:], in1=gb[1][:], op=ALU.add), reads=[k, "ln_gb"], writes=[k])

        def layer_norm_rows(st_tiles, ytile, ykey, gb, n=D):
            layer_norm_multi([(ytile, ykey)], gb, n)

        def transpose_load(name, src_ap, n, C, dst_fn, dup=1):
            with contextlib.ExitStack() as st_:
                sg = sb(f"{name}_tl", [n, C * 128], F32, st_)
                pt = ps(f"{name}_tp", [128, C * n], F32, st_)
                T.dma("sp", sg[:], src_ap, writes=[f"{name}_tl"])
                T.group("pe", [lambda c=c: P.transpose(out=pt[:, c * n:(c + 1) * n], in_=sg[0:n, c * 128:(c + 1) * 128], identity=ident_f[0:n, 0:n])
                               for c in range(C)], reads=[f"{name}_tl", "ident_f"], writes=[f"{name}_tp"])
                for c in range(C):
                    T.op("dve", lambda c=c: V.tensor_copy(out=dst_fn(c), in_=pt[:, c * n:(c + 1) * n]), reads=[f"{name}_tp"], writes=[f"{name}_dst"])
                T.barrier()

        epsb = sb("epsb", [128, 1], F32)
        T.op("pool", lambda: G.memset(epsb[:], LN_EPS), writes=["epsb"])
        lnS = [sb(f"lnS{j}", [128, 2, 6], F32) for j in range(4)]
        lnM = [sb(f"lnM{j}", [128, 4], F32) for j in range(4)]
        ln_stats = sb("ln_stats", [128, 2, 6], F32)
        ln_mv = sb("ln_mv", [128, 2], F32)
        ln_rstd = sb("ln_rstd", [128, 1], F32)
        LNT = (ln_stats, ln_mv, ln_rstd)
        T.barrier()

        def load_gb(gt, bt, g_ap, b_ap):
            T.dma("sp", gt[:], g_ap.partition_broadcast(128), writes=["ln_gb"])
            T.dma("sp", bt[:], b_ap.partition_broadcast(128), writes=["ln_gb"])

        def phase_A(l):
            with contextlib.ExitStack() as st:
                win = sb("win", [128, 8, PROJ_W], BF16, st)
                stg = [sb(f"wstg{i}", [128, PROJ_W], F32, st) for i in range(2)]
                load_cast(st, "win", lambda c: win[:, c, :], lambda c: w_in[l, c * 128:(c + 1) * 128, :], 8, PROJ_W, stg)
                gt = sb("gA", [128, D], F32, st); bt = sb("bA", [128, D], F32, st)
                if l == 0:
                    load_gb(gt, bt, ln_emb_g, ln_emb_b)
                xin = [sb(f"xinA{i}", [128, D], F32, st) for i in range(6)]
                xbf = [sb(f"xbfA{i}", [128, D], BF16, st) for i in range(2)]
                xT = [sb(f"xTA{i}", [128, 8, 512], BF16, st) for i in range(2)]
                peT = sb("peT", [128, 2, 512], F32, st)
                negbf = sb("negbf", [6, 1], F32, st)
                ob = [sb(f"obA{i}", [128, 512], BF16, st) for i in range(4)]
                of = [sb(f"ofA{i}", [128, 512], F32, st) for i in range(3)]
                sg = [sb(f"sgA{i}", [128, 512], F32, st) for i in range(2)]
                tmv = [sb(f"tmvA{i}", [128, 640], BF16, st) for i in range(2)]
                tmg = [sb(f"tmgA{i}", [128, 18], F32, st) for i in range(2)]
                ptr = [ps(f"ptrA{i}", [128, 1024], BF16, st) for i in range(2)]
                pfm = [ps(f"pfmA{i}", [128, 512], F32, st) for i in range(3)]
                ptm = [ps(f"ptmA{i}", [128, 512], F32, st) for i in range(2)]
                with contextlib.ExitStack() as st_:
                    pes = sb("pesA", [32, 2, 128], F32, st_)
                    ptp = ps("ptpA", [128, 2, 32], F32, st_)
                    for i in range(2):
                        for kvh in range(2):
                            T.dma("sp", pes[:, i, kvh * 64:(kvh + 1) * 64], pe_cmp[l, i], writes=["pesA"])
                    T.group("pe", [lambda i=i: P.transpose(out=ptp[:, i, :], in_=pes[0:32, i, :], identity=ident_f[0:32, 0:32]) for i in range(2)],
                            reads=["pesA", "ident_f"], writes=["ptpA"])
                    for i in range(2):
                        T.op("dve", lambda i=i: V.tensor_copy(out=peT[:, i, 0:32], in_=ptp[:, i, :]), reads=["ptpA"], writes=["peT"])
                    T.barrier()
                for i in range(2):
                    w_ = 32
                    while w_ < 512:
                        T.op("dve", lambda i=i, w_=w_: V.tensor_copy(out=peT[:, i, w_:2 * w_], in_=peT[:, i, 0:w_]),
                             reads=["peT"], writes=["peT"])
                        w_ *= 2
                T.dma("sp", negbf[:], b_f[l].rearrange("(h o) -> h o", o=1), writes=["negbf"], allow_slow_non_contiguous=True)
                T.op("dve", lambda: V.tensor_scalar(out=negbf[:], in0=negbf[:], scalar1=-1.0, scalar2=None, op0=ALU.mult),
                     reads=["negbf"], writes=["negbf"])
                src = x_in if l == 0 else XA
                cnt = {"ob": 0, "of": 0, "pfm": 0, "ev": 0}

                def evac(eng, out, in_, rk, wk, **kw):
                    if eng == "act":
                        return T.op("act", lambda: A.activation(out=out, in_=in_, func=kw.get("func", AF.Copy)), reads=rk, writes=wk)
                    return T.op("dve", lambda: V.tensor_copy(out=out, in_=in_), reads=rk, writes=wk)

                def prep(tt):
                    xt = xT[tt % 2]; xtk = f"xTA{tt % 2}"
                    for sub in range(4):
                        r0 = tt * 512 + sub * 128
                        n = tt * 4 + sub
                        T.dma("sp", xin[n % 6][:], src[r0:r0 + 128, :], writes=[f"xinA{n % 6}"])
                    if l == 0:
                        layer_norm_multi([(xin[(tt * 4 + sub) % 6], f"xinA{(tt * 4 + sub) % 6}") for sub in range(4)], (gt, bt))
                    for sub in range(4):
                        r0 = tt * 512 + sub * 128
                        n = tt * 4 + sub
                        xi = xin[n % 6]; xik = f"xinA{n % 6}"
                        xb = xbf[n % 2]; xbk = f"xbfA{n % 2}"
                        if l == 0:
                            T.dma("pool", XA[r0:r0 + 128, :], xi[:], reads=[xik])
                        T.op("dve", lambda xb=xb, xi=xi: V.tensor_copy(out=xb[:], in_=xi[:]), reads=[xik], writes=[xbk])
                        pt = ptr[n % 2]; ptk = f"ptrA{n % 2}"
                        T.group("pe", [lambda k=k, pt=pt, xb=xb: P.transpose(out=pt[:, k * 128:(k + 1) * 128], in_=xb[:, k * 128:(k + 1) * 128],
                                                                              identity=ident_b[:]) for k in range(8)],
                                reads=[xbk, "ident_b"], writes=[ptk])
                        T.op("act", lambda pt=pt, xt=xt, sub=sub: A.activation(
                            out=xt[:, :, sub * 128:(sub + 1) * 128], in_=pt[:].rearrange("p (k t) -> p k t", k=8), func=AF.Copy),
                            reads=[ptk], writes=[xtk])
                ntt = NQT if DBG_NTT is None else DBG_NTT
                prep(0)
                for tt in range(ntt):
                    if tt + 1 < ntt:
                        prep(tt + 1)
                    xt = xT[tt % 2]; xtk = f"xTA{tt % 2}"
                    q0 = tt * 512
                    def fm(col, width):
                        i = cnt["pfm"] % 3; cnt["pfm"] += 1
                        pp = pfm[i]
                        T.group("pe", [lambda k=k: P.matmul(pp[0:width, :], lhsT=win[:, k, col:col + width], rhs=xt[:, k, :],
                                                            start=(k == 0), stop=(k == 7)) for k in range(8)],
                                reads=["win", xtk], writes=[f"pfmA{i}"])
                        return pp, f"pfmA{i}"

                    def out_bf(pp, pk, width, dst):
                        i = cnt["ob"] % 4; cnt["ob"] += 1
                        o = ob[i]
                        cnt["ev"] += 1
                        evac(("act", "dve")[cnt["ev"] % 2], o[0:width, :], pp[0:width, :], [pk], [f"obA{i}"])
                        T.dma("sp", dst, o[0:width, :], reads=[f"obA{i}"])

                    for c in range(3):
                        pp, pk = fm(C_NQ + c * 128, 128)
                        out_bf(pp, pk, 128, QTN[c * 128:(c + 1) * 128, q0:q0 + 512])
                    for i2, (col, dst) in enumerate(((C_KCR, KCR), (C_VCR, VCR))):
                        pp, pk = fm(col, 128)
                        i = cnt["ob"] % 4; cnt["ob"] += 1
                        o = ob[i]
                        T.op("dve", lambda o=o, pp=pp, i2=i2: V.tensor_tensor(out=o[:], in0=pp[:], in1=peT[:, i2, :], op=ALU.add),
                             reads=[pk, "peT"], writes=[f"obA{i}"])
                        T.dma("sp", dst[:, q0:q0 + 512], o[:], reads=[f"obA{i}"])
                    for col, dst in ((C_KSL, KSL), (C_KWN, KWN)):
                        pp, pk = fm(col, 128)
                        out_bf(pp, pk, 128, dst[:, q0:q0 + 512])
                    pp, pk = fm(C_GATE, 18)
                    i = cnt["of"] % 3; cnt["of"] += 1
                    T.op("act", lambda pp=pp, i=i: A.activation(out=of[i][0:18, :], in_=pp[0:18, :], func=AF.Sigmoid),
                         reads=[pk], writes=[f"ofA{i}"])
                    T.dma("sp", GT[:, q0:q0 + 512], of[i][0:18, :], reads=[f"ofA{i}"])
                    for c in range(2):
                        pg, pgk = fm(C_CG + c * 128, 128)
                        s_ = sg[c % 2]
                        T.op("act", lambda pg=pg, s_=s_: A.activation(out=s_[:], in_=pg[:], func=AF.Sigmoid),
                             reads=[pgk], writes=[f"sgA{c % 2}"])
                        pa, pak = fm(C_CA + c * 128, 128)
                        i = cnt["of"] % 3; cnt["of"] += 1
                        T.op("dve", lambda pa=pa, s_=s_, i=i: V.tensor_tensor(out=of[i][:], in0=pa[:], in1=s_[:], op=ALU.mult),
                             reads=[pak, f"sgA{c % 2}"], writes=[f"ofA{i}"])
                        T.dma("sp", GLU[c * 128:(c + 1) * 128, q0:q0 + 512], of[i][:], reads=[f"ofA{i}"])
                    for c in range(3):
                        pp, pk = fm(C_FQ + c * 128, 128)
                        out_bf(pp, pk, 128, FQT[c * 128:(c + 1) * 128, q0:q0 + 512])
                    for c in range(3):
                        pp, pk = fm(C_FK + c * 128, 128)
                        out_bf(pp, pk, 128, FKT[c * 128:(c + 1) * 128, q0:q0 + 512])
                    pp, pk = fm(C_FF, 6)
                    i = cnt["of"] % 3; cnt["of"] += 1
                    T.op("act", lambda pp=pp, i=i: A.activation(out=of[i][0:6, :], in_=pp[0:6, :], func=AF.Exp, bias=negbf[:, 0:1], scale=-1.0),
                         reads=[pk, "negbf"], writes=[f"ofA{i}"])
                    T.op("act", lambda i=i: A.activation(out=of[i][0:6, :], in_=of[i][0:6, :], func=AF.Ln, bias=1.0, scale=1.0),
                         reads=[f"ofA{i}"], writes=[f"ofA{i}"])
                    T.op("dve", lambda i=i: V.tensor_scalar(out=of[i][0:6, :], in0=of[i][0:6, :], scalar1=-1.0, scalar2=None, op0=ALU.mult),
                         reads=[f"ofA{i}"], writes=[f"ofA{i}"])
                    T.dma("sp", FLT[:, q0:q0 + 512], of[i][0:6, :], reads=[f"ofA{i}"])
                    for sub in range(4):
                        r0 = q0 + sub * 128
                        n = tt * 4 + sub
                        pa_ = ptm[0]; pb_ = ptm[1]
                        lt = lambda k: xt[:, k, sub * 128:(sub + 1) * 128]
                        T.group("pe",
                                [lambda k=k: P.matmul(pa_[:, 0:128], lhsT=lt(k), rhs=win[:, k, C_VSL:C_VSL + 128], start=(k == 0), stop=(k == 7)) for k in range(8)] +
                                [lambda k=k: P.matmul(pa_[:, 128:256], lhsT=lt(k), rhs=win[:, k, C_VWN:C_VWN + 128], start=(k == 0), stop=(k == 7)) for k in range(8)] +
                                [lambda k=k: P.matmul(pa_[:, 256:274], lhsT=lt(k), rhs=win[:, k, C_GATE:C_GATE + 18], start=(k == 0), stop=(k == 7)) for k in range(8)],
                                reads=["win", xtk], writes=["ptmA0"])
                        T.group("pe",
                                [lambda k=k: P.matmul(pb_[:, 0:384], lhsT=lt(k), rhs=win[:, k, C_FV:C_FV + 384], start=(k == 0), stop=(k == 7)) for k in range(8)],
                                reads=["win", xtk], writes=["ptmA1"])
                        tv = tmv[n % 2]; tg = tmg[n % 2]
                        T.op("dve", lambda tv=tv: V.tensor_copy(out=tv[:, 0:256], in_=pa_[:, 0:256]), reads=["ptmA0"], writes=[f"tmvA{n % 2}"])
                        T.op("act", lambda tg=tg: A.activation(out=tg[:], in_=pa_[:, 256:274], func=AF.Sigmoid), reads=["ptmA0"], writes=[f"tmgA{n % 2}"])
                        T.op("act", lambda tv=tv: A.activation(out=tv[:, 256:640], in_=pb_[:, 0:384], func=AF.Copy), reads=["ptmA1"], writes=[f"tmvA{n % 2}"])
                        T.dma("sp", VSL[r0:r0 + 128, :], tv[:, 0:128], reads=[f"tmvA{n % 2}"])
                        T.dma("sp", VWN[r0:r0 + 128, :], tv[:, 128:256], reads=[f"tmvA{n % 2}"])
                        T.dma("sp", FV[r0:r0 + 128, :], tv[:, 256:640], reads=[f"tmvA{n % 2}"])
                        T.dma("sp", GTM[r0:r0 + 128, :], tg[:], reads=[f"tmgA{n % 2}"])
                T.barrier()

        negc = sb("negc", [128, 64, 6], F32)

        def phase_B(l):
            with contextlib.ExitStack() as st:
                lf = sb("lfB", [6, S], F32, st)
                on6 = sb("on6B", [6, S], F32, st)
                cs = sb("csB", [6, S], F32, st)
                cq = sb("cqB", [6, S], BF16, st)
                pst = ps("pstB", [128, 384], F32, st)
                T.dma("sp", lf[:], FLT[:, :], writes=["lfB"])
                T.op("pool", lambda: G.memset(on6[:], 1.0), writes=["on6B"])
                T.op("dve", lambda: V.tensor_tensor_scan(out=cs[:], data0=on6[:], data1=lf[:], initial=0.0, op0=ALU.mult, op1=ALU.add),
                     reads=["lfB", "on6B"], writes=["csB"])
                T.op("dve", lambda: V.tensor_scalar(out=cq[:], in0=cs[:], scalar1=8.0, scalar2=None, op0=ALU.mult),
                     reads=["csB"], writes=["cqB"])
                T.dma("sp", CQ8[:, :], cq[:], reads=["cqB"])
                if "CFX" in dbg:
                    T.dma("sp", CFX[:, :], cs[:], reads=["csB"])
                T.group("pe", [lambda kt=kt: P.transpose(out=pst[:, kt * 6:(kt + 1) * 6], in_=cs[0:6, kt * 128:(kt + 1) * 128],
                                                         identity=ident_f[0:6, 0:6]) for kt in range(64)],
                        reads=["csB", "ident_f"], writes=["pstB"])
                T.op("dve", lambda: V.tensor_scalar(out=negc[:].rearrange("p k h -> p (k h)"), in0=pst[:], scalar1=-1.0, scalar2=None, op0=ALU.mult),
                     reads=["pstB"], writes=["negc"])
                T.barrier()

        def phase_C(l):
            with contextlib.ExitStack() as st:
                w1 = [sb(f"w1C{i}", [128, 32, 256], BF16, st) for i in range(2)]
                w2 = [sb(f"w2C{i}", [128, 2, 64], BF16, st) for i in range(2)]
                kcmp = [sb(f"kcmpC{k}", [64, 256], BF16, st) for k in range(2)]
                vcmp = [sb(f"vcmpC{k}", [128, 2, 64], BF16, st) for k in range(2)]
                with contextlib.ExitStack() as st2:
                    stg = [sb(f"stgC{i}", [128, 8, 256], F32, st2) for i in range(2)]
                    stg2 = sb("stg2C", [128, 2, 64], F32, st2)
                    src = [sb(f"srcC{i}", [128, S], BF16, st2) for i in range(2)]
                    hT = [[sb(f"hTC{i}{k}", [128, 2, 256], BF16, st2) for k in range(2)] for i in range(2)]
                    tq = [sb(f"tqC{i}", [128, 256], F32, st2) for i in range(2)]
                    ph = [ps(f"phC{i}", [128, 256], F32, st2) for i in range(2)]
                    pk = ps("pkC", [128, 256], F32, st2)
                    n = 0
                    for i in range(2):
                        for t8 in range(4):
                            sg = stg[n % 2]; sk = f"stgC{n % 2}"; n += 1
                            srcap = w_cmp1[l, i, t8 * 512:(t8 + 1) * 512, :].rearrange("(t d) n -> d t n", d=64)
                            T.dma("sp", sg[0:64], srcap, writes=[sk])
                            T.dma("sp", sg[64:128], srcap, writes=[sk])
                            T.op("pool", lambda sg=sg, i=i, t8=t8: G.tensor_copy(out=w1[i][:, t8 * 8:(t8 + 1) * 8, :], in_=sg[:]),
                                 reads=[sk], writes=[f"w1C{i}"])
                        T.dma("sp", stg2[:], w_cmp2[l, i].rearrange("(c p) n -> p c n", p=128), writes=["stg2C"])
                        T.op("dve", lambda i=i: V.tensor_copy(out=w2[i][:], in_=stg2[:]), reads=["stg2C"], writes=[f"w2C{i}"])
                        T.dma("sp", src[i][:], (KCR, VCR)[i][:, :], writes=[f"srcC{i}"])
                    g = 0
                    for i in range(2):
                        for kvh in range(2):
                            sview = src[i][kvh * 64:(kvh + 1) * 64, :].rearrange("p (n t) -> p t n", t=32)
                            for hc in range(2):
                                pp = ph[g % 2]; ppk = f"phC{g % 2}"; tt_ = tq[g % 2]; tk = f"tqC{g % 2}"; g += 1
                                T.group("pe", [lambda t=t, pp=pp, sview=sview: P.matmul(
                                    pp[:], lhsT=w1[i][kvh * 64:(kvh + 1) * 64, t, hc * 128:(hc + 1) * 128], rhs=sview[:, t, :],
                                    start=(t == 0), stop=(t == 31)) for t in range(32)],
                                    reads=[f"w1C{i}", f"srcC{i}"], writes=[ppk])
                                T.op("act", lambda pp=pp, tt_=tt_: A.activation(out=tt_[:], in_=pp[:], func=AF.Square), reads=[ppk], writes=[tk])
                                T.op("dve", lambda tt_=tt_: V.tensor_scalar(out=tt_[:], in0=tt_[:], scalar1=0.044715, scalar2=1.0, op0=ALU.mult, op1=ALU.add),
                                     reads=[tk], writes=[tk])
                                T.op("dve", lambda pp=pp, tt_=tt_: V.tensor_tensor(out=tt_[:], in0=tt_[:], in1=pp[:], op=ALU.mult), reads=[tk, ppk], writes=[tk])
                                T.op("act", lambda tt_=tt_: A.activation(out=tt_[:], in_=tt_[:], func=AF.Sigmoid, scale=1.5957691216), reads=[tk], writes=[tk])
                                T.op("dve", lambda pp=pp, tt_=tt_, i=i, kvh=kvh, hc=hc: V.tensor_tensor(out=hT[i][kvh][:, hc, :], in0=tt_[:], in1=pp[:], op=ALU.mult),
                                     reads=[tk, ppk], writes=[f"hTC{i}{kvh}"])
                    for kvh in range(2):
                        T.group("pe", [lambda hc=hc: P.matmul(pk[0:64, :], lhsT=w2[0][:, hc, :], rhs=hT[0][kvh][:, hc, :], start=(hc == 0), stop=(hc == 1)) for hc in range(2)],
                                reads=["w2C0", f"hTC0{kvh}"], writes=["pkC"])
                        T.op("act", lambda kvh=kvh: A.activation(out=kcmp[kvh][:], in_=pk[0:64, :], func=AF.Copy), reads=["pkC"], writes=[f"kcmpC{kvh}"])
                        for c in range(2):
                            T.group("pe", [lambda hc=hc, c=c: P.matmul(pk[:, 0:64], lhsT=hT[1][kvh][:, hc, c * 128:(c + 1) * 128], rhs=w2[1][:, hc, :],
                                                                   start=(hc == 0), stop=(hc == 1)) for hc in range(2)],
                                    reads=["w2C1", f"hTC1{kvh}"], writes=["pkC"])
                            T.op("act", lambda kvh=kvh, c=c: A.activation(out=vcmp[kvh][:, c, :], in_=pk[:, 0:64], func=AF.Copy), reads=["pkC"], writes=[f"vcmpC{kvh}"])
                    if "KCMP" in dbg:
                        for kvh in range(2):
                            T.dma("sp", KCMP[kvh], kcmp[kvh][:], reads=[f"kcmpC{kvh}"])
                            T.dma("sp", VCMP[kvh].rearrange("(c p) d -> p c d", p=128), vcmp[kvh][:], reads=[f"vcmpC{kvh}"])
                    T.barrier()
                qc = [sb(f"qcC{i}", [64, 3, 128], BF16, st) for i in range(4)]
                gtm = [sb(f"gtmC{i}", [128, 18], F32, st) for i in range(2)]
                zt = [sb(f"zC{i}", [128, 256], F32, st) for i in range(6)]
                et = [sb(f"eC{i}", [128, 256], F32, st) for i in range(6)]
                pb = [sb(f"pbC{i}", [128, 256], BF16, st) for i in range(6)]
                pT = [sb(f"pTC{i}", [128, 2, 128], BF16, st) for i in range(6)]
                sm = [sb(f"smC{i}", [128, 4], F32, st) for i in range(6)]
                imp = [sb(f"impC{i}", [128, 256], F32, st) for i in range(2)]
                sc = [sb(f"scC{i}", [128, 128], F32, st) for i in range(2)]
                sc2 = [sb(f"sc2C{i}", [128, 128], F32, st) for i in range(2)]
                m8 = [sb(f"m8C{i}", [128, 16], F32, st) for i in range(2)]
                msk = [sb(f"mskC{i}", [128, 128], F32, st) for i in range(2)]
                mv = [sb(f"mvC{i}", [128, 128], F32, st) for i in range(6)]
                ocs = [[sb(f"ocsC{h}_{i}", [64, 512], F32, st) for i in range(2)] for h in range(6)]
                mts = [[sb(f"mtsC{h}_{i}", [128, 512], BF16, st) for i in range(2)] for h in range(6)]
                psS = [ps(f"psSC{i}", [128, 256], F32, st) for i in range(3)]
                psT = [ps(f"psTC{i}", [128, 2, 128], BF16, st) for i in range(2)]
                psO = [ps(f"psOC{i}", [64, 128], F32, st) for i in range(2)]
                psM = [ps(f"psMC{i}", [128, 128], F32, st) for i in range(1)]
                nq = S // 128

                def load_qi(qi):
                    q0 = qi * 128
                    T.dma("sp", gtm[qi % 2][:], GTM[q0:q0 + 128, :], writes=[f"gtmC{qi % 2}"])
                    for kvh in range(2):
                        qn = qi * 2 + kvh
                        T.dma("sp", qc[qn % 4][:], QTN[kvh * 192:(kvh + 1) * 192, q0:q0 + 128].rearrange("(g p) t -> p g t", p=64),
                              writes=[f"qcC{qn % 4}"])

                load_qi(0)
                cnt2 = [0]
                for qi in range(nq):
                    if qi + 1 < nq:
                        load_qi(qi + 1)
                    q0 = qi * 128
                    gm = gtm[qi % 2]; gmk = f"gtmC{qi % 2}"
                    blk = qi // 4; sub = qi % 4; par = blk % 2
                    for kvh in range(2):
                        qn = qi * 2 + kvh
                        qq = qc[qn % 4]; qqk = f"qcC{qn % 4}"
                        im = imp[kvh]; imk = f"impC{kvh}"
                        HS = [(g, kvh * 3 + g) for g in range(3)]
                        for g, h in HS:
                            T.op("pe", lambda g=g: P.matmul(psS[g][:], lhsT=qq[:, g, :], rhs=kcmp[kvh][:], start=True, stop=True),
                                 reads=[qqk, f"kcmpC{kvh}"], writes=[f"psSC{g}"])
                        for g, h in HS:
                            T.op("dve", lambda g=g, h=h: V.scalar_tensor_tensor(out=zt[h][:], in0=psS[g][:], scalar=0.125, in1=cmptab[:, h, 252 - 4 * qi:508 - 4 * qi],
                                                                               op0=ALU.mult, op1=ALU.add), reads=[f"psSC{g}", "cmptab"], writes=[f"zC{h}"])
                        for g, h in HS:
                            T.op("dve", lambda h=h: V.tensor_reduce(out=sm[h][:, 0:1], in_=zt[h][:], axis=AX.X, op=ALU.max), reads=[f"zC{h}"], writes=[f"smC{h}"])
                        for g, h in HS:
                            T.op("dve", lambda h=h: V.tensor_scalar(out=sm[h][:, 0:1], in0=sm[h][:, 0:1], scalar1=-1000.0, scalar2=-1.0, op0=ALU.max, op1=ALU.mult),
                                 reads=[f"smC{h}"], writes=[f"smC{h}"])
                        for g, h in HS:
                            T.op("act", lambda h=h: A.activation(out=et[h][:], in_=zt[h][:], func=AF.Exp, bias=sm[h][:, 0:1], scale=1.0, accum_out=sm[h][:, 1:2]),
                                 reads=[f"zC{h}", f"smC{h}"], writes=[f"eC{h}", f"smC{h}"])
                        for g, h in HS:
                            T.op("dve", lambda h=h: V.tensor_scalar(out=sm[h][:, 1:2], in0=sm[h][:, 1:2], scalar1=1e-30, scalar2=None, op0=ALU.max),
                                 reads=[f"smC{h}"], writes=[f"smC{h}"])
                        for g, h in HS:
                            T.op("dve", lambda h=h: V.reciprocal(out=sm[h][:, 2:3], in_=sm[h][:, 1:2]), reads=[f"smC{h}"], writes=[f"smC{h}"])
                        for g, h in HS:
                            T.op("dve", lambda h=h: V.tensor_tensor(out=sm[h][:, 3:4], in0=sm[h][:, 2:3], in1=gm[:, h * 3:h * 3 + 1], op=ALU.mult),
                                 reads=[f"smC{h}", gmk], writes=[f"smC{h}"])
                        for g, h in HS:
                            T.op("dve", lambda h=h: V.tensor_scalar(out=pb[h][:], in0=et[h][:], scalar1=sm[h][:, 3:4], scalar2=None, op0=ALU.mult),
                                 reads=[f"eC{h}", f"smC{h}"], writes=[f"pbC{h}"])
                        for g, h in HS:
                            if g == 0:
                                T.op("dve", lambda h=h: V.tensor_scalar(out=im[:], in0=et[h][:], scalar1=sm[h][:, 2:3], scalar2=None, op0=ALU.mult),
                                     reads=[f"eC{h}", f"smC{h}"], writes=[imk])
                            else:
                                T.op("dve", lambda h=h: V.scalar_tensor_tensor(out=im[:], in0=et[h][:], scalar=sm[h][:, 2:3], in1=im[:], op0=ALU.mult, op1=ALU.add),
                                     reads=[f"eC{h}", f"smC{h}", imk], writes=[imk])
                        for g, h in HS:
                            b = cnt2[0] % 2; cnt2[0] += 1
                            T.group("pe", [lambda c=c, b=b, h=h: P.transpose(out=psT[b][:, c, :], in_=pb[h][:, c * 128:(c + 1) * 128], identity=ident_b[:]) for c in range(2)],
                                    reads=[f"pbC{h}", "ident_b"], writes=[f"psTC{b}"])
                            T.op("act", lambda b=b, h=h: A.activation(out=pT[h][:], in_=psT[b][:], func=AF.Copy), reads=[f"psTC{b}"], writes=[f"pTC{h}"])
                        for g, h in HS:
                            b = cnt2[0] % 2; cnt2[0] += 1
                            T.group("pe", [lambda c=c, b=b, h=h: P.matmul(psO[b][:], lhsT=vcmp[kvh][:, c, :], rhs=pT[h][:, c, :], start=(c == 0), stop=(c == 1)) for c in range(2)],
                                    reads=[f"vcmpC{kvh}", f"pTC{h}"], writes=[f"psOC{b}"])
                            T.op("act", lambda b=b, h=h: A.activation(out=ocs[h][par][:, sub * 128:(sub + 1) * 128], in_=psO[b][:], func=AF.Copy),
                                 reads=[f"psOC{b}"], writes=[f"ocsC{h}_{par}"])
                            if sub == 3:
                                T.dma("pool", OCT[h][:, blk * 512:(blk + 1) * 512], ocs[h][par][:], reads=[f"ocsC{h}_{par}"])
                    chains = []
                    for kvh in range(2):
                        im = imp[kvh]; imk = f"impC{kvh}"
                        s_ = sc[kvh]; s_k = f"scC{kvh}"; s2 = sc2[kvh]; s2k = f"sc2C{kvh}"; m_ = m8[kvh]; mk = f"m8C{kvh}"; ms = msk[kvh]; msk_k = f"mskC{kvh}"
                        imv = im[:].rearrange("p (n r) -> p n r", r=2)
                        ch = [
                            ("dve", lambda s_=s_, imv=imv: V.tensor_tensor(out=s_[:], in0=imv[:, :, 0], in1=imv[:, :, 1], op=ALU.add), [imk], [s_k]),
                            ("dve", lambda s_=s_: V.tensor_tensor(out=s_[:], in0=s_[:], in1=atab[:, 126 - 2 * qi:254 - 2 * qi], op=ALU.mult), [s_k, "atab"], [s_k]),
                            ("dve", lambda s_=s_: V.tensor_tensor(out=s_[:], in0=s_[:], in1=btab[:, 126 - 2 * qi:254 - 2 * qi], op=ALU.add), [s_k, "btab"], [s_k]),
                            ("dve", lambda s_=s_: V.memset(s_[:, 0:1], 4.0), [s_k], [s_k]),
                            ("dve", lambda s_=s_, m_=m_: V.max(out=m_[:, 0:8], in_=s_[:]), [s_k], [mk]),
                            ("dve", lambda s_=s_, m_=m_, s2=s2: V.match_replace(out=s2[:], in_to_replace=m_[:, 0:8], in_values=s_[:], imm_value=-1e9), [s_k, mk], [s2k]),
                            ("dve", lambda m_=m_, s2=s2: V.max(out=m_[:, 8:16], in_=s2[:]), [s2k], [mk]),
                            ("dve", lambda s_=s_, m_=m_, ms=ms: V.tensor_scalar(out=ms[:], in0=s_[:], scalar1=m_[:, 15:16], scalar2=None, op0=ALU.is_ge), [s_k, mk], [msk_k]),
                        ]
                        for g in range(3):
                            h = kvh * 3 + g
                            ch.append(("dve", lambda ms=ms, h=h: V.tensor_scalar(out=mv[h][:], in0=ms[:], scalar1=NEGM, scalar2=nsaqb[:, h, qi:qi + 1], op0=ALU.mult, op1=ALU.add),
                                       [msk_k, "nsaqb"], [f"mvC{h}"]))
                        chains.append(ch)
                    for opa, opb in zip(chains[0], chains[1]):
                        for (e_, fn_, r_, w_) in (opa, opb):
                            T.op(e_, fn_, reads=r_, writes=w_)
                    if "SCORE" in dbg:
                        for kvh in range(2):
                            T.dma("sp", SCORE[kvh][q0:q0 + 128, :], sc[kvh][:], reads=[f"scC{kvh}"])
                            T.dma("sp", SELM[kvh][q0:q0 + 128, :], msk[kvh][:], reads=[f"mskC{kvh}"])
                    for h in range(6):
                        T.op("pe", lambda h=h: P.transpose(out=psM[0][:], in_=mv[h][:], identity=ident_f[:]), reads=[f"mvC{h}", "ident_f"], writes=["psMC0"])
                        T.op("act", lambda h=h: A.activation(out=mts[h][par][:, sub * 128:(sub + 1) * 128], in_=psM[0][:], func=AF.Copy),
                             reads=["psMC0"], writes=[f"mtsC{h}_{par}"])
                        if sub == 3:
                            T.dma("pool", MASKT[h][:, blk * 512:(blk + 1) * 512], mts[h][par][:], reads=[f"mtsC{h}_{par}"])
                T.barrier()

        def attn_pass(name, KR, heads, hkey, load_head, load_q, qgroups, bias_ap, ktasks, gate_row, prev, out_ap, out_bf16):
            with contextlib.ExitStack() as st:
                kaug = [sb(f"{name}K{i}", [KR, S], BF16, st) for i in range(2)]
                vaug = [sb(f"{name}V{i}", [128, NKT, 65], BF16, st) for i in range(2)]
                qaug = [sb(f"{name}Q{i}", [KR, 512], BF16, st) for i in range(3)]
                pbuf = [sb(f"{name}P{i}", [128, 512], BF16, st) for i in range(4)]
                rden = [sb(f"{name}rd{i}", [65, 512], F32, st) for i in range(2)]
                grow = [sb(f"{name}gr{i}", [65, 512], F32, st) for i in range(2)]
                rb = [sb(f"{name}rb{i}", [65, 512], BF16, st) for i in range(2)]
                bcs = [sb(f"{name}bc{i}", [64, 512], F32, st) for i in range(2)]
                prv = [sb(f"{name}pv{i}", [64, 512], F32, st) for i in range(2)]
                ot = [sb(f"{name}ot{i}", [64, 512], F32, st) for i in range(2)]
                otb = [sb(f"{name}ob{i}", [64, 512], BF16, st) for i in range(2)]
                sps = [ps(f"{name}s{i}", [128, 512], F32, st) for i in range(4)]
                acc = [ps(f"{name}a{i}", [128, 512], F32, st) for i in range(3)]
                bcp = ps(f"{name}bcp", [64, 512], F32, st)
                for i in range(2):
                    T.op("pool", lambda i=i: G.memset(vaug[i][:, :, 64:65], 1.0), writes=[f"{name}V{i}"])
                    if KR == 65:
                        T.op("pool", lambda i=i: G.memset(kaug[i][64:65, :], 1.0), writes=[f"{name}K{i}"])
                    else:
                        T.op("pool", lambda i=i: G.memset(kaug[i][64:128, :], 1.0), writes=[f"{name}K{i}"])
                        T.op("pool", lambda i=i: G.affine_select(out=kaug[i][64:128, :], in_=kaug[i][64:128, :], pattern=[[0, 2], [1, 64], [0, 64]],
                                                                 compare_op=ALU.is_equal, fill=0.0, base=0, channel_multiplier=-1),
                             reads=[f"{name}K{i}"], writes=[f"{name}K{i}"])
                units = []
                for h in heads:
                    for qt in range(NQT):
                        tl = ktasks(qt)
                        for g in range(qgroups):
                            sel = [t for t in tl if (qgroups == 1 or (t[0] // 32) == g)]
                            if sel:
                                units.append((h, qt, g, sel))
                flat = []
                for ui, (h, qt, g, sel) in enumerate(units):
                    first_of_hq = (ui == 0 or units[ui - 1][0] != h or units[ui - 1][1] != qt)
                    last_of_hq = (ui == len(units) - 1 or units[ui + 1][0] != h or units[ui + 1][1] != qt)
                    for j, t in enumerate(sel):
                        flat.append((ui, h, qt, t, first_of_hq and j == 0, last_of_hq and j == len(sel) - 1))
                state = {"hk": None, "hslot": -1, "uload": -1, "hq": -1}
                hk_list = []
                for h in heads:
                    if not hk_list or hk_list[-1] != hkey(h):
                        hk_list.append(hkey(h))

                def ensure_head(hk):
                    if state["hk"] == hk:
                        return
                    state["hk"] = hk
                    state["hslot"] += 1
                    sl = state["hslot"] % 2
                    load_head(hk, kaug[sl], f"{name}K{sl}", vaug[sl], f"{name}V{sl}")

                def ensure_unit(ui):
                    while state["uload"] < ui:
                        state["uload"] += 1
                        u = state["uload"]
                        h, qt, g, _ = units[u]
                        load_q(h, qt, g, qaug[u % 3], f"{name}Q{u % 3}")

                def emit_S(i):
                    ui, h, qt, (kt, c0, c1, masks), first, last = flat[i]
                    ensure_head(hkey(h))
                    ensure_unit(min(ui + 1, len(units) - 1) if False else ui)
                    sl = state["hslot"] % 2
                    b = i % 4
                    q = qaug[ui % 3]
                    T.op("pe", lambda: P.matmul(sps[b][:, c0:c1], lhsT=kaug[sl][0:KR, kt * 128:(kt + 1) * 128], rhs=q[0:KR, c0:c1], start=True, stop=True),
                         reads=[f"{name}K{sl}", f"{name}Q{ui % 3}"], writes=[f"{name}s{b}"])
                    T.op("act", lambda: A.activation(out=pbuf[b][:, c0:c1], in_=sps[b][:, c0:c1], func=AF.Exp, bias=bias_ap(h, kt), scale=0.125),
                         reads=[f"{name}s{b}"], writes=[f"{name}P{b}"])
                    for (mc, kind) in masks:
                        if kind == "diag":
                            T.op("pool", lambda mc=mc: G.affine_select(out=pbuf[b][:, mc:mc + 128], in_=pbuf[b][:, mc:mc + 128], pattern=[[1, 128]],
                                                                       compare_op=ALU.is_ge, fill=0.0, base=0, channel_multiplier=-1),
                                 reads=[f"{name}P{b}"], writes=[f"{name}P{b}"])
                        else:
                            T.op("pool", lambda mc=mc: G.affine_select(out=pbuf[b][:, mc:mc + 128], in_=pbuf[b][:, mc:mc + 128], pattern=[[-1, 128]],
                                                                       compare_op=ALU.is_ge, fill=0.0, base=-1, channel_multiplier=1),
                                 reads=[f"{name}P{b}"], writes=[f"{name}P{b}"])
                    return sl

                slot_of = {}

                def emit_PV(i):
                    ui, h, qt, (kt, c0, c1, masks), first, last = flat[i]
                    if first:
                        state["hq"] += 1
                    hq = state["hq"]
                    ab = hq % 3
                    b = i % 4
                    sl = slot_of[i]
                    T.op("pe", lambda: P.matmul(acc[ab][0:65, c0:c1], lhsT=vaug[sl][:, kt, :], rhs=pbuf[b][:, c0:c1], start=first, stop=last),
                         reads=[f"{name}V{sl}", f"{name}P{b}"], writes=[f"{name}a{ab}"])
                    if last:
                        if pending:
                            finalize(*pending.pop(0))
                        pending.append((h, qt, hq))

                pending = []

                def finalize(h, qt, hq):
                    a3 = hq % 3
                    ab = hq % 2
                    q0 = qt * 512
                    ak = f"{name}a{a3}"
                    r_ = rden[ab]; g_ = grow[ab]; rb_ = rb[ab]; bc_ = bcs[ab]; pv_ = prv[ab]; o_ = ot[ab]; ob_ = otb[ab]
                    if gate_row is not None:
                        T.dma("sp", g_[64:65, :], GT[gate_row(h):gate_row(h) + 1, q0:q0 + 512], writes=[f"{name}gr{ab}"])
                        T.op("dve", lambda: V.reciprocal(out=r_[64:65, :], in_=acc[a3][64:65, :]), reads=[ak], writes=[f"{name}rd{ab}"])
                        T.op("dve", lambda: V.tensor_tensor(out=rb_[64:65, :], in0=r_[64:65, :], in1=g_[64:65, :], op=ALU.mult),
                             reads=[f"{name}rd{ab}", f"{name}gr{ab}"], writes=[f"{name}rb{ab}"])
                    else:
                        T.op("dve", lambda: V.reciprocal(out=r_[64:65, :], in_=acc[a3][64:65, :]), reads=[ak], writes=[f"{name}rd{ab}"])
                        T.op("dve", lambda: V.tensor_copy(out=rb_[64:65, :], in_=r_[64:65, :]), reads=[f"{name}rd{ab}"], writes=[f"{name}rb{ab}"])
                    if prev is not None:
                        T.dma("sp", pv_[:], prev(h)[:, q0:q0 + 512], writes=[f"{name}pv{ab}"])
                    T.op("pe", lambda: P.matmul(bcp[:], lhsT=ones_b[64:65, 0:64], rhs=rb_[64:65, :], start=True, stop=True),
                         reads=[f"{name}rb{ab}", "ones_b"], writes=[f"{name}bcp"])
                    T.op("act", lambda: A.activation(out=bc_[:], in_=bcp[:], func=AF.Copy), reads=[f"{name}bcp"], writes=[f"{name}bc{ab}"])
                    if prev is None:
                        tgt, tk = (ob_, f"{name}ob{ab}") if out_bf16 else (o_, f"{name}ot{ab}")
                        T.op("dve", lambda: V.tensor_tensor(out=tgt[:], in0=acc[a3][0:64, :], in1=bc_[:], op=ALU.mult),
                             reads=[ak, f"{name}bc{ab}"], writes=[tk])
                    else:
                        T.op("dve", lambda: V.tensor_tensor(out=o_[:], in0=acc[a3][0:64, :], in1=bc_[:], op=ALU.mult),
                             reads=[ak, f"{name}bc{ab}"], writes=[f"{name}ot{ab}"])
                        tgt, tk = (ob_, f"{name}ob{ab}") if out_bf16 else (o_, f"{name}ot{ab}")
                        T.op("pool", lambda: G.tensor_tensor(out=tgt[:], in0=o_[:], in1=pv_[:], op=ALU.add),
                             reads=[f"{name}ot{ab}", f"{name}pv{ab}"], writes=[tk])
                    T.dma("pool", out_ap(h)[:, q0:q0 + 512], tgt[:], reads=[tk])

                LA = 2
                n = len(flat)
                for i in range(n + LA):
                    if i < n:
                        slot_of[i] = emit_S(i)
                    if i >= LA:
                        emit_PV(i - LA)
                while pending:
                    finalize(*pending.pop(0))
                T.barrier()

        def kt_causal(qt):
            tl = [(kt, 0, 512, []) for kt in range(4 * qt)]
            for j in range(4):
                tl.append((4 * qt + j, 128 * j, 512, [(128 * j, "diag")]))
            return tl

        def kt_window(qt):
            tl = []
            for m in (0, 1, 2, 3):
                tl.append((4 * qt + m, 128 * m, 512, [(128 * m, "diag")]))
            for m in (-1, -2, -3, -4):
                kt = 4 * qt + m
                if kt < 0:
                    continue
                tl.append((kt, 0, 128 * (m + 5), [(128 * (m + 4), "wtri")]))
            return tl

        def phase_D(l, which="fsw"):
            def vload(src_ap, vt, vk):
                sv = src_ap.rearrange("(k p) d -> p k d", p=128)
                for k0 in range(0, NKT, 4):
                    T.dma("sp", vt[:, k0:k0 + 4, 0:64], sv[:, k0:k0 + 4, :], writes=[vk])

            if "f" in which:
                def lh(hk, kt_, kk, vt, vk):
                    T.dma("sp", kt_[0:64, :], FKT[hk * 64:(hk + 1) * 64, :], writes=[kk])
                    vload(FV[:, hk * 64:(hk + 1) * 64], vt, vk)

                def lq(h, qt, g, qt_, qk):
                    T.dma("sp", qt_[0:64, :], FQT[h * 64:(h + 1) * 64, qt * 512:(qt + 1) * 512], writes=[qk])
                    T.dma("sp", qt_[64:65, :], CQ8[h:h + 1, qt * 512:(qt + 1) * 512], writes=[qk])
                attn_pass("fx", 65, DBG_HEADS or list(range(6)), lambda h: h, lh, lq, 1, lambda h, kt: negc[:, kt, h:h + 1], kt_causal,
                          None, None, lambda h: MIXT[640 + h * 64:640 + (h + 1) * 64, :], True)
            if "s" in which:
                def lh(hk, kt_, kk, vt, vk):
                    T.dma("sp", kt_[0:64, :], KSL[hk * 64:(hk + 1) * 64, :], writes=[kk])
                    vload(VSL[:, hk * 64:(hk + 1) * 64], vt, vk)

                def lq(h, qt, g, qt_, qk):
                    T.dma("sp", qt_[0:64, :], QTN[h * 64:(h + 1) * 64, qt * 512:(qt + 1) * 512], writes=[qk])
                    T.dma("sp", qt_[64:128, :], MASKT[h][g * 64:(g + 1) * 64, qt * 512:(qt + 1) * 512], writes=[qk])
                attn_pass("sl", 128, DBG_HEADS or list(range(6)), lambda h: h // 3, lh, lq, 2, lambda h, kt: nsakb[:, h, kt:kt + 1], kt_causal,
                          lambda h: h * 3 + 1, lambda h: OCT[h], lambda h: OST[h], False)
            if "w" in which:
                def lh(hk, kt_, kk, vt, vk):
                    T.dma("sp", kt_[0:64, :], KWN[hk * 64:(hk + 1) * 64, :], writes=[kk])
                    vload(VWN[:, hk * 64:(hk + 1) * 64], vt, vk)

                def lq(h, qt, g, qt_, qk):
                    T.dma("sp", qt_[0:64, :], QTN[h * 64:(h + 1) * 64, qt * 512:(qt + 1) * 512], writes=[qk])
                    T.dma("sp", qt_[64:65, :], NSACQ[h:h + 1, qt * 512:(qt + 1) * 512], writes=[qk])
                attn_pass("wn", 65, DBG_HEADS or list(range(6)), lambda h: h // 3, lh, lq, 1, lambda h, kt: nsakb[:, h, kt:kt + 1], kt_window,
                          lambda h: h * 3 + 2, lambda h: OST[h], lambda h: MIXT[h * 64:(h + 1) * 64, :], True)

        def phase_E(l):
            with contextlib.ExitStack() as st:
                wdw = sb("wdwE", [128, 2, 31], F32, st)
                bdw = sb("bdwE", [128, 2], F32, st)
                lng = sb("lngE", [128, 2], F32, st)
                lnb = sb("lnbE", [128, 2], F32, st)
                diag = sb("diagE", [128, 62, 128], BF16, st)
                U = [sb(f"UE{i}", [128, 542], F32, st) for i in range(4)]
                Ub = [sb(f"UbE{i}", [128, 542], BF16, st) for i in range(4)]
                Hh = [sb(f"HE{i}", [128, 512], F32, st) for i in range(4)]
                Hq = [sb(f"HqE{i}", [128, 512], F32, st) for i in range(2)]
                mean = sb("meanE", [128, 512], F32, st)
                msq = sb("msqE", [128, 512], F32, st)
                rstd = sb("rstdE", [128, 512], F32, st)
                xh = [sb(f"xhE{i}", [128, 512], F32, st) for i in range(2)]
                ob = [sb(f"obE{i}", [128, 512], BF16, st) for i in range(2)]
                p1 = ps("p1E", [128, 512], F32, st)
                p2 = ps("p2E", [128, 512], F32, st)
                pc = [ps(f"pcE{i}", [128, 512], F32, st) for i in range(2)]
                transpose_load("wdwE", w_dw[l], 31, 2, lambda c: wdw[:, c, :])
                transpose_load("bdwE", b_dw[l].rearrange("(o n) -> o n", o=1), 1, 2, lambda c: bdw[:, c:c + 1])
                transpose_load("lngE", ln_conv_g[l].rearrange("(o n) -> o n", o=1), 1, 2, lambda c: lng[:, c:c + 1])
                transpose_load("lnbE", ln_conv_b[l].rearrange("(o n) -> o n", o=1), 1, 2, lambda c: lnb[:, c:c + 1])
                for c in range(2):
                    for k in range(31):
                        eng = ("dve", "pool")[k % 2]; E_ = V if eng == "dve" else G
                        T.op(eng, lambda E_=E_, c=c, k=k: E_.tensor_scalar(out=diag[:, c * 31 + k, :], in0=ident_b[:], scalar1=wdw[:, c, k:k + 1], scalar2=None, op0=ALU.mult),
                             reads=["ident_b", "wdwE_dst"], writes=["diagE"])
                n = 0
                for tt in range(NQT):
                    q0 = tt * 512
                    hs = []
                    for c in range(2):
                        u = U[n % 4]; uk = f"UE{n % 4}"; ub = Ub[n % 4]; ubk = f"UbE{n % 4}"; hh = Hh[n % 4]; hk = f"HE{n % 4}"; pcb = pc[n % 2]; pck = f"pcE{n % 2}"; n += 1
                        if tt == 0:
                            T.op("pool", lambda u=u: G.memset(u[:, 0:30], 0.0), writes=[uk])
                            T.dma("sp", u[:, 30:542], GLU[c * 128:(c + 1) * 128, 0:512], writes=[uk])
                        else:
                            T.dma("sp", u[:], GLU[c * 128:(c + 1) * 128, q0 - 30:q0 + 512], writes=[uk])
                        T.op("pool", lambda u=u, ub=ub: G.tensor_copy(out=ub[:], in_=u[:]), reads=[uk], writes=[ubk])
                        T.group("pe", [lambda k=k, c=c, ub=ub, pcb=pcb: P.matmul(pcb[:], lhsT=diag[:, c * 31 + k, :], rhs=ub[:, k:k + 512], start=(k == 0), stop=(k == 30))
                                       for k in range(31)], reads=["diagE", ubk], writes=[pck])
                        hq = Hq[c]
                        T.op("act", lambda hh=hh, pcb=pcb, c=c: A.activation(out=hh[:], in_=pcb[:], func=AF.Identity, bias=bdw[:, c:c + 1], scale=1.0),
                             reads=[pck, "bdwE_dst"], writes=[hk])
                        T.op("act", lambda hq=hq, pcb=pcb, c=c: A.activation(out=hq[:], in_=pcb[:], func=AF.Square, bias=bdw[:, c:c + 1], scale=1.0),
                             reads=[pck, "bdwE_dst"], writes=[f"HqE{c}"])
                        hs.append((hh, hk))
                    T.group("pe", [lambda c=c: P.matmul(p1[:], lhsT=ones_f[:], rhs=hs[c][0][:], start=(c == 0), stop=(c == 1)) for c in range(2)],
                            reads=["ones_f", hs[0][1], hs[1][1]], writes=["p1E"])
                    T.group("pe", [lambda c=c: P.matmul(p2[:], lhsT=ones_f[:], rhs=Hq[c][:], start=(c == 0), stop=(c == 1)) for c in range(2)],
                            reads=["ones_f", "HqE0", "HqE1"], writes=["p2E"])
                    T.op("dve", lambda: V.tensor_scalar(out=mean[:], in0=p1[:], scalar1=1.0 / 256, scalar2=None, op0=ALU.mult), reads=["p1E"], writes=["meanE"])
                    T.op("dve", lambda: V.tensor_tensor(out=msq[:], in0=mean[:], in1=mean[:], op=ALU.mult), reads=["meanE"], writes=["msqE"])
                    T.op("dve", lambda: V.scalar_tensor_tensor(out=rstd[:], in0=p2[:], scalar=1.0 / 256, in1=msq[:], op0=ALU.mult, op1=ALU.subtract),
                         reads=["p2E", "msqE"], writes=["rstdE"])
                    T.op("act", lambda: A.activation(out=rstd[:], in_=rstd[:], func=AF.Sqrt, bias=epsb[:, 0:1], scale=1.0), reads=["rstdE", "epsb"], writes=["rstdE"])
                    T.op("dve", lambda: V.reciprocal(out=rstd[:], in_=rstd[:]), reads=["rstdE"], writes=["rstdE"])
                    for c in range(2):
                        hh, hk = hs[c]
                        x_ = xh[c]; xk = f"xhE{c}"
                        T.op("dve", lambda hh=hh, x_=x_: V.tensor_tensor(out=x_[:], in0=hh[:], in1=mean[:], op=ALU.subtract), reads=[hk, "meanE"], writes=[xk])
                        T.op("dve", lambda x_=x_: V.tensor_tensor(out=x_[:], in0=x_[:], in1=rstd[:], op=ALU.mult), reads=[xk, "rstdE"], writes=[xk])
                        T.op("dve", lambda x_=x_, c=c: V.tensor_scalar(out=x_[:], in0=x_[:], scalar1=lng[:, c:c + 1], scalar2=lnb[:, c:c + 1], op0=ALU.mult, op1=ALU.add),
                             reads=[xk, "lngE", "lnbE"], writes=[xk])
                        o_ = ob[c]
                        T.op("act", lambda x_=x_, o_=o_: A.activation(out=o_[:], in_=x_[:], func=AF.Silu), reads=[xk], writes=[f"obE{c}"])
                        T.dma("pool", MIXT[384 + c * 128:384 + (c + 1) * 128, q0:q0 + 512], o_[:], reads=[f"obE{c}"])
                T.barrier()

        def phase_F(l):
            with contextlib.ExitStack() as st:
                wo = sb("woF", [128, 8, D], BF16, st)
                stg = [sb(f"stgF{i}", [128, D], F32, st) for i in range(2)]
                load_cast(st, "woF", lambda c: wo[:, c, :], lambda c: w_out[l, c * 128:(c + 1) * 128, :], 8, D, stg)
                gt = sb("gF", [128, D], F32, st); bt = sb("bF", [128, D], F32, st)
                load_gb(gt, bt, ln1_g[l], ln1_b[l])
                mt = [sb(f"mtF{i}", [128, 8, 128], BF16, st) for i in range(4)]
                xi = [sb(f"xiF{i}", [128, D], F32, st) for i in range(4)]
                pp = [ps(f"ppF{i}", [128, 512], F32, st) for i in range(4)]
                for tp in range(S // 256):
                    items = []
                    for j in range(2):
                        ti = tp * 2 + j
                        r0 = ti * 128
                        m = mt[ti % 4]; mk = f"mtF{ti % 4}"; x_ = xi[ti % 4]; xk = f"xiF{ti % 4}"
                        for c in range(8):
                            T.dma("sp", m[:, c, :], MIXT[c * 128:(c + 1) * 128, r0:r0 + 128], writes=[mk])
                        T.dma("sp", x_[:], XA[r0:r0 + 128, :], writes=[xk])
                        items.append((x_, xk, m, mk, r0, ti))
                    for (x_, xk, m, mk, r0, ti) in items:
                        for nh in range(2):
                            pb = pp[(ti * 2 + nh) % 4]; pk = f"ppF{(ti * 2 + nh) % 4}"
                            T.group("pe", [lambda k=k, pb=pb, nh=nh, m=m: P.matmul(pb[:], lhsT=m[:, k, :], rhs=wo[:, k, nh * 512:(nh + 1) * 512], start=(k == 0), stop=(k == 7))
                                           for k in range(8)], reads=[mk, "woF"], writes=[pk])
                    for (x_, xk, m, mk, r0, ti) in items:
                        for nh in range(2):
                            pb = pp[(ti * 2 + nh) % 4]; pk = f"ppF{(ti * 2 + nh) % 4}"
                            T.op("dve", lambda pb=pb, nh=nh, x_=x_: V.scalar_tensor_tensor(out=x_[:, nh * 512:(nh + 1) * 512], in0=x_[:, nh * 512:(nh + 1) * 512], scalar=ALPHA,
                                                                                       in1=pb[:], op0=ALU.mult, op1=ALU.add), reads=[pk, xk], writes=[xk])
                    layer_norm_multi([(it_[0], it_[1]) for it_ in items], (gt, bt))
                    for (x_, xk, m, mk, r0, ti) in items:
                        T.dma("pool", XB[r0:r0 + 128, :], x_[:], reads=[xk])
                T.barrier()

        def phase_G(l, last):
            TT = 256
            with contextlib.ExitStack() as st:
                wfi = sb("wfiG", [128, 8, 2 * DFF], BF16, st)
                wfd = sb("wfdG", [128, NF, D], BF16, st)
                with contextlib.ExitStack() as st2:
                    stg = [sb(f"stgG{i}", [128, 2048], F32, st2) for i in range(2)]
                    n = 0
                    for c in range(8):
                        for q in range(3):
                            c0 = q * 2048; w_ = min(2048, 2 * DFF - c0)
                            sg = stg[n % 2]; sk = f"stgG{n % 2}"; n += 1
                            T.dma("sp", sg[:, :w_], w_ffn_in[l, c * 128:(c + 1) * 128, c0:c0 + w_], writes=[sk])
                            eng = ("dve", "pool")[n % 2]; E = V if eng == "dve" else G
                            T.op(eng, lambda E=E, sg=sg, c=c, c0=c0, w_=w_: E.tensor_copy(out=wfi[:, c, c0:c0 + w_], in_=sg[:, :w_]), reads=[sk], writes=["wfiG"])
                    for c in range(NF):
                        sg = stg[n % 2]; sk = f"stgG{n % 2}"; n += 1
                        T.dma("sp", sg[:, :D], w_ffn_down[l, c * 128:(c + 1) * 128, :], writes=[sk])
                        eng = ("dve", "pool")[n % 2]; E = V if eng == "dve" else G
                        T.op(eng, lambda E=E, sg=sg, c=c: E.tensor_copy(out=wfd[:, c, :], in_=sg[:, :D]), reads=[sk], writes=["wfdG"])
                    T.barrier()
                gt = sb("gG", [128, D], F32, st); bt = sb("bG", [128, D], F32, st)
                load_gb(gt, bt, ln2_g[l], ln2_b[l])
                wcv = sb("wcvG", [128, NF, 3], F32, st)
                bcv = sb("bcvG", [128, NF], F32, st)
                transpose_load("wcvG", w_ffn_conv[l], 3, NF, lambda c: wcv[:, c, :])
                transpose_load("bcvG", b_ffn_conv[l].rearrange("(o n) -> o n", o=1), 1, NF, lambda c: bcv[:, c:c + 1])
                halo = sb("haloG", [128, NF, 2], F32, st)
                T.op("pool", lambda: G.memset(halo[:], 0.0), writes=["haloG"])
                xi = [sb(f"xiG{i}", [128, D], F32, st) for i in range(3)]
                xb = [sb(f"xbG{i}", [128, D], BF16, st) for i in range(2)]
                xT = [sb(f"xTG{i}", [128, 8, TT], BF16, st) for i in range(2)]
                hT = [sb(f"hTG{i}", [128, NF, TT], BF16, st) for i in range(1)]
                Gs = [sb(f"GsG{i}", [128, 2, TT + 2], F32, st) for i in range(2)]
                t1 = [sb(f"t1G{i}", [128, 2, TT], F32, st) for i in range(2)]
                sgl = [sb(f"sgG{i}", [128, 2, TT], F32, st) for i in range(1)]
                ptr = [ps(f"ptrG{i}", [128, 1024], BF16, st) for i in range(2)]
                pg = [ps(f"pgG{i}", [128, 2, TT], F32, st) for i in range(2)]
                pu = [ps(f"puG{i}", [128, 2, TT], F32, st) for i in range(2)]
                pd = [ps(f"pdG{i}", [128, 512], F32, st) for i in range(2)]
                nsub = TT // 128
                dst = y_out if last else XA
                it = 0
                for tt in range(S // TT):
                    xt = xT[tt % 2]; xtk = f"xTG{tt % 2}"; ht = hT[0]; htk = "hTG0"
                    xs = []
                    for sub in range(nsub):
                        nn = tt * nsub + sub
                        r0 = nn * 128
                        x_ = xi[nn % 3]; xk = f"xiG{nn % 3}"; b_ = xb[nn % 2]; bk = f"xbG{nn % 2}"
                        T.dma("sp", x_[:], XB[r0:r0 + 128, :], writes=[xk])
                        T.op("pool", lambda b_=b_, x_=x_: G.tensor_copy(out=b_[:], in_=x_[:]), reads=[xk], writes=[bk])
                        pt = ptr[nn % 2]; ptk = f"ptrG{nn % 2}"
                        T.group("pe", [lambda k=k, pt=pt, b_=b_: P.transpose(out=pt[:, k * 128:(k + 1) * 128], in_=b_[:, k * 128:(k + 1) * 128], identity=ident_b[:]) for k in range(8)],
                                reads=[bk, "ident_b"], writes=[ptk])
                        T.op("act", lambda pt=pt, xt=xt, sub=sub: A.activation(out=xt[:, :, sub * 128:(sub + 1) * 128], in_=pt[:].rearrange("p (k t) -> p k t", k=8), func=AF.Copy),
                             reads=[ptk], writes=[xtk])
                        xs.append((x_, xk, r0))
                    for fp in range(NF // 2):
                        b = it % 2; it += 1
                        f0 = 2 * fp
                        T.group("pe", [lambda k=k, b=b, j=j, f0=f0: P.matmul(pg[b][:, j, :], lhsT=wfi[:, k, (f0 + j) * 128:(f0 + j + 1) * 128], rhs=xt[:, k, :],
                                                                            start=(k == 0), stop=(k == 7)) for j in range(2) for k in range(8)],
                                reads=["wfiG", xtk], writes=[f"pgG{b}"])
                        T.group("pe", [lambda k=k, b=b, j=j, f0=f0: P.matmul(pu[b][:, j, :], lhsT=wfi[:, k, DFF + (f0 + j) * 128:DFF + (f0 + j + 1) * 128], rhs=xt[:, k, :],
                                                                            start=(k == 0), stop=(k == 7)) for j in range(2) for k in range(8)],
                                reads=["wfiG", xtk], writes=[f"puG{b}"])
                        gs = Gs[b]; gk = f"GsG{b}"; t_ = t1[b]; s_ = sgl[0]; sk = "sgG0"
                        T.op("pool", lambda gs=gs, f0=f0: G.tensor_copy(out=gs[:, :, 0:2], in_=halo[:, f0:f0 + 2, :]), reads=["haloG"], writes=[gk])
                        T.op("act", lambda gs=gs, b=b: A.activation(out=gs[:, :, 2:TT + 2], in_=pg[b][:], func=AF.Copy), reads=[f"pgG{b}"], writes=[gk])
                        T.op("pool", lambda gs=gs, f0=f0: G.tensor_copy(out=halo[:, f0:f0 + 2, :], in_=gs[:, :, TT:TT + 2]), reads=[gk], writes=["haloG"])
                        for j in range(2):
                            T.op("dve", lambda gs=gs, t_=t_, j=j, f0=f0: V.tensor_scalar(out=t_[:, j, :], in0=gs[:, j, 0:TT], scalar1=wcv[:, f0 + j, 0:1], scalar2=bcv[:, f0 + j:f0 + j + 1],
                                                                                   op0=ALU.mult, op1=ALU.add), reads=[gk, "wcvG", "bcvG"], writes=[f"t1G{b}_{j}"])
                        for j in range(2):
                            T.op("dve", lambda gs=gs, t_=t_, j=j, f0=f0: V.scalar_tensor_tensor(out=t_[:, j, :], in0=gs[:, j, 1:TT + 1], scalar=wcv[:, f0 + j, 1:2], in1=t_[:, j, :],
                                                                                          op0=ALU.mult, op1=ALU.add), reads=[gk, "wcvG", f"t1G{b}_{j}"], writes=[f"t1G{b}_{j}"])
                        for j in range(2):
                            T.op("dve", lambda gs=gs, t_=t_, j=j, f0=f0: V.scalar_tensor_tensor(out=t_[:, j, :], in0=gs[:, j, 2:TT + 2], scalar=wcv[:, f0 + j, 2:3], in1=t_[:, j, :],
                                                                                          op0=ALU.mult, op1=ALU.add), reads=[gk, "wcvG", f"t1G{b}_{j}"], writes=[f"t1G{b}_{j}"])
                        T.op("act", lambda t_=t_, s_=s_: A.activation(out=s_[:], in_=t_[:], func=AF.Silu), reads=[f"t1G{b}_0", f"t1G{b}_1"], writes=[sk])
                        T.op("dve", lambda s_=s_, b=b, f0=f0: V.tensor_tensor(out=ht[:, f0:f0 + 2, :], in0=s_[:], in1=pu[b][:], op=ALU.mult), reads=[sk, f"puG{b}"], writes=[htk])
                    for sub in range(nsub):
                        x_, xk, r0 = xs[sub]
                        for nh in range(2):
                            pb = pd[nh]; pk = f"pdG{nh}"
                            T.group("pe", [lambda fc=fc, pb=pb, nh=nh, sub=sub: P.matmul(pb[:], lhsT=ht[:, fc, sub * 128:(sub + 1) * 128], rhs=wfd[:, fc, nh * 512:(nh + 1) * 512],
                                                                                     start=(fc == 0), stop=(fc == NF - 1)) for fc in range(NF)],
                                    reads=[htk, "wfdG"], writes=[pk])
                            T.op("dve", lambda pb=pb, nh=nh, x_=x_: V.scalar_tensor_tensor(out=x_[:, nh * 512:(nh + 1) * 512], in0=x_[:, nh * 512:(nh + 1) * 512], scalar=ALPHA,
                                                                                       in1=pb[:], op0=ALU.mult, op1=ALU.add), reads=[pk, xk], writes=[xk])
                    layer_norm_multi([(xs[sub][0], xs[sub][1]) for sub in range(nsub)], (gt, bt))
                    for sub in range(nsub):
                        x_, xk, r0 = xs[sub]
                        T.dma("pool", dst[r0:r0 + 128, :], x_[:], reads=[xk])
                T.barrier()

        for l in range(DEPTH):
            if layers is not None and l not in layers:
                continue
            if "A" in phases:
                phase_A(l)
            if "B" in phases:
                phase_B(l)
            if "C" in phases:
                phase_C(l)
            if "D" in phases:
                phase_D(l, DBG_WHICH)
            if "E" in phases:
                phase_E(l)
            if "F" in phases:
                phase_F(l)
            if "G" in phases:
                phase_G(l, l == DEPTH - 1)
        T.barrier()
        print("ninst", T.ninst, "nsem", T.nsem)
    return nc


_NC = None


def kernel(**inputs):
    global _NC
    if _NC is None:
        _NC = build()
    x = np.asarray(inputs["x"], dtype=np.float32)
    nb = x.shape[0]
    shared = {k: np.ascontiguousarray(np.asarray(v, dtype=np.float32)) for k, v in inputs.items() if k != "x"}
    in_maps = []
    for b in range(nb):
        m = dict(shared)
        m["x"] = np.ascontiguousarray(x[b])
        in_maps.append(m)
    res = run_bass_kernel_spmd(_NC, in_maps, core_ids=list(range(nb)))
    return np.stack([np.asarray(r["y"], dtype=np.float32) for r in res.results], axis=0)
```
